# Optimizing a Trainium2 kernel written in Bass

```python
import math
import jax, jax.numpy as jnp
from jax import lax
import numpy as np

D_MODEL = 1024
BATCH = 8
SEQ = 2048
DEPTH = 1
DEC_BATCH = 128
DEC_SEQ = 1
PAST_LEN = 16384
PAGE_SIZE = 128

SSM_WIDTH = D_MODEL // 2
SSM_GROUP = 16
SSM_GROUPS = SSM_WIDTH // SSM_GROUP
SSM_STATE = 64
SGU_WIDTH = D_MODEL // 2
SGU_CHUNK = 128
SGU_HEADS = 4
SGU_HEAD_DIM = SGU_WIDTH // SGU_HEADS
D_FF = 4 * D_MODEL
PLE_DIM = 256
IN_COLS = SSM_WIDTH + 2 * SGU_WIDTH + 2 * D_MODEL
DN_ALPHA = float((2 * DEPTH) ** 0.25)
DN_BETA = float((8 * DEPTH) ** -0.25)
LN_EPS = 1e-5

kernel_name = "s5_gmlp_gated_hybrid_step"


def layer_norm(x, g, b):
    xf = x.astype(jnp.float32)
    mu = jnp.mean(xf, axis=-1, keepdims=True)
    xc = xf - mu
    var = jnp.mean(xc * xc, axis=-1, keepdims=True)
    return (xc * lax.rsqrt(var + LN_EPS) * g.astype(jnp.float32) + b.astype(jnp.float32)).astype(x.dtype)


def _complex_affine_combine(e1, e2):
    a1r, a1i, b1r, b1i = e1
    a2r, a2i, b2r, b2i = e2
    ar = a2r * a1r - a2i * a1i
    ai = a2r * a1i + a2i * a1r
    br = a2r * b1r - a2i * b1i + b2r
    bi = a2r * b1i + a2i * b1r + b2i
    return (ar, ai, br, bi)


def s5_mixer(u, h0_re, h0_im, lam_re, lam_im, log_dt, b_re, b_im, c_re, c_im, d_skip, w_glu, b_glu):
    f32 = jnp.float32
    Bn, L, _ = u.shape
    uf = u.astype(f32)
    ug = uf.reshape(Bn, L, SSM_GROUPS, SSM_GROUP)
    lr = lam_re.astype(f32)
    li = lam_im.astype(f32)
    dt = jnp.exp(log_dt.astype(f32))[:, None]
    mag = jnp.exp(lr * dt)
    ab_re = mag * jnp.cos(li * dt)
    ab_im = mag * jnp.sin(li * dt)
    den = lr * lr + li * li
    nr = ab_re - 1.0
    ni = ab_im
    co_re = (nr * lr + ni * li) / den
    co_im = (ni * lr - nr * li) / den
    br = b_re.astype(f32)
    bi = b_im.astype(f32)
    bb_re = co_re[..., None] * br - co_im[..., None] * bi
    bb_im = co_re[..., None] * bi + co_im[..., None] * br
    bu_re = jnp.einsum('blgc,gpc->lbgp', ug, bb_re)
    bu_im = jnp.einsum('blgc,gpc->lbgp', ug, bb_im)
    a_re_t = jnp.broadcast_to(ab_re[None, None], (L, 1, SSM_GROUPS, SSM_STATE))
    a_im_t = jnp.broadcast_to(ab_im[None, None], (L, 1, SSM_GROUPS, SSM_STATE))
    pw_re, pw_im, h_re, h_im = lax.associative_scan(
        _complex_affine_combine, (a_re_t, a_im_t, bu_re, bu_im), axis=0)
    h0r = h0_re.astype(f32)
    h0i = h0_im.astype(f32)
    h_re, h_im = (h_re + pw_re * h0r - pw_im * h0i,
                  h_im + pw_re * h0i + pw_im * h0r)
    y = (jnp.einsum('lbgp,gcp->blgc', h_re, c_re.astype(f32))
         - jnp.einsum('lbgp,gcp->blgc', h_im, c_im.astype(f32)))
    y = y.reshape(Bn, L, SSM_WIDTH) + d_skip.astype(f32) * uf
    y = jax.nn.gelu(y).astype(u.dtype)
    y = y * jax.nn.sigmoid(y @ w_glu + b_glu)
    return y, h_re[-1], h_im[-1]


def spatial_mix(v, w_s, b_s):
    Bn, L, _ = v.shape
    Lp = -(-L // SGU_CHUNK) * SGU_CHUNK
    vp = jnp.pad(v, ((0, 0), (0, Lp - L), (0, 0)))
    vp = vp.reshape(Bn, Lp // SGU_CHUNK, SGU_CHUNK, SGU_HEADS, SGU_HEAD_DIM)
    mask = jnp.tril(jnp.ones((SGU_CHUNK, SGU_CHUNK), dtype=w_s.dtype))
    s = jnp.einsum('hts,bnshd->bnthd', w_s * mask, vp)
    s = s + jnp.transpose(b_s)[None, None, :, :, None]
    return s.reshape(Bn, Lp, SGU_WIDTH)[:, :L]


def decoder_layer(x, p, h0_re, h0_im, w_in, b_in, lam_re, lam_im, log_dt, b_re, b_im, c_re, c_im,
                  d_skip, w_glu, b_glu, sgu_ln_g, sgu_ln_b, sgu_w, sgu_b, w_branch_a, w_branch_b,
                  w_out, b_out, ln1_g, ln1_b, w_up, b_up, w_down, b_down, w_ple, w_ple_gate,
                  b_ple_gate, ln2_g, ln2_b):
    z = x @ w_in + b_in
    u_a, z_b, gate_a, gate_b = jnp.split(
        z, [SSM_WIDTH, SSM_WIDTH + 2 * SGU_WIDTH, SSM_WIDTH + 2 * SGU_WIDTH + D_MODEL], axis=-1)
    y_a, hT_re, hT_im = s5_mixer(u_a, h0_re, h0_im, lam_re, lam_im, log_dt, b_re, b_im,
                                 c_re, c_im, d_skip, w_glu, b_glu)
    u_b, v_b = jnp.split(jax.nn.gelu(z_b), 2, axis=-1)
    v_b = layer_norm(v_b, sgu_ln_g, sgu_ln_b)
    y_b = u_b * spatial_mix(v_b, sgu_w, sgu_b)
    merged = (jax.nn.sigmoid(gate_a) * (y_a @ w_branch_a)
              + jax.nn.sigmoid(gate_b) * (y_b @ w_branch_b))
    mix = merged @ w_out + b_out
    x1 = layer_norm(DN_ALPHA * x + mix, ln1_g, ln1_b)
    ff = jnp.square(jax.nn.relu(x1 @ w_up + b_up)) @ w_down + b_down
    ple = jax.nn.sigmoid(x1 @ w_ple_gate + b_ple_gate) * (p @ w_ple)
    x2 = layer_norm(DN_ALPHA * x1 + ff + ple, ln2_g, ln2_b)
    return x2, hT_re, hT_im, v_b


def setup_inputs(seed: int = 0) -> dict:
    key = jax.random.key(seed)
    ks = iter(jax.random.split(key, 48))

    def nrm(shape, scale):
        return jax.random.normal(next(ks), shape, jnp.float32) * scale

    G, P, C = SSM_GROUPS, SSM_STATE, SSM_GROUP
    n_idx = jnp.arange(P, dtype=jnp.float32)
    return {
        "x_prompt": nrm((BATCH, SEQ, D_MODEL), 1.0),
        "x_sample": nrm((DEC_BATCH, DEC_SEQ, D_MODEL), 1.0),
        "state_ssm_re": nrm((DEPTH, DEC_BATCH, G, P), 0.5),
        "state_ssm_im": nrm((DEPTH, DEC_BATCH, G, P), 0.5),
        "p_prompt": nrm((DEPTH, BATCH, SEQ, PLE_DIM), 1.0),
        "p_sample": nrm((DEPTH, DEC_BATCH, DEC_SEQ, PLE_DIM), 1.0),
        "w_in": nrm((DEPTH, D_MODEL, IN_COLS), D_MODEL ** -0.5),
        "b_in": nrm((DEPTH, IN_COLS), 0.02),
        "ssm_lambda_re": -0.5 + nrm((DEPTH, G, P), 0.01),
        "ssm_lambda_im": math.pi * n_idx[None, None, :] + nrm((DEPTH, G, P), 0.01),
        "ssm_log_dt": jax.random.uniform(next(ks), (DEPTH, G), jnp.float32,
                                         minval=math.log(1e-3), maxval=math.log(1e-1)),
        "ssm_b_re": nrm((DEPTH, G, P, C), (2 * C) ** -0.5),
        "ssm_b_im": nrm((DEPTH, G, P, C), (2 * C) ** -0.5),
        "ssm_c_re": nrm((DEPTH, G, C, P), (2 * P) ** -0.5),
        "ssm_c_im": nrm((DEPTH, G, C, P), (2 * P) ** -0.5),
        "ssm_d": nrm((DEPTH, SSM_WIDTH), 0.5),
        "w_glu": nrm((DEPTH, SSM_WIDTH, SSM_WIDTH), SSM_WIDTH ** -0.5),
        "b_glu": nrm((DEPTH, SSM_WIDTH), 0.02),
        "sgu_ln_g": 1.0 + nrm((DEPTH, SGU_WIDTH), 0.02),
        "sgu_ln_b": nrm((DEPTH, SGU_WIDTH), 0.02),
        "sgu_w": nrm((DEPTH, SGU_HEADS, SGU_CHUNK, SGU_CHUNK), SGU_CHUNK ** -0.5),
        "sgu_b": 1.0 + nrm((DEPTH, SGU_HEADS, SGU_CHUNK), 0.01),
        "w_branch_a": nrm((DEPTH, SSM_WIDTH, D_MODEL), DN_BETA * SSM_WIDTH ** -0.5),
        "w_branch_b": nrm((DEPTH, SGU_WIDTH, D_MODEL), DN_BETA * SGU_WIDTH ** -0.5),
        "w_out": nrm((DEPTH, D_MODEL, D_MODEL), DN_BETA * D_MODEL ** -0.5),
        "b_out": nrm((DEPTH, D_MODEL), 0.02),
        "ln1_g": 1.0 + nrm((DEPTH, D_MODEL), 0.02),
        "ln1_b": nrm((DEPTH, D_MODEL), 0.02),
        "w_up": nrm((DEPTH, D_MODEL, D_FF), DN_BETA * D_MODEL ** -0.5),
        "b_up": nrm((DEPTH, D_FF), 0.02),
        "w_down": nrm((DEPTH, D_FF, D_MODEL), DN_BETA * D_FF ** -0.5),
        "b_down": nrm((DEPTH, D_MODEL), 0.02),
        "w_ple": nrm((DEPTH, PLE_DIM, D_MODEL), PLE_DIM ** -0.5),
        "w_ple_gate": nrm((DEPTH, D_MODEL, D_MODEL), D_MODEL ** -0.5),
        "b_ple_gate": nrm((DEPTH, D_MODEL), 0.02),
        "ln2_g": 1.0 + nrm((DEPTH, D_MODEL), 0.02),
        "ln2_b": nrm((DEPTH, D_MODEL), 0.02),
    }


def reference(x_prompt, x_sample, state_ssm_re, state_ssm_im, p_prompt, p_sample,
              w_in, b_in, ssm_lambda_re, ssm_lambda_im, ssm_log_dt, ssm_b_re, ssm_b_im,
              ssm_c_re, ssm_c_im, ssm_d, w_glu, b_glu, sgu_ln_g, sgu_ln_b, sgu_w, sgu_b,
              w_branch_a, w_branch_b, w_out, b_out, ln1_g, ln1_b, w_up, b_up, w_down, b_down,
              w_ple, w_ple_gate, b_ple_gate, ln2_g, ln2_b):
    xp = x_prompt
    xs = x_sample
    h0p = jnp.zeros((x_prompt.shape[0], SSM_GROUPS, SSM_STATE), jnp.float32)
    pre_l, pim_l, sre_l, sim_l, sv_l = [], [], [], [], []
    for i in range(DEPTH):
        w = (w_in[i], b_in[i], ssm_lambda_re[i], ssm_lambda_im[i], ssm_log_dt[i], ssm_b_re[i],
             ssm_b_im[i], ssm_c_re[i], ssm_c_im[i], ssm_d[i], w_glu[i], b_glu[i], sgu_ln_g[i],
             sgu_ln_b[i], sgu_w[i], sgu_b[i], w_branch_a[i], w_branch_b[i], w_out[i], b_out[i],
             ln1_g[i], ln1_b[i], w_up[i], b_up[i], w_down[i], b_down[i], w_ple[i],
             w_ple_gate[i], b_ple_gate[i], ln2_g[i], ln2_b[i])
        xp, hpr, hpi, _ = decoder_layer(xp, p_prompt[i], h0p, h0p, *w)
        xs, hsr, hsi, vs = decoder_layer(xs, p_sample[i], state_ssm_re[i], state_ssm_im[i], *w)
        pre_l.append(hpr)
        pim_l.append(hpi)
        sre_l.append(hsr)
        sim_l.append(hsi)
        sv_l.append(vs)
    ssm_re_prompt = jnp.stack(pre_l)
    ssm_im_prompt = jnp.stack(pim_l)
    ssm_re_sample = jnp.stack(sre_l)
    ssm_im_sample = jnp.stack(sim_l)
    sgu_v_sample = jnp.stack(sv_l)
    return (xp, xs, ssm_re_prompt, ssm_im_prompt, ssm_re_sample, ssm_im_sample, sgu_v_sample)
```

```python
import contextlib
import math
import numpy as np
import concourse.bass as bass
import concourse.mybir as mybir
from concourse.bass_utils import run_bass_kernel_spmd

F32 = mybir.dt.float32
BF16 = mybir.dt.bfloat16
ALU = mybir.AluOpType
AF = mybir.ActivationFunctionType

ENGS = ("pe", "act", "dve", "pool", "sp")
NSLOT = {"sp": 24, "act": 8, "pool": 16}
NBG = 40

D = 1024
SEQ = 2048
NS = 16
TT = SEQ + NS
L = 16
NK = SEQ // L
ALPHA = float(2.0 ** 0.25)
EPS = 1e-5
PI = math.pi


class Prog:
    def __init__(self, nc):
        self.nc = nc
        self.q = {e: [] for e in ENGS}
        self.cnt = {}
        self.last_w = {}
        self.readers = {}
        self.waited = {}
        self.dma_n = {e: 0 for e in ENGS}
        self.sems = {}
        self.n_inst = 0

    def _deps(self, eng, reads, writes, skip_self_w=False):
        deps = {}

        def add(d, same_ok=True):
            if d is None:
                return
            pid, c = d
            if pid == eng and not same_ok:
                return
            if deps.get(pid, 0) < c:
                deps[pid] = c

        for r in reads:
            add(self.last_w.get(r))
        for w in writes:
            add(self.last_w.get(w), same_ok=not skip_self_w)
            for rd in self.readers.get(w, ()):
                add(rd, same_ok=not skip_self_w)
        out = []
        for pid, c in deps.items():
            if self.waited.get((eng, pid), 0) >= c:
                continue
            if pid in ENGS and c > self.cnt.get(pid, 0):
                if pid == eng:
                    continue
                raise RuntimeError(f"dep on pending no-inc instr {pid}:{c} from {eng}")
            self.waited[(eng, pid)] = c
            out.append((pid, c))
        return out

    def _record(self, pid, c, reads, writes):
        for r in reads:
            self.readers.setdefault(r, []).append((pid, c))
        for w in writes:
            self.last_w[w] = (pid, c)
            self.readers[w] = []

    def op(self, eng, fn, reads=(), writes=(), inc=True, skip_self_w=False):
        reads = tuple(reads)
        writes = tuple(writes)
        deps = self._deps(eng, reads, writes, skip_self_w)
        c = self.cnt.get(eng, 0) + 1
        if inc:
            self.cnt[eng] = c
        self._record(eng, c, reads, writes)
        self.q[eng].append(("op", fn, deps, inc))
        self.n_inst += 1

    def dma(self, eng, out, in_, reads=(), writes=(), bg=False):
        reads = tuple(reads)
        writes = tuple(writes)
        if bg:
            n = self.dma_n.get("bg", 0)
            self.dma_n["bg"] = n + 1
            pid = ("dma", "bg", n % NBG)
        else:
            n = self.dma_n[eng]
            self.dma_n[eng] = n + 1
            pid = ("dma", eng, n % NSLOT[eng])
        deps = self._deps(eng, reads, writes)
        pc = self.cnt.get(pid, 0)
        if pc > 0 and self.waited.get((eng, pid), 0) < pc:
            self.waited[(eng, pid)] = pc
            deps.append((pid, pc))
        c = pc + 16
        self.cnt[pid] = c
        self._record(pid, c, reads, writes)
        self.q[eng].append(("dma", (out, in_), deps, pid))
        self.n_inst += 1

    def barrier(self):
        snap = dict(self.cnt)
        for e in ENGS:
            deps = []
            for pid, c in snap.items():
                if pid == e or c == 0 or (isinstance(pid, tuple) and pid[1] == "bg"):
                    continue
                if self.waited.get((e, pid), 0) < c:
                    self.waited[(e, pid)] = c
                    deps.append((pid, c))
            self.q[e].append(("wait", None, deps, None))

    def emit(self):
        nc = self.nc
        pids = list(ENGS)
        for e in ("sp", "act", "pool"):
            for s in range(NSLOT[e]):
                pids.append(("dma", e, s))
        for s in range(NBG):
            pids.append(("dma", "bg", s))
        with contextlib.ExitStack() as st:
            for pid in pids:
                name = pid if isinstance(pid, str) else f"dma_{pid[1]}_{pid[2]}"
                self.sems[pid] = st.enter_context(nc.semaphore("s_" + name))
            block = st.enter_context(nc.Block())
            final_waits = {e: [] for e in ENGS}
            for e in ("sp", "act", "pool"):
                for s in range(NSLOT[e]):
                    pid = ("dma", e, s)
                    c = self.cnt.get(pid, 0)
                    if c:
                        final_waits[e].append((pid, c))
            for s in range(NBG):
                c = self.cnt.get(("dma", "bg", s), 0)
                if c:
                    final_waits["pool"].append((("dma", "bg", s), c))

            def run(e, handle):
                for kind, payload, deps, extra in self.q[e]:
                    for pid, c in deps:
                        handle.wait_ge(self.sems[pid], c)
                    if kind == "op":
                        ins = payload(handle)
                        if extra:
                            ins.then_inc(self.sems[e], 1)
                    elif kind == "dma":
                        out, in_ = payload
                        handle.dma_start(out=out, in_=in_).then_inc(self.sems[extra], 16)
                for pid, c in final_waits[e]:
                    handle.wait_ge(self.sems[pid], c)

            @block.sync
            def _(h):
                run("sp", h)

            @block.scalar
            def _(h):
                run("act", h)

            @block.vector
            def _(h):
                run("dve", h)

            @block.gpsimd
            def _(h):
                run("pool", h)

            @block.tensor
            def _(h):
                run("pe", h)


class T:
    def __init__(self, ap, key):
        self.ap = ap
        self.key = key

    def __getitem__(self, idx):
        return T(self.ap[idx], self.key)

    def r(self, pat, **kw):
        return T(self.ap.rearrange(pat, **kw), self.key)

    def bc(self, shape):
        return T(self.ap.to_broadcast(list(shape)), self.key)

    def us(self, ax):
        return T(self.ap.unsqueeze(ax), self.key)

    def k(self, key):
        return T(self.ap, key)


def _pp(v, m):
    return np.ascontiguousarray(v.reshape(m, 128).T)


def _host_layout(inp):
    f = np.float32
    sh = {}
    sh["w_in"] = np.ascontiguousarray(inp["w_in"][0])
    sh["w_glu"] = np.ascontiguousarray(inp["w_glu"][0])
    sh["w_pa"] = np.ascontiguousarray(inp["w_branch_a"][0])
    sh["w_pb"] = np.ascontiguousarray(inp["w_branch_b"][0])
    sh["w_out"] = np.ascontiguousarray(inp["w_out"][0])
    sh["w_up"] = np.ascontiguousarray(inp["w_up"][0])
    sh["w_down"] = np.ascontiguousarray(inp["w_down"][0])
    sh["w_ple"] = np.ascontiguousarray(inp["w_ple"][0])
    sh["w_pg"] = np.ascontiguousarray(inp["w_ple_gate"][0])
    sh["b_in_pp"] = _pp(inp["b_in"][0], 28)
    sh["b_glu_pp"] = _pp(inp["b_glu"][0], 4)
    sh["b_up_pp"] = _pp(inp["b_up"][0], 32)
    rows = [inp["b_in"][0][1024:1536], inp["sgu_ln_g"][0], inp["sgu_ln_b"][0],
            inp["sgu_b"][0].reshape(-1),
            inp["b_out"][0], inp["ln1_g"][0], inp["ln1_b"][0], inp["b_down"][0],
            inp["b_ple_gate"][0], inp["ln2_g"][0], inp["ln2_b"][0],
            inp["sgu_b"][0][:, 0]]
    row = np.concatenate([np.asarray(r, f).reshape(-1) for r in rows])
    sh["bc"] = np.ascontiguousarray(np.broadcast_to(row[None, :], (128, row.size)))
    sh["wsT"] = np.ascontiguousarray(np.transpose(inp["sgu_w"][0], (2, 0, 1)))
    sh["mask"] = np.ascontiguousarray(np.triu(np.ones((128, 128), f)))
    ws00 = np.zeros((16, 4, 16), f)
    for h in range(4):
        ws00[np.arange(16), h, np.arange(16)] = inp["sgu_w"][0, h, 0, 0]
    sh["ws00"] = ws00
    sh["ident"] = np.eye(128, dtype=f)
    lr = inp["ssm_lambda_re"][0]; li = inp["ssm_lambda_im"][0]; ldt = inp["ssm_log_dt"][0]
    def o1(a):
        return np.ascontiguousarray(a.reshape(16, 2, 64).transpose(1, 2, 0).reshape(128, 16))
    sh["lr1"] = o1(lr); sh["li1"] = o1(li)
    sh["ldt1"] = o1(np.broadcast_to(ldt[:, None], (32, 64)))
    def o1b(b):
        out = np.zeros((2, 64, 16, 2, 16), f)
        bb = b.reshape(16, 2, 64, 16)
        for g2 in range(2):
            out[g2, :, :, g2, :] = bb[:, g2].transpose(1, 0, 2)
        return out.reshape(128, 16, 32)
    sh["B1re"] = o1b(inp["ssm_b_re"][0]); sh["B1im"] = o1b(inp["ssm_b_im"][0])
    sh["C1re"] = o1b(np.transpose(inp["ssm_c_re"][0], (0, 2, 1)))
    sh["C1im"] = o1b(np.transpose(inp["ssm_c_im"][0], (0, 2, 1)))
    def o2(a):
        aa = a.reshape(4, 4, 2, 64)
        out = np.broadcast_to(aa.transpose(1, 0, 2, 3)[:, None, None], (4, 2, 16, 4, 2, 64))
        return np.ascontiguousarray(out.reshape(128, 4, 128))
    sh["lrc"] = np.ascontiguousarray(np.concatenate([sh.pop("lr1"), o2(lr).reshape(128, 512)], 1))
    sh["lic"] = np.ascontiguousarray(np.concatenate([sh.pop("li1"), o2(li).reshape(128, 512)], 1))
    sh["ldtc"] = np.ascontiguousarray(np.concatenate(
        [sh.pop("ldt1"), o2(np.broadcast_to(ldt[:, None], (32, 64))).reshape(128, 512)], 1))
    def o2b(b):
        bb = b.reshape(4, 4, 2, 64, 16)
        out = np.zeros((4, 2, 16, 4, 2, 64), f)
        for g2 in range(2):
            out[:, g2, :, :, g2, :] = bb[:, :, g2].transpose(1, 3, 0, 2)
        return out.reshape(128, 4, 128)
    sh["B2re"] = o2b(inp["ssm_b_re"][0]); sh["B2im"] = o2b(inp["ssm_b_im"][0])
    dsk = np.zeros((4, 2, 16, 4, 2, 16), f)
    dd = inp["ssm_d"][0].reshape(4, 4, 2, 16)
    for g2 in range(2):
        for c in range(16):
            dsk[:, g2, c, :, g2, c] = dd[:, :, g2, c].T
    sh["dsk"] = dsk.reshape(128, 4, 32)
    return sh


def _core_layout(inp, b):
    f = np.float32
    xs = inp["x_sample"][16 * b:16 * b + 16, 0]
    xtok = np.concatenate([inp["x_prompt"][b], xs], 0)
    ptok = np.concatenate([inp["p_prompt"][0, b], inp["p_sample"][0, 16 * b:16 * b + 16, 0]], 0)
    def oh(a):
        return np.ascontiguousarray(a.reshape(16, 16, 2, 64).transpose(2, 3, 1, 0).reshape(128, 16, 16))
    return {
        "x_tok": np.ascontiguousarray(xtok, f),
        "xT": np.ascontiguousarray(xtok.T, f),
        "pT": np.ascontiguousarray(ptok.T, f),
        "h0re": oh(inp["state_ssm_re"][0, 16 * b:16 * b + 16]),
        "h0im": oh(inp["state_ssm_im"][0, 16 * b:16 * b + 16]),
    }


DBG = {"stop": None}


def build_program(shared_shapes, core_shapes):
    nc = bass.Bass("TRN2", target_bir_lowering=False)
    P = Prog(nc)
    dbg_outs = {}

    def dump(name, t, shape):
        o = nc.dram_tensor("dbg_" + name, list(shape), F32, kind="ExternalOutput").ap()
        P.dma("pool", o, t.ap, reads=[t.key])
        dbg_outs[name] = o
    dr = {}
    for k, shp in list(shared_shapes.items()) + list(core_shapes.items()):
        dr[k] = nc.dram_tensor(k, list(shp), F32, kind="ExternalInput").ap()
    o_y = nc.dram_tensor("o_y", [TT, D], F32, kind="ExternalOutput").ap()
    o_spre = nc.dram_tensor("o_spre", [128, 16], F32, kind="ExternalOutput").ap()
    o_spim = nc.dram_tensor("o_spim", [128, 16], F32, kind="ExternalOutput").ap()
    o_ssre = nc.dram_tensor("o_ssre", [128, 16, 16], F32, kind="ExternalOutput").ap()
    o_ssim = nc.dram_tensor("o_ssim", [128, 16, 16], F32, kind="ExternalOutput").ap()
    o_vs = nc.dram_tensor("o_vs", [16, 512], F32, kind="ExternalOutput").ap()

    st = contextlib.ExitStack()
    with st:
        A32W = 22900
        A16W = 60500
        a32 = st.enter_context(nc.sbuf_tensor("a32", [128, A32W], F32))
        a16 = st.enter_context(nc.sbuf_tensor("a16", [128, A16W], BF16))
        psb = [st.enter_context(nc.psum_tensor(f"ps{i}", [128, 512], F32)) for i in range(8)]
        PS = [T(psb[i][:, :], f"ps{i}") for i in range(8)]

        class Arena:
            def __init__(s, t, size, nm):
                s.t, s.size, s.nm, s.off, s.n = t, size, nm, 0, 0

            def alloc(s, shape, key=None):
                n = int(np.prod(shape[1:]))
                assert s.off + n <= s.size, (s.nm, s.off, n, s.size)
                ap = s.t[0:shape[0], s.off:s.off + n]
                s.off += n
                s.n += 1
                key = key or f"{s.nm}_{s.n}"
                if len(shape) == 3:
                    ap = ap.rearrange("p (a b) -> p a b", a=shape[1])
                elif len(shape) == 4:
                    ap = ap.rearrange("p (a b c) -> p a b c", a=shape[1], b=shape[2])
                return T(ap, key)

        R32 = Arena(a32, A32W, "f")
        R16 = Arena(a16, A16W, "h")

        yT = R16.alloc([128, 4, TT], "yT")
        bpp_in = R32.alloc([128, 28], "bpp_in")
        bpp_glu = R32.alloc([128, 4], "bpp_glu")
        bpp_up = R32.alloc([128, 32], "bpp_up")
        P.dma("sp", bpp_in.ap, dr["b_in_pp"], writes=[bpp_in.key])
        P.dma("sp", bpp_glu.ap, dr["b_glu_pp"], writes=[bpp_glu.key])
        P.dma("sp", bpp_up.ap, dr["b_up_pp"], writes=[bpp_up.key])
        mark32, mark16 = R32.off, R16.off

        def tt(eng, out, a, b, op):
            P.op(eng, lambda e: e.tensor_tensor(out=out.ap, in0=a.ap, in1=b.ap, op=op),
                 reads=[a.key, b.key], writes=[out.key])

        def ts(eng, out, a, s1, s2, op0, op1=None):
            rd = [a.key]
            s1a = s1.ap if isinstance(s1, T) else s1
            s2a = s2.ap if isinstance(s2, T) else s2
            if isinstance(s1, T):
                rd.append(s1.key)
            if isinstance(s2, T):
                rd.append(s2.key)
            if op1 is None:
                P.op(eng, lambda e: e.tensor_scalar(out=out.ap, in0=a.ap, scalar1=s1a, scalar2=None, op0=op0),
                     reads=rd, writes=[out.key])
            else:
                P.op(eng, lambda e: e.tensor_scalar(out=out.ap, in0=a.ap, scalar1=s1a, scalar2=s2a, op0=op0, op1=op1),
                     reads=rd, writes=[out.key])

        def stt(eng, out, a, s, b, op0, op1):
            eng = "dve"
            rd = [a.key, b.key]
            sa = s.ap if isinstance(s, T) else s
            if isinstance(s, T):
                rd.append(s.key)
            P.op(eng, lambda e: e.scalar_tensor_tensor(out=out.ap, in0=a.ap, scalar=sa, in1=b.ap, op0=op0, op1=op1),
                 reads=rd, writes=[out.key])

        def cp(eng, out, a):
            P.op(eng, lambda e: e.tensor_copy(out=out.ap, in_=a.ap), reads=[a.key], writes=[out.key])

        def act(out, a, func, bias=None, scale=None):
            rd = [a.key]
            kw = {}
            if bias is not None:
                kw["bias"] = bias.ap if isinstance(bias, T) else bias
                if isinstance(bias, T):
                    rd.append(bias.key)
            if scale is not None:
                kw["scale"] = scale.ap if isinstance(scale, T) else scale
                if isinstance(scale, T):
                    rd.append(scale.key)
            P.op("act", lambda e: e.activation(out=out.ap, in_=a.ap, func=func, **kw), reads=rd, writes=[out.key])

        def memset(eng, out, v):
            P.op(eng, lambda e: e.memset(out.ap, v), writes=[out.key])

        def mm(out, lhsT, rhs, start, stop, inc, tp=None):
            kw = {}
            if tp is not None:
                kw["tile_position"] = tp
            P.op("pe", lambda e: e.matmul(out.ap, lhsT=lhsT.ap, rhs=rhs.ap, start=start, stop=stop, **kw),
                 reads=[lhsT.key, rhs.key], writes=[out.key], inc=inc, skip_self_w=True)

        def cmul(eng, ore, oim, are, aim, bre, bim, t1, t2):
            tt(eng, t1, are, bre, ALU.mult)
            tt(eng, t2, aim, bim, ALU.mult)
            tt(eng, ore, t1, t2, ALU.subtract)
            tt(eng, t1, are, bim, ALU.mult)
            tt(eng, t2, aim, bre, ALU.mult)
            tt(eng, oim, t1, t2, ALU.add)

        def exp_taylor(eng, out, x, deg):
            ts(eng, out, x, 1.0 / math.factorial(deg), None, ALU.mult)
            for kk in range(deg - 1, 0, -1):
                stt(eng, out, out, 1.0 / math.factorial(kk), x, ALU.add, ALU.mult)
            ts(eng, out, out, 1.0, None, ALU.add)

        def disc(eng, shape, lr, li, ldt, pref):
            al = lambda nm: R32.alloc(shape, pref + nm)
            s0 = al("s0"); s1 = al("s1"); s2 = al("s2"); s3 = al("s3"); s4 = al("s4"); s5 = al("s5")
            s6 = al("s6"); s7 = al("s7"); s8 = al("s8"); t1 = al("t1"); t2 = al("t2")
            dt, a, th, mag, imag, n, thr, sn, cs = s0, s1, s2, s3, s4, s5, s6, s7, s8
            c0 = 4.605170185988092
            ts(eng, t1, ldt, c0, 0.25, ALU.add, ALU.mult)
            exp_taylor(eng, dt, t1, 10)
            tt(eng, dt, dt, dt, ALU.mult)
            tt(eng, dt, dt, dt, ALU.mult)
            ts(eng, dt, dt, math.exp(-c0), None, ALU.mult)
            tt(eng, a, lr, dt, ALU.mult)
            tt(eng, th, li, dt, ALU.mult)
            exp_taylor(eng, mag, a, 6)
            ts(eng, t1, a, -1.0, None, ALU.mult)
            exp_taylor(eng, imag, t1, 6)
            ts(eng, n, th, PI, None, ALU.is_ge)
            for m_ in (3, 5, 7):
                stt(eng, n, th, m_ * PI, n, ALU.is_ge, ALU.add)
            stt(eng, thr, n, -2.0 * PI, th, ALU.mult, ALU.add)
            tt(eng, t2, thr, thr, ALU.mult)
            sc = [(-1.0) ** k / math.factorial(2 * k + 1) for k in range(10)]
            cc = [(-1.0) ** k / math.factorial(2 * k) for k in range(11)]
            ts(eng, sn, t2, sc[9], None, ALU.mult)
            for k in range(8, 0, -1):
                stt(eng, sn, sn, sc[k], t2, ALU.add, ALU.mult)
            stt(eng, sn, sn, 1.0, thr, ALU.add, ALU.mult)
            ts(eng, cs, t2, cc[10], None, ALU.mult)
            for k in range(9, 0, -1):
                stt(eng, cs, cs, cc[k], t2, ALU.add, ALU.mult)
            ts(eng, cs, cs, 1.0, None, ALU.add)
            are, aim, ire, iim = s0, s1, s2, s4
            tt(eng, are, mag, cs, ALU.mult)
            tt(eng, aim, mag, sn, ALU.mult)
            tt(eng, ire, imag, cs, ALU.mult)
            tt(eng, iim, imag, sn, ALU.mult)
            ts(eng, iim, iim, -1.0, None, ALU.mult)
            den, nr = s5, s6
            tt(eng, t1, lr, lr, ALU.mult)
            tt(eng, t2, li, li, ALU.mult)
            tt(eng, den, t1, t2, ALU.add)
            P.op("dve", lambda e: e.reciprocal(out=den.ap, in_=den.ap), reads=[den.key], writes=[den.key])
            ts(eng, nr, are, -1.0, None, ALU.add)
            cre, cim = s7, s8
            tt(eng, t1, nr, lr, ALU.mult)
            tt(eng, t2, aim, li, ALU.mult)
            tt(eng, t1, t1, t2, ALU.add)
            tt(eng, cre, t1, den, ALU.mult)
            tt(eng, t1, aim, lr, ALU.mult)
            tt(eng, t2, nr, li, ALU.mult)
            tt(eng, t1, t1, t2, ALU.subtract)
            tt(eng, cim, t1, den, ALU.mult)
            return dict(are=are, aim=aim, ire=ire, iim=iim, cre=cre, cim=cim, t1=t1, t2=t2, mag=mag)

        mark16_main = R16.off
        uT = R16.alloc([128, 4, TT], "uT")
        mark32, mark16 = R32.off, R16.off
        blocks = [(i * 512, 512) for i in range(4)] + [(SEQ, NS)]
        Bb = R16.alloc([128, 2, 16, 32], "Bb")
        CA = R16.alloc([128, 2, 16 * 17 * 32], "CA")
        mark16_lt = R16.off
        Wu = R16.alloc([128, 8, 512], "Wu")
        P.dma("pool", Wu.ap, dr["w_in"][:, 0:512].rearrange("(k p) n -> p k n", p=128), writes=[Wu.key])
        xb_s5 = [R16.alloc([128, 8, 512], f"xb_s5_{i}") for i in range(5)]
        for bi, (t0, n) in enumerate(blocks):
            P.dma("pool", xb_s5[bi].ap[:, :, 0:n], dr["xT"][:, t0:t0 + n].rearrange("(k p) n -> p k n", p=128),
                  writes=[xb_s5[bi].key])
        wsc = {}
        cast_list = [("w_in", 1024, 3584), ("w_glu", 512, 512), ("w_pa", 512, 1024), ("w_pb", 512, 1024),
                     ("w_out", 1024, 1024), ("w_up", 1024, 4096), ("w_down", 4096, 1024), ("w_ple", 256, 1024),
                     ("w_pg", 1024, 1024), ("xT", 1024, TT), ("pT", 256, TT)]
        for nm, r_, c_ in cast_list:
            wsc[nm] = nc.dram_tensor("sc_" + nm, [r_, c_], BF16, kind="Internal").ap()
            step = 256 if c_ <= 2064 else 128
            for k in range(0, r_, step):
                P.dma("pool", wsc[nm][k:k + step, :], dr[nm][k:k + step, :], writes=[f"sc_{nm}_{k // 128}"] +
                      ([f"sc_{nm}_{k // 128 + 1}"] if step == 256 else []), bg=True)
        def ld(name, shape, eng="sp"):
            t = R32.alloc(shape, name)
            P.dma(eng, t.ap, dr[name], writes=[t.key])
            return t

        dsk = ld("dsk", [128, 4, 32], "act")
        P1re = R32.alloc([128, 17, 16], "P1re"); P1im = R32.alloc([128, 17, 16], "P1im")
        I1re = R32.alloc([128, 16, 16], "I1re"); I1im = R32.alloc([128, 16, 16], "I1im")
        rho = R32.alloc([128, 16], "rho")
        mark32b = R32.off
        lrc = ld("lrc", [128, 528]); lic = ld("lic", [128, 528]); ldtc = ld("ldtc", [128, 528])
        B1re = ld("B1re", [128, 16, 32]); B1im = ld("B1im", [128, 16, 32])
        C1re = ld("C1re", [128, 16, 32]); C1im = ld("C1im", [128, 16, 32])
        B2re = ld("B2re", [128, 4, 128], "act"); B2im = ld("B2im", [128, 4, 128], "act")
        dd = disc("dve", [128, 528], lrc, lic, ldtc, "dd")
        d1 = {k: v[:, 0:16] for k, v in dd.items()}
        d2 = {k: v[:, 16:528].r("p (a b) -> p a b", a=4) for k, v in dd.items()}
        pt1 = R32.alloc([128, 8, 16], "pt1"); pt2 = R32.alloc([128, 8, 16], "pt2")
        tt("dve", rho, d1["mag"], d1["mag"], ALU.mult)
        for _ in range(3):
            tt("dve", rho, rho, rho, ALU.mult)
        memset("dve", P1re[:, 0, :], 1.0)
        memset("dve", P1im[:, 0, :], 0.0)
        cp("dve", P1re[:, 1, :], d1["are"])
        cp("dve", P1im[:, 1, :], d1["aim"])
        n_ = 1
        while n_ < 16:
            sl_o = slice(n_ + 1, 2 * n_ + 1)
            sl_i = slice(1, n_ + 1)
            bre = P1re[:, n_:n_ + 1, :].bc([128, n_, 16]); bim = P1im[:, n_:n_ + 1, :].bc([128, n_, 16])
            cmul("dve", P1re[:, sl_o, :], P1im[:, sl_o, :],
                 P1re[:, sl_i, :], P1im[:, sl_i, :], bre, bim, pt1[:, 0:n_, :], pt2[:, 0:n_, :])
            n_ *= 2
        w1 = dd["t1"][:, 16:528].r("p (a b) -> p a b", a=16); w2 = dd["t2"][:, 16:528].r("p (a b) -> p a b", a=16)
        cob_re = d1["cre"].us(2).bc([128, 16, 32]); cob_im = d1["cim"].us(2).bc([128, 16, 32])
        cmul("dve", Bb[:, 0], Bb[:, 1], B1re, B1im, cob_re, cob_im, w1, w2)
        CAv = lambda ri: CA[:, ri, :].r("p (q m c) -> p q m c", q=16, m=17)
        y1_ = R32.alloc([128, 17, 32], "y1_"); y2_ = R32.alloc([128, 17, 32], "y2_")
        z1_ = R32.alloc([128, 17, 32], "z1_"); z2_ = R32.alloc([128, 17, 32], "z2_")
        for q in range(16):
            cr = C1re[:, q:q + 1, :].bc([128, 17, 32]); ci = C1im[:, q:q + 1, :].bc([128, 17, 32])
            pr = P1re[:, :, q:q + 1].bc([128, 17, 32]); pi_ = P1im[:, :, q:q + 1].bc([128, 17, 32])
            eng, a1, a2 = ("dve", z1_, z2_) if q % 2 == 0 else ("dve", y1_, y2_)
            tt(eng, a1, cr, pr, ALU.mult)
            tt(eng, a2, ci, pi_, ALU.mult)
            tt(eng, CAv(0)[:, q].k(f"CA0_{q}"), a1, a2, ALU.subtract)
            tt(eng, a1, cr, pi_, ALU.mult)
            tt(eng, a2, ci, pr, ALU.mult)
            stt(eng, CAv(1)[:, q].k(f"CA1_{q}"), a1, -1.0, a2, ALU.mult, ALU.subtract)
        Are = P1re[:, 16, :]; Aim = P1im[:, 16, :]
        cp("dve", I1re[:, 0, :], d1["ire"]); cp("dve", I1im[:, 0, :], d1["iim"])
        n_ = 1
        while n_ < 16:
            sl_o = slice(n_, 2 * n_); sl_i = slice(0, n_)
            bre = I1re[:, n_ - 1:n_, :].bc([128, n_, 16]); bim = I1im[:, n_ - 1:n_, :].bc([128, n_, 16])
            cmul("dve", I1re[:, sl_o, :], I1im[:, sl_o, :],
                 I1re[:, sl_i, :], I1im[:, sl_i, :], bre, bim, pt1[:, 0:n_, :], pt2[:, 0:n_, :])
            n_ *= 2
        ai15re = I1re[:, 14, :]; ai15im = I1im[:, 14, :]

        for bi, (t0, n) in enumerate(blocks):
            xb = xb_s5[bi]
            for m in range(4):
                ps = PS[m % 2]
                for k in range(8):
                    mm(ps[:, 0:n], Wu[:, k, m * 128:(m + 1) * 128], xb[:, k, 0:n], k == 0, k == 7, k == 7)
                act(uT[:, m, t0:t0 + n], ps[:, 0:n], AF.Identity, bias=bpp_in[:, m:m + 1])
        if DBG["stop"] == "p0":
            dump("uT", uT, [128, 4, TT])
            P.emit()
            return nc
        P.barrier()
        R16.off = mark16_lt

        LT = R16.alloc([128, 4, 2 * 16 * 128], "LT")
        LTv = lambda qt, ri: LT[:, qt, :].r("p (r j n) -> p r j n", r=2, j=16)[:, ri].k(f"LT{qt}")
        Bb2re = R32.alloc([128, 4, 128], "Bb2re"); Bb2im = R32.alloc([128, 4, 128], "Bb2im")
        cmul("dve", Bb2re, Bb2im, B2re, B2im, d2["cre"], d2["cim"], d2["t1"], d2["t2"])
        IPre = R32.alloc([128, 4, 128], "IPre"); IPim = R32.alloc([128, 4, 128], "IPim")
        iq1 = dd["t1"][:, 16:144]; iq2 = dd["t2"][:, 16:144]
        L32re = R32.alloc([128, 16, 128], "L32re"); L32im = R32.alloc([128, 16, 128], "L32im")
        q1 = R32.alloc([128, 8, 128], "q1"); q2 = R32.alloc([128, 8, 128], "q2")
        for qt in range(4):
            cs_ = slice(0, 128)
            cp("dve", IPre[:, 0, :], d2["ire"][:, qt, :]); cp("dve", IPim[:, 0, :], d2["iim"][:, qt, :])
            for lv in range(3):
                cmul("dve", IPre[:, lv + 1, :], IPim[:, lv + 1, :], IPre[:, lv, :], IPim[:, lv, :],
                     IPre[:, lv, :], IPim[:, lv, :], iq1, iq2)
            cmul("dve", L32re[:, 0, :], L32im[:, 0, :], Bb2re[:, qt, :], Bb2im[:, qt, :],
                 IPre[:, 0, cs_], IPim[:, 0, cs_], q1[:, 0, :], q2[:, 0, :])
            n_ = 1
            lv = 0
            while n_ < 16:
                sl_o = slice(n_, 2 * n_); sl_i = slice(0, n_)
                bre = IPre[:, lv, cs_].us(1).bc([128, n_, 128]); bim = IPim[:, lv, cs_].us(1).bc([128, n_, 128])
                cmul("dve", L32re[:, sl_o, :], L32im[:, sl_o, :],
                     L32re[:, sl_i, :], L32im[:, sl_i, :], bre, bim, q1[:, 0:n_, :], q2[:, 0:n_, :])
                n_ *= 2
                lv += 1
            act(LTv(qt, 0), L32re, AF.Copy)
            act(LTv(qt, 1), L32im, AF.Copy)

        Kt = R16.alloc([128, 4, 16, 32], "Kt")
        kt32 = R32.alloc([128, 16, 32], "kt32")
        for qt in range(4):
            ps = PS[2 + qt % 2]
            for q4 in range(4):
                q = qt * 4 + q4
                for ri in range(2):
                    mm(ps[32 * q4:32 * q4 + 32, :], Bb[:, ri, q, :],
                       CA[:, ri, q * 544:q * 544 + 512].k(f"CA{ri}_{q}"), ri == 0, ri == 1, (ri == 1 and q4 == 3), tp=(0, 32 * q4))
            cp("dve", kt32, ps.r("p (m c) -> p m c", m=16))
            tt("dve", kt32[:, 0, :], kt32[:, 0, :], dsk[:, qt, :], ALU.add)
            cp("dve", Kt[:, qt], kt32)
        Hp = R16.alloc([128, 2, 16, NK + NS], "Hp")
        if DBG["stop"] == "p1a":
            dump("P1re", P1re, [128, 17, 16]); dump("P1im", P1im, [128, 17, 16])
            dump("I1re", I1re, [128, 16, 16]); dump("I1im", I1im, [128, 16, 16])
            dump("Bb", Bb, [128, 2, 16, 32]); dump("CA", CA, [128, 2, 16 * 17 * 32])
            dump("LT", LT, [128, 4, 2 * 16 * 128]); dump("Kt", Kt, [128, 4, 16, 32])
            P.emit()
            return nc
        P.barrier()
        R32.off = mark32b

        h0re = ld("h0re", [128, 16, 16], "act"); h0im = ld("h0im", [128, 16, 16], "act")
        Gp = R32.alloc([128, 2, 16, 128], "Gp")
        Gs = R32.alloc([128, 2, 16, 16], "Gs")
        for ri in range(2):
            for qt in range(4):
                for j in range(16):
                    for q4 in range(4):
                        ps = PS[4 + q4]
                        mm(ps[:, qt * 128:(qt + 1) * 128], LTv(qt, ri)[32 * q4:32 * q4 + 32, j, :],
                           uT[32 * q4:32 * q4 + 32, qt, j:SEQ:16], j == 0, j == 15,
                           (j == 15 and qt == 3), tp=(32 * q4, 0))
            for q4 in range(4):
                cp("dve", Gp[:, ri, q4:16:4, :], PS[4 + q4].r("p (a k) -> p a k", a=4))
        for q4 in range(4):
            ps = PS[4 + q4]
            for ri in range(2):
                for qt in range(4):
                    cb = (ri * 4 + qt) * 16
                    mm(ps[:, cb:cb + 16], LTv(qt, ri)[32 * q4:32 * q4 + 32, 15, :],
                       uT[32 * q4:32 * q4 + 32, qt, SEQ:TT], True, True, (ri == 1 and qt == 3), tp=(32 * q4, 0))
        for q4 in range(4):
            for ri in range(2):
                cp("dve", Gs[:, ri, q4:16:4, :], PS[4 + q4][:, ri * 64:ri * 64 + 64].r("p (a t) -> p a t", a=4))
        if DBG["stop"] == "p1b1":
            dump("Gp", Gp, [128, 2, 16, 128]); dump("Gs", Gs, [128, 2, 16, 16])
            P.emit()
            return nc
        KA = R32.alloc([128, 2, 16, 128], "KA")
        k1 = R32.alloc([128, 16, 128], "k1"); k2 = R32.alloc([128, 16, 128], "k2")
        Wre = R32.alloc([128, 16, 128], "Wre"); Wim = R32.alloc([128, 16, 128], "Wim")
        RHO = R32.alloc([128, 16, 128], "RHO")
        ure = R32.alloc([128, 16], "ure"); uim = R32.alloc([128, 16], "uim")
        sp1 = R32.alloc([128, 16], "sp1"); sp2 = R32.alloc([128, 16], "sp2")
        P.op("dve", lambda e: e.reciprocal(out=sp1.ap, in_=rho.ap), reads=[rho.key], writes=[sp1.key])
        tt("dve", ure, Are, sp1, ALU.mult)
        tt("dve", uim, Aim, sp1, ALU.mult)
        tt("dve", sp1, ure, ure, ALU.mult)
        tt("dve", sp2, uim, uim, ALU.mult)
        tt("dve", sp1, sp1, sp2, ALU.add)
        ts("dve", sp1, sp1, -0.5, 1.5, ALU.mult, ALU.add)
        tt("dve", ure, ure, sp1, ALU.mult)
        tt("dve", uim, uim, sp1, ALU.mult)
        cp("dve", Wre[:, :, 0], ure); cp("dve", Wim[:, :, 0], uim)
        n_ = 1
        while n_ < 128:
            sl_o = slice(n_, 2 * n_); sl_i = slice(0, n_)
            bre = Wre[:, :, n_ - 1:n_].bc([128, 16, n_]); bim = Wim[:, :, n_ - 1:n_].bc([128, 16, n_])
            cmul("dve", Wre[:, :, sl_o], Wim[:, :, sl_o], Wre[:, :, sl_i], Wim[:, :, sl_i], bre, bim,
                 k1[:, :, 0:n_], k2[:, :, 0:n_])
            n_ *= 2
        cp("pool", RHO, rho.us(2).bc([128, 16, 128]))
        memset("pool", RHO[:, :, 0:1], 0.0)
        Ab_re = Are.us(2).bc([128, 16, 128]); Ab_im = Aim.us(2).bc([128, 16, 128])
        cmul("dve", KA[:, 0], KA[:, 1], Gp[:, 0], Gp[:, 1], Ab_re, Ab_im, k1, k2)
        tt("dve", k1, Wre, KA[:, 0], ALU.mult)
        tt("dve", k2, Wim, KA[:, 1], ALU.mult)
        tt("dve", Gp[:, 0], k1, k2, ALU.add)
        tt("dve", k1, Wre, KA[:, 1], ALU.mult)
        tt("dve", k2, Wim, KA[:, 0], ALU.mult)
        tt("dve", Gp[:, 1], k1, k2, ALU.subtract)
        for ri in range(2):
            P.op("dve", lambda e, ri=ri: e.tensor_tensor_scan(
                out=KA.ap[:, ri].rearrange("p q k -> p (q k)"), data0=RHO.ap.rearrange("p q k -> p (q k)"),
                data1=Gp.ap[:, ri].rearrange("p q k -> p (q k)"), initial=0.0, op0=ALU.mult, op1=ALU.add),
                reads=[RHO.key, Gp.key], writes=[KA.key])
        cmul("dve", Gp[:, 0], Gp[:, 1], KA[:, 0], KA[:, 1], Wre, Wim, k1, k2)
        Hre, Him = Gp[:, 0], Gp[:, 1]
        KAre, KAim = Hre, Him
        fin = R32.alloc([128, 2, 16], "fin")
        cp("dve", fin[:, 0, :], Hre[:, :, 127]); cp("dve", fin[:, 1, :], Him[:, :, 127])
        P.dma("sp", o_spre, fin.ap[:, 0, :], reads=[fin.key])
        P.dma("sp", o_spim, fin.ap[:, 1, :], reads=[fin.key])
        memset("dve", Hp[:, :, :, 0:1], 0.0)
        cp("dve", Hp[:, 0, :, 1:128], Hre[:, :, 0:127])
        cp("pool", Hp[:, 1, :, 1:128], Him[:, :, 0:127])
        s1 = R32.alloc([128, 16, 16], "s1"); s2 = R32.alloc([128, 16, 16], "s2")
        hsre = R32.alloc([128, 16, 16], "hsre"); hsim = R32.alloc([128, 16, 16], "hsim")
        cmul("dve", hsre, hsim, h0re, h0im, ai15re.us(2).bc([128, 16, 16]), ai15im.us(2).bc([128, 16, 16]), s1, s2)
        cp("dve", Hp[:, 0, :, 128:144], hsre)
        cp("dve", Hp[:, 1, :, 128:144], hsim)
        s3 = R32.alloc([128, 16, 16], "s3"); s4 = R32.alloc([128, 16, 16], "s4")
        tt("dve", s3, hsre, Gs[:, 0], ALU.add)
        tt("dve", s4, hsim, Gs[:, 1], ALU.add)
        nsre = R32.alloc([128, 16, 16], "nsre"); nsim = R32.alloc([128, 16, 16], "nsim")
        cmul("dve", nsre, nsim, s3, s4, Are.us(2).bc([128, 16, 16]), Aim.us(2).bc([128, 16, 16]), s1, s2)
        P.dma("sp", o_ssre, nsre.ap, reads=[nsre.key])
        P.dma("sp", o_ssim, nsim.ap, reads=[nsim.key])

        if DBG["stop"] == "p1b2":
            dump("KAre", KAre, [128, 16, 128]); dump("KAim", KAim, [128, 16, 128]); dump("Hp", Hp, [128, 2, 16, NK + NS])
            P.emit()
            return nc
        ystg = [R16.alloc([128, 512], "ystg0"), R16.alloc([128, 512], "ystg1")]
        for qt in range(4):
            for jg in range(4):
                YB = [PS[q4_ + 4 * ((qt * 4 + jg) % 2)] for q4_ in range(4)]
                for jl in range(4):
                    jt = jg * 4 + jl
                    cs_ = slice(jl * 128, jl * 128 + 128)
                    for ri in range(2):
                        for q4 in range(4):
                            q = qt * 4 + q4
                            rows = slice(32 * q4, 32 * q4 + 32)
                            mm(YB[q4][rows, cs_], CAv(ri)[:, q, jt + 1, :].k(f"CA{ri}_{q}"), Hp[:, ri, q, 0:128],
                               ri == 0, False, False, tp=(0, 32 * q4))
                    for js in range(jt + 1):
                        last = js == jt
                        for q4 in range(4):
                            rows = slice(32 * q4, 32 * q4 + 32)
                            mm(YB[q4][rows, cs_], Kt[rows, qt, jt - js, :], uT[rows, qt, js:SEQ:16],
                               False, last, (last and jl == 3), tp=(32 * q4, 32 * q4))
                if DBG["stop"] == "yA" and qt == 0 and jg == 0:
                    P.emit()
                    return nc
                stg = ystg[(qt * 4 + jg) % 2]
                for q4 in range(4):
                    rows = slice(32 * q4, 32 * q4 + 32)
                    act(stg[rows, :].k(f"{stg.key}_{q4}"), YB[q4][rows, :], AF.Gelu_apprx_tanh)
                P.op("dve", lambda e, stg=stg, qt=qt, jg=jg: e.tensor_copy(
                    out=yT.ap[:, qt, 0:SEQ].rearrange("p (k j) -> p j k", j=16)[:, jg * 4:(jg + 1) * 4, :],
                    in_=stg.ap.rearrange("p (j k) -> p j k", j=4)),
                    reads=[f"{stg.key}_{q4_}" for q4_ in range(4)], writes=[yT.key])
                if DBG["stop"] == "yB" and qt == 0 and jg == 0:
                    P.emit()
                    return nc
                if DBG["stop"] == "yC" and qt == 0 and jg == 1:
                    P.emit()
                    return nc
                if DBG["stop"] == "yD" and qt == 0 and jg == 3:
                    P.emit()
                    return nc
            ps = PS[4 + qt % 2]
            for q4 in range(4):
                q = qt * 4 + q4
                rows = slice(32 * q4, 32 * q4 + 32)
                for ri in range(2):
                    mm(ps[rows, 0:16], CAv(ri)[:, q, 16, :].k(f"CA{ri}_{q}"), Hp[:, ri, q, 128:144], ri == 0, False, False,
                       tp=(0, 32 * q4))
                mm(ps[rows, 0:16], Kt[rows, qt, 0, :], uT[rows, qt, SEQ:TT], False, True, q4 == 3,
                   tp=(32 * q4, 32 * q4))
            act(yT[:, qt, SEQ:TT], ps[:, 0:16], AF.Gelu_apprx_tanh)
            if DBG["stop"] == "yE" and qt == 0:
                P.emit()
                return nc
            if DBG["stop"] == "yF" and qt == 1:
                P.emit()
                return nc

        if DBG["stop"] == "p1":
            dump("yT", yT, [128, 4, TT]); dump("Hp", Hp, [128, 2, 16, NK + NS]); dump("Gp", Gp, [128, 2, 16, 128])
            P.emit()
            return nc
        P.barrier()
        R32.off, R16.off = mark32, mark16_main
        BCW = 512 * 3 + 512 + 7 * 1024 + 4
        bc = R32.alloc([128, BCW], "bc")
        P.dma("sp", bc.ap, dr["bc"], writes=[bc.key])
        o_ = [0]
        def bcs(n):
            v = bc[:, o_[0]:o_[0] + n]
            o_[0] += n
            return v
        bc_binv = bcs(512); bc_sg = bcs(512); bc_sb = bcs(512); bc_bs = bcs(512)
        bc_bo = bcs(1024); bc_g1 = bcs(1024); bc_b1 = bcs(1024); bc_bdn = bcs(1024)
        bc_bpg = bcs(1024); bc_g2 = bcs(1024); bc_b2 = bcs(1024); bc_bs0 = bcs(4)
        wsT32 = R32.alloc([128, 4, 128], "wsT32"); mask = R32.alloc([128, 128], "mask")
        P.dma("act", wsT32.ap, dr["wsT"], writes=[wsT32.key])
        P.dma("act", mask.ap, dr["mask"], writes=[mask.key])
        wsTb = R16.alloc([128, 4, 128], "wsTb")
        tt("dve", wsTb, wsT32, mask.us(1).bc([128, 4, 128]), ALU.mult)
        ws00b = R16.alloc([16, 4, 16], "ws00b")
        P.dma("pool", ws00b.ap, dr["ws00"], writes=[ws00b.key])
        identb = R16.alloc([128, 128], "identb")
        P.dma("pool", identb.ap, dr["ident"], writes=[identb.key])
        PSB4 = [T(psb[6][:, :].bitcast(BF16)[:, 0:512].rearrange("p (j t) -> p j t", j=4), "ps6"),
                T(psb[7][:, :].bitcast(BF16)[:, 0:512].rearrange("p (j t) -> p j t", j=4), "ps7")]

        NRING = 28
        ring = [R16.alloc([128, 512], f"ring{i}") for i in range(NRING)]
        rn = [0]
        def wload(nm, k, c0_):
            t = ring[rn[0] % NRING]
            rn[0] += 1
            P.dma("sp", t.ap[:, 0:512], wsc[nm][k * 128:(k + 1) * 128, c0_:c0_ + 512],
                  reads=[f"sc_{nm}_{k}"], writes=[t.key])
            return t
        xTb = R16.alloc([128, 8, 512], "xTb")
        pTb = R16.alloc([128, 2, 512], "pTb")
        strm = R32.alloc([128, 4, 1024], "strm")
        ub = R32.alloc([128, 4, 512], "ub")
        vg4 = R32.alloc([128, 4, 512], "vg4")
        vy_raw = R16.alloc([128, 4096], "vy")
        vhb = vy_raw[:, 0:2048].r("p (a b) -> p a b", a=4).k("vhb")
        yb = vy_raw[:, 2048:4096].r("p (a b) -> p a b", a=4).k("yb")
        x1b4 = vy_raw.r("p (a b) -> p a b", a=4).k("x1b4")
        ya = R16.alloc([128, 4, 512], "ya")
        sga = R32.alloc([128, 512], "sga"); sgb = R32.alloc([128, 512], "sgb")
        mg2 = R32.alloc([128, 512], "mg2"); mg3 = R32.alloc([128, 512], "mg3")
        sgc = R32.alloc([128, 512], "sgc")
        rl2 = R32.alloc([128, 512], "rl2")
        merged = R16.alloc([128, 8, 512], "merged")
        x1T = R16.alloc([128, 8, 512], "x1T")
        hid_raw = R16.alloc([128, 32 * 512], "hid")
        hid = hid_raw.r("p (a b) -> p a b", a=32)
        mgA = T(hid_raw.ap.bitcast(F32), "hid").r("p (a b) -> p a b", a=8)
        rl = R32.alloc([128, 512], "rl")
        stats = R32.alloc([128, 4, 2, 6], "stats"); mv = R32.alloc([128, 4, 2], "mv")
        rstd = R32.alloc([128, 4], "rstd"); nmr = R32.alloc([128, 4], "nmr")
        SB1 = (stats, mv, rstd, nmr, R32.alloc([128, 4], "vv1"), R32.alloc([128, 4], "tq1"))
        SB2 = (R32.alloc([128, 4, 2, 6], "stats2"), R32.alloc([128, 4, 2], "mv2"),
               R32.alloc([128, 4], "rstd2"), R32.alloc([128, 4], "nmr2"),
               R32.alloc([128, 4], "vv2"), R32.alloc([128, 4], "tq2"))
        tmp1k = R32.alloc([128, 1024], "tmp1k")

        def ln_stat_step(xc, cw, c, h, sb=None):
            stats = (sb or SB1)[0]
            P.op("dve", lambda e: e.bn_stats(out=stats.ap[0:cw, c, h, :], in_=xc.ap[0:cw, h * 512:(h + 1) * 512]),
                 reads=[xc.key], writes=[stats.key])

        def layer_norm_steps(chunks, cw, width, g, b, out_bfs=None, sb=None, with_stats=True, offload=False):
            stats, mv, rstd, nmr, vv, tq = sb or SB1
            nch = len(chunks)
            nchk = width // 512
            steps = []
            if with_stats:
                for c in range(nch):
                    for h in range(nchk):
                        steps.append(lambda c=c, h=h: ln_stat_step(chunks[c], cw, c, h, sb))
            for c in range(nch):
                steps.append(lambda c=c: P.op(
                    "dve", lambda e: e.bn_aggr(out=mv.ap[0:cw, c, :], in_=stats.ap[0:cw, c, 0:nchk, :]),
                    reads=[stats.key], writes=[mv.key]))

            def rs():
                y_ = rstd[0:cw, 0:nch]
                ts("dve", y_, mv[0:cw, 0:nch, 1], EPS, None, ALU.add)
                act(y_, y_, AF.Sqrt)
                P.op("dve", lambda e: e.reciprocal(out=y_.ap, in_=y_.ap), reads=[y_.key], writes=[y_.key])
                stt("dve", nmr[0:cw, 0:nch], mv[0:cw, 0:nch, 0], -1.0, y_, ALU.mult, ALU.mult)
            steps.append(rs)
            for c in range(nch):
                def aff(c=c):
                    xc = chunks[c][0:cw, 0:width]
                    if offload:
                        act(xc, xc, AF.Identity, bias=nmr[0:cw, c:c + 1], scale=rstd[0:cw, c:c + 1])
                        tt("dve", xc, xc, g[0:cw, 0:width], ALU.mult)
                        tt("pool" if offload == "pool" else "dve", xc, xc, b[0:cw, 0:width], ALU.add)
                    else:
                        stt("dve", xc, xc, mv[0:cw, c, 0:1], g[0:cw, 0:width], ALU.subtract, ALU.mult)
                        stt("dve", xc, xc, rstd[0:cw, c:c + 1], b[0:cw, 0:width], ALU.mult, ALU.add)
                    if out_bfs is not None:
                        act(out_bfs[c][0:cw, 0:width], xc, AF.Copy)
                steps.append(aff)
            return steps

        STRM = [T(strm.ap[:, c, :], f"strm{c}") for c in range(4)]
        SKEYS = [f"strm{c}" for c in range(4)]
        VG = [T(vg4.ap[:, c, :], f"vg{c}") for c in range(4)]
        VHB = [T(vhb.ap[:, c, :], f"vhb{c}") for c in range(4)]
        X1B = [T(x1b4.ap[:, c, :], f"x1b{c}") for c in range(4)]
        deferred = []

        def tick(k=1):
            for _ in range(k):
                if deferred:
                    deferred.pop(0)()

        stores = []

        def flush():
            while deferred:
                deferred.pop(0)()
            while stores:
                stores.pop(0)()

        def finish_block(t0, n):
            nch = max(1, n // 128)
            cw = min(n, 128)
            deferred.extend(layer_norm_steps(STRM[0:nch], cw, 1024, bc_g2, bc_b2, sb=SB2, with_stats=False, offload="pool"))
            stores.append(lambda: P.dma("sp", o_y[t0:t0 + n, :].rearrange("(c p) d -> p c d", p=cw),
                                        strm.ap[0:cw, 0:nch, :], reads=SKEYS[0:nch]))

        pending_finish = None
        for bi, (t0, n) in enumerate(blocks):
            nch = max(1, n // 128)
            cw = min(n, 128)
            P.dma("sp", xTb.ap[:, :, 0:n], wsc["xT"][:, t0:t0 + n].rearrange("(k p) n -> p k n", p=128),
                  reads=[f"sc_xT_{k}" for k in range(8)], writes=[xTb.key])
            wp = [wload("w_in", k, 512) for k in range(8)]
            for m in range(4):
                ps = PS[m % 2]
                for k in range(8):
                    mm(ps[:, 0:n], wp[k][:, m * 128:(m + 1) * 128], xTb[:, k, 0:n], k == 0, k == 7, k == 7)
                act(ub[:, m, 0:n], ps[:, 0:n], AF.Gelu_apprx_tanh, bias=bpp_in[:, 4 + m:5 + m])
            if pending_finish is not None:
                finish_block(*pending_finish)
                pending_finish = None
            wv = [wload("w_in", k, 1024) for k in range(8)]
            for c in range(nch):
                ps = PS[2 + c]
                for k in range(8):
                    mm(ps[0:cw, :], xTb[:, k, c * 128:c * 128 + cw], wv[k][:, 0:512], k == 0, k == 7, k == 7)
                tick(1)
            for c in range(nch):
                tt("dve", VG[c][0:cw], PS[2 + c][0:cw, :], bc_binv[0:cw], ALU.add)
            for c in range(nch):
                act(VG[c][0:cw], VG[c][0:cw], AF.Gelu_apprx_tanh)
            deferred.extend(layer_norm_steps(VG[0:nch], cw, 512, bc_sg, bc_sb, out_bfs=VHB[0:nch], offload=True))
            if n == NS:
                stores.append(lambda: P.dma("sp", o_vs, vg4.ap[0:NS, 0, :], reads=["vg0"]))
            wg = [wload("w_glu", k, 0) for k in range(4)]
            for m in range(4):
                ps = PS[m % 2]
                for k in range(4):
                    mm(ps[:, 0:n], wg[k][:, m * 128:(m + 1) * 128], yT[:, k, t0:t0 + n], k == 0, k == 3, k == 3)
                sg_ = sga if m % 2 == 0 else sgc
                act(sg_[:, 0:n], ps[:, 0:n], AF.Sigmoid, bias=bpp_glu[:, m:m + 1])
                tt("dve", ya[:, m, 0:n], sg_[:, 0:n], yT[:, m, t0:t0 + n], ALU.mult)
                tick(2)
            P.dma("sp", pTb.ap[:, :, 0:n], wsc["pT"][:, t0:t0 + n].rearrange("(k p) n -> p k n", p=128),
                  reads=[f"sc_pT_{k}" for k in range(2)], writes=[pTb.key])

            def merge_half(half):
                ysrc = ya if half == 0 else yb
                wnm = "w_pa" if half == 0 else "w_pb"
                for mg in range(2):
                    wsrc = [wload(wnm, k, mg * 512) for k in range(4)]
                    wga = [wload("w_in", k, 1536 + half * 1024 + mg * 512) for k in range(8)]
                    for ml in range(4):
                        m = mg * 4 + ml
                        mcol = 12 + half * 8 + m
                        ps = PS[m % 2]
                        for k in range(8):
                            mm(ps[:, 0:n], wga[k][:, ml * 128:(ml + 1) * 128], xTb[:, k, 0:n], k == 0, k == 7, k == 7)
                        sg_ = sga if m % 2 == 0 else sgc
                        mg_ = mg2 if m % 2 == 0 else mg3
                        act(sg_[:, 0:n], ps[:, 0:n], AF.Sigmoid, bias=bpp_in[:, mcol:mcol + 1])
                        ps2 = PS[2 + m % 2]
                        for k in range(4):
                            mm(ps2[:, 0:n], wsrc[k][:, ml * 128:(ml + 1) * 128], ysrc[:, k, 0:n], k == 0, k == 3, k == 3)
                        if half == 0:
                            tt("dve", mgA[:, m, 0:n].k(f"mgA{m}"), sg_[:, 0:n], ps2[:, 0:n], ALU.mult)
                            tick(2)
                        else:
                            tt("dve", mg_[:, 0:n], sg_[:, 0:n], ps2[:, 0:n], ALU.mult)
                            tt("pool", merged[:, m, 0:n].k(f"merged{m}"), mg_[:, 0:n], mgA[:, m, 0:n].k(f"mgA{m}"),
                               ALU.add)

            merge_half(0)
            flush()
            P.dma("sp", strm.ap[0:cw, 0:nch, :], dr["x_tok"][t0:t0 + n, :].rearrange("(c p) d -> p c d", p=cw),
                  writes=SKEYS[0:nch])
            for h in range(4):
                ps = PS[h % 2]
                if n == NS:
                    mm(ps[:, 0:NS], VHB[0][0:NS, h * 128:(h + 1) * 128], ws00b[0:NS, h, :], True, True, True)
                    ts("dve", sgb[:, 0:NS], ps[:, 0:NS], bc_bs0[:, h:h + 1], None, ALU.add)
                else:
                    for c in range(nch):
                        mm(ps[:, c * 128:(c + 1) * 128], VHB[c][:, h * 128:(h + 1) * 128], wsTb[:, h, :],
                           True, True, c == nch - 1)
                    tt("dve", sgb.r("p (c t) -> p c t", c=4), ps.r("p (c t) -> p c t", c=4),
                       bc_bs[:, h * 128:(h + 1) * 128].us(1).bc([128, 4, 128]), ALU.add)
                tt("dve", yb[:, h, 0:n], sgb[:, 0:n], ub[:, h, 0:n], ALU.mult)
            merge_half(1)
            if DBG["stop"] == "mF" and bi == 0:
                dump("merged", merged, [128, 8, 512])
                P.emit()
                return nc
            for half in range(2):
                hs = slice(half * 512, (half + 1) * 512)
                wo = [wload("w_out", k, half * 512) for k in range(8)]
                for c in range(nch):
                    ps = PS[4 + c]
                    for k in range(8):
                        mm(ps[0:cw, :], merged[:, k, c * 128:c * 128 + cw].k(f"merged{k}"), wo[k][:, 0:512],
                           k == 0, k == 7, k == 7)
                    tt("dve", tmp1k[0:cw, hs], ps[0:cw, :], bc_bo[0:cw, hs], ALU.add)
                    stt("dve", STRM[c][0:cw, hs], STRM[c][0:cw, hs], ALPHA, tmp1k[0:cw, hs], ALU.mult, ALU.add)
                    ln_stat_step(STRM[c], cw, c, half, SB1)
            ln1 = layer_norm_steps(STRM[0:nch], cw, 1024, bc_g1, bc_b1, out_bfs=X1B[0:nch], with_stats=False)
            for st_ in ln1[0:nch + 1]:
                st_()
            def transposes(c):
                for kq in range(2):
                    pst = PSB4[kq]
                    for j in range(4):
                        k = kq * 4 + j
                        P.op("pe", lambda e, pst=pst, k=k, j=j, cw=cw, c=c: e.transpose(
                            pst.ap[:, j, 0:cw], x1b4.ap[0:cw, c, k * 128:(k + 1) * 128], identb.ap[0:cw, 0:cw]),
                            reads=[f"x1b{c}", identb.key], writes=[pst.key], inc=(j == 3), skip_self_w=True)
                    act(x1T[:, kq * 4:(kq + 1) * 4, c * 128:c * 128 + cw], pst[:, :, 0:cw], AF.Copy)
            for c in range(nch):
                ln1[nch + 1 + c]()
                if c >= 1:
                    transposes(c - 1)
            transposes(nch - 1)
            if DBG["stop"] == "mG" and bi == 0:
                dump("strm", strm, [128, 4, 1024]); dump("x1T", x1T, [128, 8, 512])
                P.emit()
                return nc
            for cg in range(8):
                wu_ = [wload("w_up", k, cg * 512) for k in range(8)]
                for ml in range(4):
                    ps = PS[ml % 2]
                    for k in range(8):
                        mm(ps[:, 0:n], wu_[k][:, ml * 128:(ml + 1) * 128], x1T[:, k, 0:n], k == 0, k == 7, k == 7)
                    mi = cg * 4 + ml
                    rl_ = rl if ml % 2 == 0 else rl2
                    act(rl_[:, 0:n], ps[:, 0:n], AF.Relu, bias=bpp_up[:, mi:mi + 1])
                    tt("dve", hid[:, mi, 0:n].k(f"hid{mi}"), rl_[:, 0:n], rl_[:, 0:n], ALU.mult)
            for half in range(2):
                hs = slice(half * 512, (half + 1) * 512)
                for kg in range(4):
                    wd = [wload("w_down", kg * 8 + k, half * 512) for k in range(8)]
                    for c in range(nch):
                        ps = PS[4 + c]
                        for k in range(8):
                            kk = kg * 8 + k
                            mm(ps[0:cw, :], hid[:, kk, c * 128:c * 128 + cw].k(f"hid{kk}"), wd[k][:, 0:512],
                               kk == 0, kk == 31, k == 7)
                wpl = [wload("w_ple", k, half * 512) for k in range(2)]
                wpg = [wload("w_pg", k, half * 512) for k in range(8)]
                for c in range(nch):
                    ps = PS[4 + c]
                    tt("dve", tmp1k[0:cw, 0:512], ps[0:cw, :], bc_bdn[0:cw, hs], ALU.add)
                    stt("dve", STRM[c][0:cw, hs], STRM[c][0:cw, hs], ALPHA, tmp1k[0:cw, 0:512], ALU.mult, ALU.add)
                    psl = PS[0 + 2 * (c % 2)]
                    for k in range(2):
                        mm(psl[0:cw, :], pTb[:, k, c * 128:c * 128 + cw], wpl[k][:, 0:512], k == 0, k == 1, k == 1)
                    psg = PS[1 + 2 * (c % 2)]
                    for k in range(8):
                        mm(psg[0:cw, :], x1T[:, k, c * 128:c * 128 + cw], wpg[k][:, 0:512], k == 0, k == 7, k == 7)
                    tt("dve", tmp1k[0:cw, 512:1024], psg[0:cw, :], bc_bpg[0:cw, hs], ALU.add)
                    sg_ = sga if c % 2 == 0 else sgc
                    mg_ = mg2 if c % 2 == 0 else mg3
                    act(sg_[0:cw, :], tmp1k[0:cw, 512:1024], AF.Sigmoid)
                    tt("dve", mg_[0:cw, :], sg_[0:cw, :], psl[0:cw, :], ALU.mult)
                    tt("pool", STRM[c][0:cw, hs], STRM[c][0:cw, hs], mg_[0:cw, :], ALU.add)
                    ln_stat_step(STRM[c], cw, c, half, SB2)
            pending_finish = (t0, n)
            if DBG["stop"] == "blk%d" % bi:
                finish_block(*pending_finish)
                flush()
                P.emit()
                return nc
        finish_block(*pending_finish)
        flush()
        P.emit()
    return nc


_CACHE = {}


def kernel(**inputs):
    inp = {k: np.asarray(v) for k, v in inputs.items()}
    sh = _host_layout(inp)
    cores = [_core_layout(inp, b) for b in range(8)]
    key = "prog"
    if key not in _CACHE:
        _CACHE[key] = build_program({k: v.shape for k, v in sh.items()}, {k: v.shape for k, v in cores[0].items()})
    nc = _CACHE[key]
    in_maps = [{**sh, **cores[b]} for b in range(8)]
    res = run_bass_kernel_spmd(nc, in_maps, core_ids=list(range(8)))
    R = res.results
    y_p = np.stack([R[b]["o_y"][:SEQ] for b in range(8)], 0).astype(np.float32)
    y_s = np.concatenate([R[b]["o_y"][SEQ:TT] for b in range(8)], 0).reshape(128, 1, D).astype(np.float32)

    def unp(a):
        return a.reshape(2, 64, 16).transpose(2, 0, 1).reshape(32, 64)

    def uns(a):
        return a.reshape(2, 64, 16, 16).transpose(3, 2, 0, 1).reshape(16, 32, 64)

    spre = np.stack([unp(R[b]["o_spre"]) for b in range(8)], 0)[None].astype(np.float32)
    spim = np.stack([unp(R[b]["o_spim"]) for b in range(8)], 0)[None].astype(np.float32)
    ssre = np.concatenate([uns(R[b]["o_ssre"]) for b in range(8)], 0)[None].astype(np.float32)
    ssim = np.concatenate([uns(R[b]["o_ssim"]) for b in range(8)], 0)[None].astype(np.float32)
    vs = np.concatenate([R[b]["o_vs"] for b in range(8)], 0).reshape(1, 128, 1, 512).astype(np.float32)
    return (y_p, y_s, spre, spim, ssre, ssim, vs)
```

```python
import contextlib
import math
import numpy as np
import concourse.bass as bass
import concourse.mybir as mybir
from concourse.bass_utils import run_bass_kernel_spmd

F32 = mybir.dt.float32
BF16 = mybir.dt.bfloat16
ALU = mybir.AluOpType
AF = mybir.ActivationFunctionType

ENGS = ("pe", "act", "dve", "pool", "sp")
NSLOT = {"sp": 24, "act": 8, "pool": 16}
NBG = 40

D = 1024
SEQ = 2048
NS = 16
TT = SEQ + NS
L = 16
NK = SEQ // L
ALPHA = float(2.0 ** 0.25)
EPS = 1e-5
PI = math.pi


class Prog:
    def __init__(self, nc):
        self.nc = nc
        self.q = {e: [] for e in ENGS}
        self.cnt = {}
        self.last_w = {}
        self.readers = {}
        self.waited = {}
        self.dma_n = {e: 0 for e in ENGS}
        self.sems = {}
        self.n_inst = 0

    def _deps(self, eng, reads, writes, skip_self_w=False):
        deps = {}

        def add(d, same_ok=True):
            if d is None:
                return
            pid, c = d
            if pid == eng and not same_ok:
                return
            if deps.get(pid, 0) < c:
                deps[pid] = c

        for r in reads:
            add(self.last_w.get(r))
        for w in writes:
            add(self.last_w.get(w), same_ok=not skip_self_w)
            for rd in self.readers.get(w, ()):
                add(rd, same_ok=not skip_self_w)
        out = []
        for pid, c in deps.items():
            if self.waited.get((eng, pid), 0) >= c:
                continue
            if pid in ENGS and c > self.cnt.get(pid, 0):
                if pid == eng:
                    continue
                raise RuntimeError(f"dep on pending no-inc instr {pid}:{c} from {eng}")
            self.waited[(eng, pid)] = c
            out.append((pid, c))
        return out

    def _record(self, pid, c, reads, writes):
        for r in reads:
            self.readers.setdefault(r, []).append((pid, c))
        for w in writes:
            self.last_w[w] = (pid, c)
            self.readers[w] = []

    def op(self, eng, fn, reads=(), writes=(), inc=True, skip_self_w=False):
        reads = tuple(reads)
        writes = tuple(writes)
        deps = self._deps(eng, reads, writes, skip_self_w)
        c = self.cnt.get(eng, 0) + 1
        if inc:
            self.cnt[eng] = c
        self._record(eng, c, reads, writes)
        self.q[eng].append(("op", fn, deps, inc))
        self.n_inst += 1

    def dma(self, eng, out, in_, reads=(), writes=(), bg=False):
        reads = tuple(reads)
        writes = tuple(writes)
        if bg:
            n = self.dma_n.get("bg", 0)
            self.dma_n["bg"] = n + 1
            pid = ("dma", "bg", n % NBG)
        else:
            n = self.dma_n[eng]
            self.dma_n[eng] = n + 1
            pid = ("dma", eng, n % NSLOT[eng])
        deps = self._deps(eng, reads, writes)
        pc = self.cnt.get(pid, 0)
        if pc > 0 and self.waited.get((eng, pid), 0) < pc:
            self.waited[(eng, pid)] = pc
            deps.append((pid, pc))
        c = pc + 16
        self.cnt[pid] = c
        self._record(pid, c, reads, writes)
        self.q[eng].append(("dma", (out, in_), deps, pid))
        self.n_inst += 1

    def barrier(self):
        snap = dict(self.cnt)
        for e in ENGS:
            deps = []
            for pid, c in snap.items():
                if pid == e or c == 0 or (isinstance(pid, tuple) and pid[1] == "bg"):
                    continue
                if self.waited.get((e, pid), 0) < c:
                    self.waited[(e, pid)] = c
                    deps.append((pid, c))
            self.q[e].append(("wait", None, deps, None))

    def emit(self):
        nc = self.nc
        pids = list(ENGS)
        for e in ("sp", "act", "pool"):
            for s in range(NSLOT[e]):
                pids.append(("dma", e, s))
        for s in range(NBG):
            pids.append(("dma", "bg", s))
        with contextlib.ExitStack() as st:
            for pid in pids:
                name = pid if isinstance(pid, str) else f"dma_{pid[1]}_{pid[2]}"
                self.sems[pid] = st.enter_context(nc.semaphore("s_" + name))
            block = st.enter_context(nc.Block())
            final_waits = {e: [] for e in ENGS}
            for e in ("sp", "act", "pool"):
                for s in range(NSLOT[e]):
                    pid = ("dma", e, s)
                    c = self.cnt.get(pid, 0)
                    if c:
                        final_waits[e].append((pid, c))
            for s in range(NBG):
                c = self.cnt.get(("dma", "bg", s), 0)
                if c:
                    final_waits["pool"].append((("dma", "bg", s), c))

            def run(e, handle):
                for kind, payload, deps, extra in self.q[e]:
                    for pid, c in deps:
                        handle.wait_ge(self.sems[pid], c)
                    if kind == "op":
                        ins = payload(handle)
                        if extra:
                            ins.then_inc(self.sems[e], 1)
                    elif kind == "dma":
                        out, in_ = payload
                        handle.dma_start(out=out, in_=in_).then_inc(self.sems[extra], 16)
                for pid, c in final_waits[e]:
                    handle.wait_ge(self.sems[pid], c)

            @block.sync
            def _(h):
                run("sp", h)

            @block.scalar
            def _(h):
                run("act", h)

            @block.vector
            def _(h):
                run("dve", h)

            @block.gpsimd
            def _(h):
                run("pool", h)

            @block.tensor
            def _(h):
                run("pe", h)


class T:
    def __init__(self, ap, key):
        self.ap = ap
        self.key = key

    def __getitem__(self, idx):
        return T(self.ap[idx], self.key)

    def r(self, pat, **kw):
        return T(self.ap.rearrange(pat, **kw), self.key)

    def bc(self, shape):
        return T(self.ap.to_broadcast(list(shape)), self.key)

    def us(self, ax):
        return T(self.ap.unsqueeze(ax), self.key)

    def k(self, key):
        return T(self.ap, key)


def _pp(v, m):
    return np.ascontiguousarray(v.reshape(m, 128).T)


def _host_layout(inp):
    f = np.float32
    sh = {}
    sh["w_in"] = np.ascontiguousarray(inp["w_in"][0])
    sh["w_glu"] = np.ascontiguousarray(inp["w_glu"][0])
    sh["w_pa"] = np.ascontiguousarray(inp["w_branch_a"][0])
    sh["w_pb"] = np.ascontiguousarray(inp["w_branch_b"][0])
    sh["w_out"] = np.ascontiguousarray(inp["w_out"][0])
    sh["w_up"] = np.ascontiguousarray(inp["w_up"][0])
    sh["w_down"] = np.ascontiguousarray(inp["w_down"][0])
    sh["w_ple"] = np.ascontiguousarray(inp["w_ple"][0])
    sh["w_pg"] = np.ascontiguousarray(inp["w_ple_gate"][0])
    sh["b_in_pp"] = _pp(inp["b_in"][0], 28)
    sh["b_glu_pp"] = _pp(inp["b_glu"][0], 4)
    sh["b_up_pp"] = _pp(inp["b_up"][0], 32)
    rows = [inp["b_in"][0][1024:1536], inp["sgu_ln_g"][0], inp["sgu_ln_b"][0],
            inp["sgu_b"][0].reshape(-1),
            inp["b_out"][0], inp["ln1_g"][0], inp["ln1_b"][0], inp["b_down"][0],
            inp["b_ple_gate"][0], inp["ln2_g"][0], inp["ln2_b"][0],
            inp["sgu_b"][0][:, 0]]
    row = np.concatenate([np.asarray(r, f).reshape(-1) for r in rows])
    sh["bc"] = np.ascontiguousarray(np.broadcast_to(row[None, :], (128, row.size)))
    sh["wsT"] = np.ascontiguousarray(np.transpose(inp["sgu_w"][0], (2, 0, 1)))
    sh["mask"] = np.ascontiguousarray(np.triu(np.ones((128, 128), f)))
    ws00 = np.zeros((16, 4, 16), f)
    for h in range(4):
        ws00[np.arange(16), h, np.arange(16)] = inp["sgu_w"][0, h, 0, 0]
    sh["ws00"] = ws00
    sh["ident"] = np.eye(128, dtype=f)
    lr = inp["ssm_lambda_re"][0]; li = inp["ssm_lambda_im"][0]; ldt = inp["ssm_log_dt"][0]
    def o1(a):
        return np.ascontiguousarray(a.reshape(16, 2, 64).transpose(1, 2, 0).reshape(128, 16))
    sh["lr1"] = o1(lr); sh["li1"] = o1(li)
    sh["ldt1"] = o1(np.broadcast_to(ldt[:, None], (32, 64)))
    def o1b(b):
        out = np.zeros((2, 64, 16, 2, 16), f)
        bb = b.reshape(16, 2, 64, 16)
        for g2 in range(2):
            out[g2, :, :, g2, :] = bb[:, g2].transpose(1, 0, 2)
        return out.reshape(128, 16, 32)
    sh["B1re"] = o1b(inp["ssm_b_re"][0]); sh["B1im"] = o1b(inp["ssm_b_im"][0])
    sh["C1re"] = o1b(np.transpose(inp["ssm_c_re"][0], (0, 2, 1)))
    sh["C1im"] = o1b(np.transpose(inp["ssm_c_im"][0], (0, 2, 1)))
    def o2(a):
        aa = a.reshape(4, 4, 2, 64)
        out = np.broadcast_to(aa.transpose(1, 0, 2, 3)[:, None, None], (4, 2, 16, 4, 2, 64))
        return np.ascontiguousarray(out.reshape(128, 4, 128))
    sh["lrc"] = np.ascontiguousarray(np.concatenate([sh.pop("lr1"), o2(lr).reshape(128, 512)], 1))
    sh["lic"] = np.ascontiguousarray(np.concatenate([sh.pop("li1"), o2(li).reshape(128, 512)], 1))
    sh["ldtc"] = np.ascontiguousarray(np.concatenate(
        [sh.pop("ldt1"), o2(np.broadcast_to(ldt[:, None], (32, 64))).reshape(128, 512)], 1))
    def o2b(b):
        bb = b.reshape(4, 4, 2, 64, 16)
        out = np.zeros((4, 2, 16, 4, 2, 64), f)
        for g2 in range(2):
            out[:, g2, :, :, g2, :] = bb[:, :, g2].transpose(1, 3, 0, 2)
        return out.reshape(128, 4, 128)
    sh["B2re"] = o2b(inp["ssm_b_re"][0]); sh["B2im"] = o2b(inp["ssm_b_im"][0])
    dsk = np.zeros((4, 2, 16, 4, 2, 16), f)
    dd = inp["ssm_d"][0].reshape(4, 4, 2, 16)
    for g2 in range(2):
        for c in range(16):
            dsk[:, g2, c, :, g2, c] = dd[:, :, g2, c].T
    sh["dsk"] = dsk.reshape(128, 4, 32)
    return sh


def _core_layout(inp, b):
    f = np.float32
    xs = inp["x_sample"][16 * b:16 * b + 16, 0]
    xtok = np.concatenate([inp["x_prompt"][b], xs], 0)
    ptok = np.concatenate([inp["p_prompt"][0, b], inp["p_sample"][0, 16 * b:16 * b + 16, 0]], 0)
    def oh(a):
        return np.ascontiguousarray(a.reshape(16, 16, 2, 64).transpose(2, 3, 1, 0).reshape(128, 16, 16))
    return {
        "x_tok": np.ascontiguousarray(xtok, f),
        "xT": np.ascontiguousarray(xtok.T, f),
        "pT": np.ascontiguousarray(ptok.T, f),
        "h0re": oh(inp["state_ssm_re"][0, 16 * b:16 * b + 16]),
        "h0im": oh(inp["state_ssm_im"][0, 16 * b:16 * b + 16]),
    }


DBG = {"stop": None}


def build_program(shared_shapes, core_shapes):
    nc = bass.Bass("TRN2", target_bir_lowering=False)
    P = Prog(nc)
    dbg_outs = {}

    def dump(name, t, shape):
        o = nc.dram_tensor("dbg_" + name, list(shape), F32, kind="ExternalOutput").ap()
        P.dma("pool", o, t.ap, reads=[t.key])
        dbg_outs[name] = o
    dr = {}
    for k, shp in list(shared_shapes.items()) + list(core_shapes.items()):
        dr[k] = nc.dram_tensor(k, list(shp), F32, kind="ExternalInput").ap()
    o_y = nc.dram_tensor("o_y", [TT, D], F32, kind="ExternalOutput").ap()
    o_spre = nc.dram_tensor("o_spre", [128, 16], F32, kind="ExternalOutput").ap()
    o_spim = nc.dram_tensor("o_spim", [128, 16], F32, kind="ExternalOutput").ap()
    o_ssre = nc.dram_tensor("o_ssre", [128, 16, 16], F32, kind="ExternalOutput").ap()
    o_ssim = nc.dram_tensor("o_ssim", [128, 16, 16], F32, kind="ExternalOutput").ap()
    o_vs = nc.dram_tensor("o_vs", [16, 512], F32, kind="ExternalOutput").ap()

    st = contextlib.ExitStack()
    with st:
        A32W = 22900
        A16W = 60500
        a32 = st.enter_context(nc.sbuf_tensor("a32", [128, A32W], F32))
        a16 = st.enter_context(nc.sbuf_tensor("a16", [128, A16W], BF16))
        psb = [st.enter_context(nc.psum_tensor(f"ps{i}", [128, 512], F32)) for i in range(8)]
        PS = [T(psb[i][:, :], f"ps{i}") for i in range(8)]

        class Arena:
            def __init__(s, t, size, nm):
                s.t, s.size, s.nm, s.off, s.n = t, size, nm, 0, 0

            def alloc(s, shape, key=None):
                n = int(np.prod(shape[1:]))
                assert s.off + n <= s.size, (s.nm, s.off, n, s.size)
                ap = s.t[0:shape[0], s.off:s.off + n]
                s.off += n
                s.n += 1
                key = key or f"{s.nm}_{s.n}"
                if len(shape) == 3:
                    ap = ap.rearrange("p (a b) -> p a b", a=shape[1])
                elif len(shape) == 4:
                    ap = ap.rearrange("p (a b c) -> p a b c", a=shape[1], b=shape[2])
                return T(ap, key)

        R32 = Arena(a32, A32W, "f")
        R16 = Arena(a16, A16W, "h")

        yT = R16.alloc([128, 4, TT], "yT")
        bpp_in = R32.alloc([128, 28], "bpp_in")
        bpp_glu = R32.alloc([128, 4], "bpp_glu")
        bpp_up = R32.alloc([128, 32], "bpp_up")
        P.dma("sp", bpp_in.ap, dr["b_in_pp"], writes=[bpp_in.key])
        P.dma("sp", bpp_glu.ap, dr["b_glu_pp"], writes=[bpp_glu.key])
        P.dma("sp", bpp_up.ap, dr["b_up_pp"], writes=[bpp_up.key])
        mark32, mark16 = R32.off, R16.off

        def tt(eng, out, a, b, op):
            P.op(eng, lambda e: e.tensor_tensor(out=out.ap, in0=a.ap, in1=b.ap, op=op),
                 reads=[a.key, b.key], writes=[out.key])

        def ts(eng, out, a, s1, s2, op0, op1=None):
            rd = [a.key]
            s1a = s1.ap if isinstance(s1, T) else s1
            s2a = s2.ap if isinstance(s2, T) else s2
            if isinstance(s1, T):
                rd.append(s1.key)
            if isinstance(s2, T):
                rd.append(s2.key)
            if op1 is None:
                P.op(eng, lambda e: e.tensor_scalar(out=out.ap, in0=a.ap, scalar1=s1a, scalar2=None, op0=op0),
                     reads=rd, writes=[out.key])
            else:
                P.op(eng, lambda e: e.tensor_scalar(out=out.ap, in0=a.ap, scalar1=s1a, scalar2=s2a, op0=op0, op1=op1),
                     reads=rd, writes=[out.key])

        def stt(eng, out, a, s, b, op0, op1):
            eng = "dve"
            rd = [a.key, b.key]
            sa = s.ap if isinstance(s, T) else s
            if isinstance(s, T):
                rd.append(s.key)
            P.op(eng, lambda e: e.scalar_tensor_tensor(out=out.ap, in0=a.ap, scalar=sa, in1=b.ap, op0=op0, op1=op1),
                 reads=rd, writes=[out.key])

        def cp(eng, out, a):
            P.op(eng, lambda e: e.tensor_copy(out=out.ap, in_=a.ap), reads=[a.key], writes=[out.key])

        def act(out, a, func, bias=None, scale=None):
            rd = [a.key]
            kw = {}
            if bias is not None:
                kw["bias"] = bias.ap if isinstance(bias, T) else bias
                if isinstance(bias, T):
                    rd.append(bias.key)
            if scale is not None:
                kw["scale"] = scale.ap if isinstance(scale, T) else scale
                if isinstance(scale, T):
                    rd.append(scale.key)
            P.op("act", lambda e: e.activation(out=out.ap, in_=a.ap, func=func, **kw), reads=rd, writes=[out.key])

        def memset(eng, out, v):
            P.op(eng, lambda e: e.memset(out.ap, v), writes=[out.key])

        def mm(out, lhsT, rhs, start, stop, inc, tp=None):
            kw = {}
            if tp is not None:
                kw["tile_position"] = tp
            P.op("pe", lambda e: e.matmul(out.ap, lhsT=lhsT.ap, rhs=rhs.ap, start=start, stop=stop, **kw),
                 reads=[lhsT.key, rhs.key], writes=[out.key], inc=inc, skip_self_w=True)

        def cmul(eng, ore, oim, are, aim, bre, bim, t1, t2):
            tt(eng, t1, are, bre, ALU.mult)
            tt(eng, t2, aim, bim, ALU.mult)
            tt(eng, ore, t1, t2, ALU.subtract)
            tt(eng, t1, are, bim, ALU.mult)
            tt(eng, t2, aim, bre, ALU.mult)
            tt(eng, oim, t1, t2, ALU.add)

        def exp_taylor(eng, out, x, deg):
            ts(eng, out, x, 1.0 / math.factorial(deg), None, ALU.mult)
            for kk in range(deg - 1, 0, -1):
                stt(eng, out, out, 1.0 / math.factorial(kk), x, ALU.add, ALU.mult)
            ts(eng, out, out, 1.0, None, ALU.add)

        def disc(eng, shape, lr, li, ldt, pref):
            al = lambda nm: R32.alloc(shape, pref + nm)
            s0 = al("s0"); s1 = al("s1"); s2 = al("s2"); s3 = al("s3"); s4 = al("s4"); s5 = al("s5")
            s6 = al("s6"); s7 = al("s7"); s8 = al("s8"); t1 = al("t1"); t2 = al("t2")
            dt, a, th, mag, imag, n, thr, sn, cs = s0, s1, s2, s3, s4, s5, s6, s7, s8
            c0 = 4.605170185988092
            ts(eng, t1, ldt, c0, 0.25, ALU.add, ALU.mult)
            exp_taylor(eng, dt, t1, 10)
            tt(eng, dt, dt, dt, ALU.mult)
            tt(eng, dt, dt, dt, ALU.mult)
            ts(eng, dt, dt, math.exp(-c0), None, ALU.mult)
            tt(eng, a, lr, dt, ALU.mult)
            tt(eng, th, li, dt, ALU.mult)
            exp_taylor(eng, mag, a, 6)
            ts(eng, t1, a, -1.0, None, ALU.mult)
            exp_taylor(eng, imag, t1, 6)
            ts(eng, n, th, PI, None, ALU.is_ge)
            for m_ in (3, 5, 7):
                stt(eng, n, th, m_ * PI, n, ALU.is_ge, ALU.add)
            stt(eng, thr, n, -2.0 * PI, th, ALU.mult, ALU.add)
            tt(eng, t2, thr, thr, ALU.mult)
            sc = [(-1.0) ** k / math.factorial(2 * k + 1) for k in range(10)]
            cc = [(-1.0) ** k / math.factorial(2 * k) for k in range(11)]
            ts(eng, sn, t2, sc[9], None, ALU.mult)
            for k in range(8, 0, -1):
                stt(eng, sn, sn, sc[k], t2, ALU.add, ALU.mult)
            stt(eng, sn, sn, 1.0, thr, ALU.add, ALU.mult)
            ts(eng, cs, t2, cc[10], None, ALU.mult)
            for k in range(9, 0, -1):
                stt(eng, cs, cs, cc[k], t2, ALU.add, ALU.mult)
            ts(eng, cs, cs, 1.0, None, ALU.add)
            are, aim, ire, iim = s0, s1, s2, s4
            tt(eng, are, mag, cs, ALU.mult)
            tt(eng, aim, mag, sn, ALU.mult)
            tt(eng, ire, imag, cs, ALU.mult)
            tt(eng, iim, imag, sn, ALU.mult)
            ts(eng, iim, iim, -1.0, None, ALU.mult)
            den, nr = s5, s6
            tt(eng, t1, lr, lr, ALU.mult)
            tt(eng, t2, li, li, ALU.mult)
            tt(eng, den, t1, t2, ALU.add)
            P.op("dve", lambda e: e.reciprocal(out=den.ap, in_=den.ap), reads=[den.key], writes=[den.key])
            ts(eng, nr, are, -1.0, None, ALU.add)
            cre, cim = s7, s8
            tt(eng, t1, nr, lr, ALU.mult)
            tt(eng, t2, aim, li, ALU.mult)
            tt(eng, t1, t1, t2, ALU.add)
            tt(eng, cre, t1, den, ALU.mult)
            tt(eng, t1, aim, lr, ALU.mult)
            tt(eng, t2, nr, li, ALU.mult)
            tt(eng, t1, t1, t2, ALU.subtract)
            tt(eng, cim, t1, den, ALU.mult)
            return dict(are=are, aim=aim, ire=ire, iim=iim, cre=cre, cim=cim, t1=t1, t2=t2, mag=mag)

        mark16_main = R16.off
        uT = R16.alloc([128, 4, TT], "uT")
        mark32, mark16 = R32.off, R16.off
        blocks = [(i * 512, 512) for i in range(4)] + [(SEQ, NS)]
        Bb = R16.alloc([128, 2, 16, 32], "Bb")
        CA = R16.alloc([128, 2, 16 * 17 * 32], "CA")
        mark16_lt = R16.off
        Wu = R16.alloc([128, 8, 512], "Wu")
        P.dma("pool", Wu.ap, dr["w_in"][:, 0:512].rearrange("(k p) n -> p k n", p=128), writes=[Wu.key])
        xb_s5 = [R16.alloc([128, 8, 512], f"xb_s5_{i}") for i in range(5)]
        for bi, (t0, n) in enumerate(blocks):
            P.dma("pool", xb_s5[bi].ap[:, :, 0:n], dr["xT"][:, t0:t0 + n].rearrange("(k p) n -> p k n", p=128),
                  writes=[xb_s5[bi].key])
        wsc = {}
        cast_list = [("w_in", 1024, 3584), ("w_glu", 512, 512), ("w_pa", 512, 1024), ("w_pb", 512, 1024),
                     ("w_out", 1024, 1024), ("w_up", 1024, 4096), ("w_down", 4096, 1024), ("w_ple", 256, 1024),
                     ("w_pg", 1024, 1024), ("xT", 1024, TT), ("pT", 256, TT)]
        for nm, r_, c_ in cast_list:
            wsc[nm] = nc.dram_tensor("sc_" + nm, [r_, c_], BF16, kind="Internal").ap()
            step = 256 if c_ <= 2064 else 128
            for k in range(0, r_, step):
                P.dma("pool", wsc[nm][k:k + step, :], dr[nm][k:k + step, :], writes=[f"sc_{nm}_{k // 128}"] +
                      ([f"sc_{nm}_{k // 128 + 1}"] if step == 256 else []), bg=True)
        def ld(name, shape, eng="sp"):
            t = R32.alloc(shape, name)
            P.dma(eng, t.ap, dr[name], writes=[t.key])
            return t

        dsk = ld("dsk", [128, 4, 32], "act")
        P1re = R32.alloc([128, 17, 16], "P1re"); P1im = R32.alloc([128, 17, 16], "P1im")
        I1re = R32.alloc([128, 16, 16], "I1re"); I1im = R32.alloc([128, 16, 16], "I1im")
        rho = R32.alloc([128, 16], "rho")
        mark32b = R32.off
        lrc = ld("lrc", [128, 528]); lic = ld("lic", [128, 528]); ldtc = ld("ldtc", [128, 528])
        B1re = ld("B1re", [128, 16, 32]); B1im = ld("B1im", [128, 16, 32])
        C1re = ld("C1re", [128, 16, 32]); C1im = ld("C1im", [128, 16, 32])
        B2re = ld("B2re", [128, 4, 128], "act"); B2im = ld("B2im", [128, 4, 128], "act")
        dd = disc("dve", [128, 528], lrc, lic, ldtc, "dd")
        d1 = {k: v[:, 0:16] for k, v in dd.items()}
        d2 = {k: v[:, 16:528].r("p (a b) -> p a b", a=4) for k, v in dd.items()}
        pt1 = R32.alloc([128, 8, 16], "pt1"); pt2 = R32.alloc([128, 8, 16], "pt2")
        tt("dve", rho, d1["mag"], d1["mag"], ALU.mult)
        for _ in range(3):
            tt("dve", rho, rho, rho, ALU.mult)
        memset("dve", P1re[:, 0, :], 1.0)
        memset("dve", P1im[:, 0, :], 0.0)
        cp("dve", P1re[:, 1, :], d1["are"])
        cp("dve", P1im[:, 1, :], d1["aim"])
        n_ = 1
        while n_ < 16:
            sl_o = slice(n_ + 1, 2 * n_ + 1)
            sl_i = slice(1, n_ + 1)
            bre = P1re[:, n_:n_ + 1, :].bc([128, n_, 16]); bim = P1im[:, n_:n_ + 1, :].bc([128, n_, 16])
            cmul("dve", P1re[:, sl_o, :], P1im[:, sl_o, :],
                 P1re[:, sl_i, :], P1im[:, sl_i, :], bre, bim, pt1[:, 0:n_, :], pt2[:, 0:n_, :])
            n_ *= 2
        w1 = dd["t1"][:, 16:528].r("p (a b) -> p a b", a=16); w2 = dd["t2"][:, 16:528].r("p (a b) -> p a b", a=16)
        cob_re = d1["cre"].us(2).bc([128, 16, 32]); cob_im = d1["cim"].us(2).bc([128, 16, 32])
        cmul("dve", Bb[:, 0], Bb[:, 1], B1re, B1im, cob_re, cob_im, w1, w2)
        CAv = lambda ri: CA[:, ri, :].r("p (q m c) -> p q m c", q=16, m=17)
        y1_ = R32.alloc([128, 17, 32], "y1_"); y2_ = R32.alloc([128, 17, 32], "y2_")
        z1_ = R32.alloc([128, 17, 32], "z1_"); z2_ = R32.alloc([128, 17, 32], "z2_")
        for q in range(16):
            cr = C1re[:, q:q + 1, :].bc([128, 17, 32]); ci = C1im[:, q:q + 1, :].bc([128, 17, 32])
            pr = P1re[:, :, q:q + 1].bc([128, 17, 32]); pi_ = P1im[:, :, q:q + 1].bc([128, 17, 32])
            eng, a1, a2 = ("dve", z1_, z2_) if q % 2 == 0 else ("dve", y1_, y2_)
            tt(eng, a1, cr, pr, ALU.mult)
            tt(eng, a2, ci, pi_, ALU.mult)
            tt(eng, CAv(0)[:, q].k(f"CA0_{q}"), a1, a2, ALU.subtract)
            tt(eng, a1, cr, pi_, ALU.mult)
            tt(eng, a2, ci, pr, ALU.mult)
            stt(eng, CAv(1)[:, q].k(f"CA1_{q}"), a1, -1.0, a2, ALU.mult, ALU.subtract)
        Are = P1re[:, 16, :]; Aim = P1im[:, 16, :]
        cp("dve", I1re[:, 0, :], d1["ire"]); cp("dve", I1im[:, 0, :], d1["iim"])
        n_ = 1
        while n_ < 16:
            sl_o = slice(n_, 2 * n_); sl_i = slice(0, n_)
            bre = I1re[:, n_ - 1:n_, :].bc([128, n_, 16]); bim = I1im[:, n_ - 1:n_, :].bc([128, n_, 16])
            cmul("dve", I1re[:, sl_o, :], I1im[:, sl_o, :],
                 I1re[:, sl_i, :], I1im[:, sl_i, :], bre, bim, pt1[:, 0:n_, :], pt2[:, 0:n_, :])
            n_ *= 2
        ai15re = I1re[:, 14, :]; ai15im = I1im[:, 14, :]

        for bi, (t0, n) in enumerate(blocks):
            xb = xb_s5[bi]
            for m in range(4):
                ps = PS[m % 2]
                for k in range(8):
                    mm(ps[:, 0:n], Wu[:, k, m * 128:(m + 1) * 128], xb[:, k, 0:n], k == 0, k == 7, k == 7)
                act(uT[:, m, t0:t0 + n], ps[:, 0:n], AF.Identity, bias=bpp_in[:, m:m + 1])
        if DBG["stop"] == "p0":
            dump("uT", uT, [128, 4, TT])
            P.emit()
            return nc
        P.barrier()
        R16.off = mark16_lt

        LT = R16.alloc([128, 4, 2 * 16 * 128], "LT")
        LTv = lambda qt, ri: LT[:, qt, :].r("p (r j n) -> p r j n", r=2, j=16)[:, ri].k(f"LT{qt}")
        Bb2re = R32.alloc([128, 4, 128], "Bb2re"); Bb2im = R32.alloc([128, 4, 128], "Bb2im")
        cmul("dve", Bb2re, Bb2im, B2re, B2im, d2["cre"], d2["cim"], d2["t1"], d2["t2"])
        IPre = R32.alloc([128, 4, 128], "IPre"); IPim = R32.alloc([128, 4, 128], "IPim")
        iq1 = dd["t1"][:, 16:144]; iq2 = dd["t2"][:, 16:144]
        L32re = R32.alloc([128, 16, 128], "L32re"); L32im = R32.alloc([128, 16, 128], "L32im")
        q1 = R32.alloc([128, 8, 128], "q1"); q2 = R32.alloc([128, 8, 128], "q2")
        for qt in range(4):
            cs_ = slice(0, 128)
            cp("dve", IPre[:, 0, :], d2["ire"][:, qt, :]); cp("dve", IPim[:, 0, :], d2["iim"][:, qt, :])
            for lv in range(3):
                cmul("dve", IPre[:, lv + 1, :], IPim[:, lv + 1, :], IPre[:, lv, :], IPim[:, lv, :],
                     IPre[:, lv, :], IPim[:, lv, :], iq1, iq2)
            cmul("dve", L32re[:, 0, :], L32im[:, 0, :], Bb2re[:, qt, :], Bb2im[:, qt, :],
                 IPre[:, 0, cs_], IPim[:, 0, cs_], q1[:, 0, :], q2[:, 0, :])
            n_ = 1
            lv = 0
            while n_ < 16:
                sl_o = slice(n_, 2 * n_); sl_i = slice(0, n_)
                bre = IPre[:, lv, cs_].us(1).bc([128, n_, 128]); bim = IPim[:, lv, cs_].us(1).bc([128, n_, 128])
                cmul("dve", L32re[:, sl_o, :], L32im[:, sl_o, :],
                     L32re[:, sl_i, :], L32im[:, sl_i, :], bre, bim, q1[:, 0:n_, :], q2[:, 0:n_, :])
                n_ *= 2
                lv += 1
            act(LTv(qt, 0), L32re, AF.Copy)
            act(LTv(qt, 1), L32im, AF.Copy)

        Kt = R16.alloc([128, 4, 16, 32], "Kt")
        kt32 = R32.alloc([128, 16, 32], "kt32")
        for qt in range(4):
            ps = PS[2 + qt % 2]
            for q4 in range(4):
                q = qt * 4 + q4
                for ri in range(2):
                    mm(ps[32 * q4:32 * q4 + 32, :], Bb[:, ri, q, :],
                       CA[:, ri, q * 544:q * 544 + 512].k(f"CA{ri}_{q}"), ri == 0, ri == 1, (ri == 1 and q4 == 3), tp=(0, 32 * q4))
            cp("dve", kt32, ps.r("p (m c) -> p m c", m=16))
            tt("dve", kt32[:, 0, :], kt32[:, 0, :], dsk[:, qt, :], ALU.add)
            cp("dve", Kt[:, qt], kt32)
        Hp = R16.alloc([128, 2, 16, NK + NS], "Hp")
        if DBG["stop"] == "p1a":
            dump("P1re", P1re, [128, 17, 16]); dump("P1im", P1im, [128, 17, 16])
            dump("I1re", I1re, [128, 16, 16]); dump("I1im", I1im, [128, 16, 16])
            dump("Bb", Bb, [128, 2, 16, 32]); dump("CA", CA, [128, 2, 16 * 17 * 32])
            dump("LT", LT, [128, 4, 2 * 16 * 128]); dump("Kt", Kt, [128, 4, 16, 32])
            P.emit()
            return nc
        P.barrier()
        R32.off = mark32b

        h0re = ld("h0re", [128, 16, 16], "act"); h0im = ld("h0im", [128, 16, 16], "act")
        Gp = R32.alloc([128, 2, 16, 128], "Gp")
        Gs = R32.alloc([128, 2, 16, 16], "Gs")
        for ri in range(2):
            for qt in range(4):
                for j in range(16):
                    for q4 in range(4):
                        ps = PS[4 + q4]
                        mm(ps[:, qt * 128:(qt + 1) * 128], LTv(qt, ri)[32 * q4:32 * q4 + 32, j, :],
                           uT[32 * q4:32 * q4 + 32, qt, j:SEQ:16], j == 0, j == 15,
                           (j == 15 and qt == 3), tp=(32 * q4, 0))
            for q4 in range(4):
                cp("dve", Gp[:, ri, q4:16:4, :], PS[4 + q4].r("p (a k) -> p a k", a=4))
        for q4 in range(4):
            ps = PS[4 + q4]
            for ri in range(2):
                for qt in range(4):
                    cb = (ri * 4 + qt) * 16
                    mm(ps[:, cb:cb + 16], LTv(qt, ri)[32 * q4:32 * q4 + 32, 15, :],
                       uT[32 * q4:32 * q4 + 32, qt, SEQ:TT], True, True, (ri == 1 and qt == 3), tp=(32 * q4, 0))
        for q4 in range(4):
            for ri in range(2):
                cp("dve", Gs[:, ri, q4:16:4, :], PS[4 + q4][:, ri * 64:ri * 64 + 64].r("p (a t) -> p a t", a=4))
        if DBG["stop"] == "p1b1":
            dump("Gp", Gp, [128, 2, 16, 128]); dump("Gs", Gs, [128, 2, 16, 16])
            P.emit()
            return nc
        KA = R32.alloc([128, 2, 16, 128], "KA")
        k1 = R32.alloc([128, 16, 128], "k1"); k2 = R32.alloc([128, 16, 128], "k2")
        Wre = R32.alloc([128, 16, 128], "Wre"); Wim = R32.alloc([128, 16, 128], "Wim")
        RHO = R32.alloc([128, 16, 128], "RHO")
        ure = R32.alloc([128, 16], "ure"); uim = R32.alloc([128, 16], "uim")
        sp1 = R32.alloc([128, 16], "sp1"); sp2 = R32.alloc([128, 16], "sp2")
        P.op("dve", lambda e: e.reciprocal(out=sp1.ap, in_=rho.ap), reads=[rho.key], writes=[sp1.key])
        tt("dve", ure, Are, sp1, ALU.mult)
        tt("dve", uim, Aim, sp1, ALU.mult)
        tt("dve", sp1, ure, ure, ALU.mult)
        tt("dve", sp2, uim, uim, ALU.mult)
        tt("dve", sp1, sp1, sp2, ALU.add)
        ts("dve", sp1, sp1, -0.5, 1.5, ALU.mult, ALU.add)
        tt("dve", ure, ure, sp1, ALU.mult)
        tt("dve", uim, uim, sp1, ALU.mult)
        cp("dve", Wre[:, :, 0], ure); cp("dve", Wim[:, :, 0], uim)
        n_ = 1
        while n_ < 128:
            sl_o = slice(n_, 2 * n_); sl_i = slice(0, n_)
            bre = Wre[:, :, n_ - 1:n_].bc([128, 16, n_]); bim = Wim[:, :, n_ - 1:n_].bc([128, 16, n_])
            cmul("dve", Wre[:, :, sl_o], Wim[:, :, sl_o], Wre[:, :, sl_i], Wim[:, :, sl_i], bre, bim,
                 k1[:, :, 0:n_], k2[:, :, 0:n_])
            n_ *= 2
        cp("pool", RHO, rho.us(2).bc([128, 16, 128]))
        memset("pool", RHO[:, :, 0:1], 0.0)
        Ab_re = Are.us(2).bc([128, 16, 128]); Ab_im = Aim.us(2).bc([128, 16, 128])
        cmul("dve", KA[:, 0], KA[:, 1], Gp[:, 0], Gp[:, 1], Ab_re, Ab_im, k1, k2)
        tt("dve", k1, Wre, KA[:, 0], ALU.mult)
        tt("dve", k2, Wim, KA[:, 1], ALU.mult)
        tt("dve", Gp[:, 0], k1, k2, ALU.add)
        tt("dve", k1, Wre, KA[:, 1], ALU.mult)
        tt("dve", k2, Wim, KA[:, 0], ALU.mult)
        tt("dve", Gp[:, 1], k1, k2, ALU.subtract)
        for ri in range(2):
            P.op("dve", lambda e, ri=ri: e.tensor_tensor_scan(
                out=KA.ap[:, ri].rearrange("p q k -> p (q k)"), data0=RHO.ap.rearrange("p q k -> p (q k)"),
                data1=Gp.ap[:, ri].rearrange("p q k -> p (q k)"), initial=0.0, op0=ALU.mult, op1=ALU.add),
                reads=[RHO.key, Gp.key], writes=[KA.key])
        cmul("dve", Gp[:, 0], Gp[:, 1], KA[:, 0], KA[:, 1], Wre, Wim, k1, k2)
        Hre, Him = Gp[:, 0], Gp[:, 1]
        KAre, KAim = Hre, Him
        fin = R32.alloc([128, 2, 16], "fin")
        cp("dve", fin[:, 0, :], Hre[:, :, 127]); cp("dve", fin[:, 1, :], Him[:, :, 127])
        P.dma("sp", o_spre, fin.ap[:, 0, :], reads=[fin.key])
        P.dma("sp", o_spim, fin.ap[:, 1, :], reads=[fin.key])
        memset("dve", Hp[:, :, :, 0:1], 0.0)
        cp("dve", Hp[:, 0, :, 1:128], Hre[:, :, 0:127])
        cp("pool", Hp[:, 1, :, 1:128], Him[:, :, 0:127])
        s1 = R32.alloc([128, 16, 16], "s1"); s2 = R32.alloc([128, 16, 16], "s2")
        hsre = R32.alloc([128, 16, 16], "hsre"); hsim = R32.alloc([128, 16, 16], "hsim")
        cmul("dve", hsre, hsim, h0re, h0im, ai15re.us(2).bc([128, 16, 16]), ai15im.us(2).bc([128, 16, 16]), s1, s2)
        cp("dve", Hp[:, 0, :, 128:144], hsre)
        cp("dve", Hp[:, 1, :, 128:144], hsim)
        s3 = R32.alloc([128, 16, 16], "s3"); s4 = R32.alloc([128, 16, 16], "s4")
        tt("dve", s3, hsre, Gs[:, 0], ALU.add)
        tt("dve", s4, hsim, Gs[:, 1], ALU.add)
        nsre = R32.alloc([128, 16, 16], "nsre"); nsim = R32.alloc([128, 16, 16], "nsim")
        cmul("dve", nsre, nsim, s3, s4, Are.us(2).bc([128, 16, 16]), Aim.us(2).bc([128, 16, 16]), s1, s2)
        P.dma("sp", o_ssre, nsre.ap, reads=[nsre.key])
        P.dma("sp", o_ssim, nsim.ap, reads=[nsim.key])

        if DBG["stop"] == "p1b2":
            dump("KAre", KAre, [128, 16, 128]); dump("KAim", KAim, [128, 16, 128]); dump("Hp", Hp, [128, 2, 16, NK + NS])
            P.emit()
            return nc
        ystg = [R16.alloc([128, 512], "ystg0"), R16.alloc([128, 512], "ystg1")]
        for qt in range(4):
            for jg in range(4):
                YB = [PS[q4_ + 4 * ((qt * 4 + jg) % 2)] for q4_ in range(4)]
                for jl in range(4):
                    jt = jg * 4 + jl
                    cs_ = slice(jl * 128, jl * 128 + 128)
                    for ri in range(2):
                        for q4 in range(4):
                            q = qt * 4 + q4
                            rows = slice(32 * q4, 32 * q4 + 32)
                            mm(YB[q4][rows, cs_], CAv(ri)[:, q, jt + 1, :].k(f"CA{ri}_{q}"), Hp[:, ri, q, 0:128],
                               ri == 0, False, False, tp=(0, 32 * q4))
                    for js in range(jt + 1):
                        last = js == jt
                        for q4 in range(4):
                            rows = slice(32 * q4, 32 * q4 + 32)
                            mm(YB[q4][rows, cs_], Kt[rows, qt, jt - js, :], uT[rows, qt, js:SEQ:16],
                               False, last, (last and jl == 3), tp=(32 * q4, 32 * q4))
                if DBG["stop"] == "yA" and qt == 0 and jg == 0:
                    P.emit()
                    return nc
                stg = ystg[(qt * 4 + jg) % 2]
                for q4 in range(4):
                    rows = slice(32 * q4, 32 * q4 + 32)
                    act(stg[rows, :].k(f"{stg.key}_{q4}"), YB[q4][rows, :], AF.Gelu_apprx_tanh)
                P.op("dve", lambda e, stg=stg, qt=qt, jg=jg: e.tensor_copy(
                    out=yT.ap[:, qt, 0:SEQ].rearrange("p (k j) -> p j k", j=16)[:, jg * 4:(jg + 1) * 4, :],
                    in_=stg.ap.rearrange("p (j k) -> p j k", j=4)),
                    reads=[f"{stg.key}_{q4_}" for q4_ in range(4)], writes=[yT.key])
                if DBG["stop"] == "yB" and qt == 0 and jg == 0:
                    P.emit()
                    return nc
                if DBG["stop"] == "yC" and qt == 0 and jg == 1:
                    P.emit()
                    return nc
                if DBG["stop"] == "yD" and qt == 0 and jg == 3:
                    P.emit()
                    return nc
            ps = PS[4 + qt % 2]
            for q4 in range(4):
                q = qt * 4 + q4
                rows = slice(32 * q4, 32 * q4 + 32)
                for ri in range(2):
                    mm(ps[rows, 0:16], CAv(ri)[:, q, 16, :].k(f"CA{ri}_{q}"), Hp[:, ri, q, 128:144], ri == 0, False, False,
                       tp=(0, 32 * q4))
                mm(ps[rows, 0:16], Kt[rows, qt, 0, :], uT[rows, qt, SEQ:TT], False, True, q4 == 3,
                   tp=(32 * q4, 32 * q4))
            act(yT[:, qt, SEQ:TT], ps[:, 0:16], AF.Gelu_apprx_tanh)
            if DBG["stop"] == "yE" and qt == 0:
                P.emit()
                return nc
            if DBG["stop"] == "yF" and qt == 1:
                P.emit()
                return nc

        if DBG["stop"] == "p1":
            dump("yT", yT, [128, 4, TT]); dump("Hp", Hp, [128, 2, 16, NK + NS]); dump("Gp", Gp, [128, 2, 16, 128])
            P.emit()
            return nc
        P.barrier()
        R32.off, R16.off = mark32, mark16_main
        BCW = 512 * 3 + 512 + 7 * 1024 + 4
        bc = R32.alloc([128, BCW], "bc")
        P.dma("sp", bc.ap, dr["bc"], writes=[bc.key])
        o_ = [0]
        def bcs(n):
            v = bc[:, o_[0]:o_[0] + n]
            o_[0] += n
            return v
        bc_binv = bcs(512); bc_sg = bcs(512); bc_sb = bcs(512); bc_bs = bcs(512)
        bc_bo = bcs(1024); bc_g1 = bcs(1024); bc_b1 = bcs(1024); bc_bdn = bcs(1024)
        bc_bpg = bcs(1024); bc_g2 = bcs(1024); bc_b2 = bcs(1024); bc_bs0 = bcs(4)
        wsT32 = R32.alloc([128, 4, 128], "wsT32"); mask = R32.alloc([128, 128], "mask")
        P.dma("act", wsT32.ap, dr["wsT"], writes=[wsT32.key])
        P.dma("act", mask.ap, dr["mask"], writes=[mask.key])
        wsTb = R16.alloc([128, 4, 128], "wsTb")
        tt("dve", wsTb, wsT32, mask.us(1).bc([128, 4, 128]), ALU.mult)
        ws00b = R16.alloc([16, 4, 16], "ws00b")
        P.dma("pool", ws00b.ap, dr["ws00"], writes=[ws00b.key])
        identb = R16.alloc([128, 128], "identb")
        P.dma("pool", identb.ap, dr["ident"], writes=[identb.key])
        PSB4 = [T(psb[6][:, :].bitcast(BF16)[:, 0:512].rearrange("p (j t) -> p j t", j=4), "ps6"),
                T(psb[7][:, :].bitcast(BF16)[:, 0:512].rearrange("p (j t) -> p j t", j=4), "ps7")]

        NRING = 28
        ring = [R16.alloc([128, 512], f"ring{i}") for i in range(NRING)]
        rn = [0]
        def wload(nm, k, c0_):
            t = ring[rn[0] % NRING]
            rn[0] += 1
            P.dma("sp", t.ap[:, 0:512], wsc[nm][k * 128:(k + 1) * 128, c0_:c0_ + 512],
                  reads=[f"sc_{nm}_{k}"], writes=[t.key])
            return t
        xTb = R16.alloc([128, 8, 512], "xTb")
        pTb = R16.alloc([128, 2, 512], "pTb")
        strm = R32.alloc([128, 4, 1024], "strm")
        ub = R32.alloc([128, 4, 512], "ub")
        vg4 = R32.alloc([128, 4, 512], "vg4")
        vy_raw = R16.alloc([128, 4096], "vy")
        vhb = vy_raw[:, 0:2048].r("p (a b) -> p a b", a=4).k("vhb")
        yb = vy_raw[:, 2048:4096].r("p (a b) -> p a b", a=4).k("yb")
        x1b4 = vy_raw.r("p (a b) -> p a b", a=4).k("x1b4")
        ya = R16.alloc([128, 4, 512], "ya")
        sga = R32.alloc([128, 512], "sga"); sgb = R32.alloc([128, 512], "sgb")
        mg2 = R32.alloc([128, 512], "mg2"); mg3 = R32.alloc([128, 512], "mg3")
        sgc = R32.alloc([128, 512], "sgc")
        rl2 = R32.alloc([128, 512], "rl2")
        merged = R16.alloc([128, 8, 512], "merged")
        x1T = R16.alloc([128, 8, 512], "x1T")
        hid_raw = R16.alloc([128, 32 * 512], "hid")
        hid = hid_raw.r("p (a b) -> p a b", a=32)
        mgA = T(hid_raw.ap.bitcast(F32), "hid").r("p (a b) -> p a b", a=8)
        rl = R32.alloc([128, 512], "rl")
        stats = R32.alloc([128, 4, 2, 6], "stats"); mv = R32.alloc([128, 4, 2], "mv")
        rstd = R32.alloc([128, 4], "rstd"); nmr = R32.alloc([128, 4], "nmr")
        SB1 = (stats, mv, rstd, nmr, R32.alloc([128, 4], "vv1"), R32.alloc([128, 4], "tq1"))
        SB2 = (R32.alloc([128, 4, 2, 6], "stats2"), R32.alloc([128, 4, 2], "mv2"),
               R32.alloc([128, 4], "rstd2"), R32.alloc([128, 4], "nmr2"),
               R32.alloc([128, 4], "vv2"), R32.alloc([128, 4], "tq2"))
        ones32 = R32.alloc([1, 128], "ones32")
        memset("dve", ones32, 1.0)

        def bias_mm(ps_ap, cw_, brow):
            mm(ps_ap, ones32[0:1, 0:cw_], brow, True, False, False)

        def ln_stat_step(xc, cw, c, h, sb=None):
            stats = (sb or SB1)[0]
            P.op("dve", lambda e: e.bn_stats(out=stats.ap[0:cw, c, h, :], in_=xc.ap[0:cw, h * 512:(h + 1) * 512]),
                 reads=[xc.key], writes=[stats.key])

        def layer_norm_steps(chunks, cw, width, g, b, out_bfs=None, sb=None, with_stats=True, offload=False):
            stats, mv, rstd, nmr, vv, tq = sb or SB1
            nch = len(chunks)
            nchk = width // 512
            steps = []
            if with_stats:
                for c in range(nch):
                    for h in range(nchk):
                        steps.append(lambda c=c, h=h: ln_stat_step(chunks[c], cw, c, h, sb))
            for c in range(nch):
                steps.append(lambda c=c: P.op(
                    "dve", lambda e: e.bn_aggr(out=mv.ap[0:cw, c, :], in_=stats.ap[0:cw, c, 0:nchk, :]),
                    reads=[stats.key], writes=[mv.key]))

            def rs():
                y_ = rstd[0:cw, 0:nch]
                ts("dve", y_, mv[0:cw, 0:nch, 1], EPS, None, ALU.add)
                act(y_, y_, AF.Sqrt)
                P.op("dve", lambda e: e.reciprocal(out=y_.ap, in_=y_.ap), reads=[y_.key], writes=[y_.key])
                stt("dve", nmr[0:cw, 0:nch], mv[0:cw, 0:nch, 0], -1.0, y_, ALU.mult, ALU.mult)
            steps.append(rs)
            for c in range(nch):
                def aff(c=c):
                    xc = chunks[c][0:cw, 0:width]
                    if offload:
                        act(xc, xc, AF.Identity, bias=nmr[0:cw, c:c + 1], scale=rstd[0:cw, c:c + 1])
                        tt("dve", xc, xc, g[0:cw, 0:width], ALU.mult)
                        tt("pool" if offload == "pool" else "dve", xc, xc, b[0:cw, 0:width], ALU.add)
                    else:
                        stt("dve", xc, xc, mv[0:cw, c, 0:1], g[0:cw, 0:width], ALU.subtract, ALU.mult)
                        stt("dve", xc, xc, rstd[0:cw, c:c + 1], b[0:cw, 0:width], ALU.mult, ALU.add)
                    if out_bfs is not None:
                        act(out_bfs[c][0:cw, 0:width], xc, AF.Copy)
                steps.append(aff)
            return steps

        STRM = [T(strm.ap[:, c, :], f"strm{c}") for c in range(4)]
        SKEYS = [f"strm{c}" for c in range(4)]
        VG = [T(vg4.ap[:, c, :], f"vg{c}") for c in range(4)]
        VHB = [T(vhb.ap[:, c, :], f"vhb{c}") for c in range(4)]
        X1B = [T(x1b4.ap[:, c, :], f"x1b{c}") for c in range(4)]
        deferred = []

        def tick(k=1):
            for _ in range(k):
                if deferred:
                    deferred.pop(0)()

        stores = []

        def flush():
            while deferred:
                deferred.pop(0)()
            while stores:
                stores.pop(0)()

        def finish_block(t0, n):
            nch = max(1, n // 128)
            cw = min(n, 128)
            deferred.extend(layer_norm_steps(STRM[0:nch], cw, 1024, bc_g2, bc_b2, sb=SB2, with_stats=False, offload="pool"))
            stores.append(lambda: P.dma("sp", o_y[t0:t0 + n, :].rearrange("(c p) d -> p c d", p=cw),
                                        strm.ap[0:cw, 0:nch, :], reads=SKEYS[0:nch]))

        pending_finish = None
        for bi, (t0, n) in enumerate(blocks):
            nch = max(1, n // 128)
            cw = min(n, 128)
            P.dma("sp", xTb.ap[:, :, 0:n], wsc["xT"][:, t0:t0 + n].rearrange("(k p) n -> p k n", p=128),
                  reads=[f"sc_xT_{k}" for k in range(8)], writes=[xTb.key])
            wp = [wload("w_in", k, 512) for k in range(8)]
            for m in range(4):
                ps = PS[m % 2]
                for k in range(8):
                    mm(ps[:, 0:n], wp[k][:, m * 128:(m + 1) * 128], xTb[:, k, 0:n], k == 0, k == 7, k == 7)
                act(ub[:, m, 0:n], ps[:, 0:n], AF.Gelu_apprx_tanh, bias=bpp_in[:, 4 + m:5 + m])
            if pending_finish is not None:
                finish_block(*pending_finish)
                pending_finish = None
            wv = [wload("w_in", k, 1024) for k in range(8)]
            for c in range(nch):
                ps = PS[2 + c]
                for k in range(8):
                    mm(ps[0:cw, :], xTb[:, k, c * 128:c * 128 + cw], wv[k][:, 0:512], k == 0, k == 7, k == 7)
                tick(1)
            for c in range(nch):
                tt("dve", VG[c][0:cw], PS[2 + c][0:cw, :], bc_binv[0:cw], ALU.add)
            for c in range(nch):
                act(VG[c][0:cw], VG[c][0:cw], AF.Gelu_apprx_tanh)
            deferred.extend(layer_norm_steps(VG[0:nch], cw, 512, bc_sg, bc_sb, out_bfs=VHB[0:nch], offload=True))
            if n == NS:
                stores.append(lambda: P.dma("sp", o_vs, vg4.ap[0:NS, 0, :], reads=["vg0"]))
            wg = [wload("w_glu", k, 0) for k in range(4)]
            for m in range(4):
                ps = PS[m % 2]
                for k in range(4):
                    mm(ps[:, 0:n], wg[k][:, m * 128:(m + 1) * 128], yT[:, k, t0:t0 + n], k == 0, k == 3, k == 3)
                sg_ = sga if m % 2 == 0 else sgc
                act(sg_[:, 0:n], ps[:, 0:n], AF.Sigmoid, bias=bpp_glu[:, m:m + 1])
                tt("dve", ya[:, m, 0:n], sg_[:, 0:n], yT[:, m, t0:t0 + n], ALU.mult)
                tick(1)
            P.dma("sp", pTb.ap[:, :, 0:n], wsc["pT"][:, t0:t0 + n].rearrange("(k p) n -> p k n", p=128),
                  reads=[f"sc_pT_{k}" for k in range(2)], writes=[pTb.key])

            def merge_half(half):
                ysrc = ya if half == 0 else yb
                wnm = "w_pa" if half == 0 else "w_pb"
                for mg in range(2):
                    wsrc = [wload(wnm, k, mg * 512) for k in range(4)]
                    wga = [wload("w_in", k, 1536 + half * 1024 + mg * 512) for k in range(8)]
                    for ml in range(4):
                        m = mg * 4 + ml
                        mcol = 12 + half * 8 + m
                        ps = PS[m % 2]
                        for k in range(8):
                            mm(ps[:, 0:n], wga[k][:, ml * 128:(ml + 1) * 128], xTb[:, k, 0:n], k == 0, k == 7, k == 7)
                        sg_ = sga if m % 2 == 0 else sgc
                        mg_ = mg2 if m % 2 == 0 else mg3
                        act(sg_[:, 0:n], ps[:, 0:n], AF.Sigmoid, bias=bpp_in[:, mcol:mcol + 1])
                        ps2 = PS[2 + m % 2]
                        for k in range(4):
                            mm(ps2[:, 0:n], wsrc[k][:, ml * 128:(ml + 1) * 128], ysrc[:, k, 0:n], k == 0, k == 3, k == 3)
                        if half == 0:
                            tt("dve", mgA[:, m, 0:n].k(f"mgA{m}"), sg_[:, 0:n], ps2[:, 0:n], ALU.mult)
                            tick(2)
                        else:
                            tt("dve", mg_[:, 0:n], sg_[:, 0:n], ps2[:, 0:n], ALU.mult)
                            tt("pool", merged[:, m, 0:n].k(f"merged{m}"), mg_[:, 0:n], mgA[:, m, 0:n].k(f"mgA{m}"),
                               ALU.add)

            merge_half(0)
            flush()
            P.dma("sp", strm.ap[0:cw, 0:nch, :], dr["x_tok"][t0:t0 + n, :].rearrange("(c p) d -> p c d", p=cw),
                  writes=SKEYS[0:nch])
            for h in range(4):
                ps = PS[h % 2]
                if n == NS:
                    mm(ps[:, 0:NS], VHB[0][0:NS, h * 128:(h + 1) * 128], ws00b[0:NS, h, :], True, True, True)
                    ts("dve", sgb[:, 0:NS], ps[:, 0:NS], bc_bs0[:, h:h + 1], None, ALU.add)
                else:
                    for c in range(nch):
                        mm(ps[:, c * 128:(c + 1) * 128], VHB[c][:, h * 128:(h + 1) * 128], wsTb[:, h, :],
                           True, True, c == nch - 1)
                    tt("dve", sgb.r("p (c t) -> p c t", c=4), ps.r("p (c t) -> p c t", c=4),
                       bc_bs[:, h * 128:(h + 1) * 128].us(1).bc([128, 4, 128]), ALU.add)
                tt("dve", yb[:, h, 0:n], sgb[:, 0:n], ub[:, h, 0:n], ALU.mult)
            merge_half(1)
            if DBG["stop"] == "mF" and bi == 0:
                dump("merged", merged, [128, 8, 512])
                P.emit()
                return nc
            for half in range(2):
                hs = slice(half * 512, (half + 1) * 512)
                wo = [wload("w_out", k, half * 512) for k in range(8)]
                for c in range(nch):
                    ps = PS[4 + c]
                    bias_mm(ps[0:cw, :], cw, bc_bo[0:1, hs])
                    for k in range(8):
                        mm(ps[0:cw, :], merged[:, k, c * 128:c * 128 + cw].k(f"merged{k}"), wo[k][:, 0:512],
                           False, k == 7, k == 7)
                    stt("dve", STRM[c][0:cw, hs], STRM[c][0:cw, hs], ALPHA, ps[0:cw, :], ALU.mult, ALU.add)
                    ln_stat_step(STRM[c], cw, c, half, SB1)
            ln1 = layer_norm_steps(STRM[0:nch], cw, 1024, bc_g1, bc_b1, out_bfs=X1B[0:nch], with_stats=False)
            for st_ in ln1[0:nch + 1]:
                st_()
            def transposes(c):
                for kq in range(2):
                    pst = PSB4[kq]
                    for j in range(4):
                        k = kq * 4 + j
                        P.op("pe", lambda e, pst=pst, k=k, j=j, cw=cw, c=c: e.transpose(
                            pst.ap[:, j, 0:cw], x1b4.ap[0:cw, c, k * 128:(k + 1) * 128], identb.ap[0:cw, 0:cw]),
                            reads=[f"x1b{c}", identb.key], writes=[pst.key], inc=(j == 3), skip_self_w=True)
                    act(x1T[:, kq * 4:(kq + 1) * 4, c * 128:c * 128 + cw], pst[:, :, 0:cw], AF.Copy)
            for c in range(nch):
                ln1[nch + 1 + c]()
                if c >= 1:
                    transposes(c - 1)
            transposes(nch - 1)
            if DBG["stop"] == "mG" and bi == 0:
                dump("strm", strm, [128, 4, 1024]); dump("x1T", x1T, [128, 8, 512])
                P.emit()
                return nc
            for cg in range(8):
                wu_ = [wload("w_up", k, cg * 512) for k in range(8)]
                for ml in range(4):
                    ps = PS[ml % 2]
                    for k in range(8):
                        mm(ps[:, 0:n], wu_[k][:, ml * 128:(ml + 1) * 128], x1T[:, k, 0:n], k == 0, k == 7, k == 7)
                    mi = cg * 4 + ml
                    rl_ = rl if ml % 2 == 0 else rl2
                    act(rl_[:, 0:n], ps[:, 0:n], AF.Relu, bias=bpp_up[:, mi:mi + 1])
                    tt("dve", hid[:, mi, 0:n].k(f"hid{mi}"), rl_[:, 0:n], rl_[:, 0:n], ALU.mult)
            for half in range(2):
                hs = slice(half * 512, (half + 1) * 512)
                for kg in range(4):
                    wd = [wload("w_down", kg * 8 + k, half * 512) for k in range(8)]
                    for c in range(nch):
                        ps = PS[4 + c]
                        if kg == 0:
                            bias_mm(ps[0:cw, :], cw, bc_bdn[0:1, hs])
                        for k in range(8):
                            kk = kg * 8 + k
                            mm(ps[0:cw, :], hid[:, kk, c * 128:c * 128 + cw].k(f"hid{kk}"), wd[k][:, 0:512],
                               False, kk == 31, k == 7)
                wpl = [wload("w_ple", k, half * 512) for k in range(2)]
                wpg = [wload("w_pg", k, half * 512) for k in range(8)]
                for c in range(nch):
                    ps = PS[4 + c]
                    stt("dve", STRM[c][0:cw, hs], STRM[c][0:cw, hs], ALPHA, ps[0:cw, :], ALU.mult, ALU.add)
                    psl = PS[0 + 2 * (c % 2)]
                    for k in range(2):
                        mm(psl[0:cw, :], pTb[:, k, c * 128:c * 128 + cw], wpl[k][:, 0:512], k == 0, k == 1, k == 1)
                    psg = PS[1 + 2 * (c % 2)]
                    bias_mm(psg[0:cw, :], cw, bc_bpg[0:1, hs])
                    for k in range(8):
                        mm(psg[0:cw, :], x1T[:, k, c * 128:c * 128 + cw], wpg[k][:, 0:512], False, k == 7, k == 7)
                    sg_ = sga if c % 2 == 0 else sgc
                    mg_ = mg2 if c % 2 == 0 else mg3
                    act(sg_[0:cw, :], psg[0:cw, :], AF.Sigmoid)
                    tt("dve", mg_[0:cw, :], sg_[0:cw, :], psl[0:cw, :], ALU.mult)
                    tt("pool", STRM[c][0:cw, hs], STRM[c][0:cw, hs], mg_[0:cw, :], ALU.add)
                    ln_stat_step(STRM[c], cw, c, half, SB2)
            pending_finish = (t0, n)
            if DBG["stop"] == "blk%d" % bi:
                finish_block(*pending_finish)
                flush()
                P.emit()
                return nc
        finish_block(*pending_finish)
        flush()
        P.emit()
    return nc


_CACHE = {}


def kernel(**inputs):
    inp = {k: np.asarray(v) for k, v in inputs.items()}
    sh = _host_layout(inp)
    cores = [_core_layout(inp, b) for b in range(8)]
    key = "prog"
    if key not in _CACHE:
        _CACHE[key] = build_program({k: v.shape for k, v in sh.items()}, {k: v.shape for k, v in cores[0].items()})
    nc = _CACHE[key]
    in_maps = [{**sh, **cores[b]} for b in range(8)]
    res = run_bass_kernel_spmd(nc, in_maps, core_ids=list(range(8)))
    R = res.results
    y_p = np.stack([R[b]["o_y"][:SEQ] for b in range(8)], 0).astype(np.float32)
    y_s = np.concatenate([R[b]["o_y"][SEQ:TT] for b in range(8)], 0).reshape(128, 1, D).astype(np.float32)

    def unp(a):
        return a.reshape(2, 64, 16).transpose(2, 0, 1).reshape(32, 64)

    def uns(a):
        return a.reshape(2, 64, 16, 16).transpose(3, 2, 0, 1).reshape(16, 32, 64)

    spre = np.stack([unp(R[b]["o_spre"]) for b in range(8)], 0)[None].astype(np.float32)
    spim = np.stack([unp(R[b]["o_spim"]) for b in range(8)], 0)[None].astype(np.float32)
    ssre = np.concatenate([uns(R[b]["o_ssre"]) for b in range(8)], 0)[None].astype(np.float32)
    ssim = np.concatenate([uns(R[b]["o_ssim"]) for b in range(8)], 0)[None].astype(np.float32)
    vs = np.concatenate([R[b]["o_vs"] for b in range(8)], 0).reshape(1, 128, 1, 512).astype(np.float32)
    return (y_p, y_s, spre, spim, ssre, ssim, vs)
```

```python
import contextlib
import math
import numpy as np
import concourse.bass as bass
import concourse.mybir as mybir
from concourse.bass_utils import run_bass_kernel_spmd

F32 = mybir.dt.float32
BF16 = mybir.dt.bfloat16
ALU = mybir.AluOpType
AF = mybir.ActivationFunctionType

ENGS = ("pe", "act", "dve", "pool", "sp")
NSLOT = {"sp": 24, "act": 8, "pool": 16}
NBG = 40

D = 1024
SEQ = 2048
NS = 16
TT = SEQ + NS
L = 16
NK = SEQ // L
ALPHA = float(2.0 ** 0.25)
EPS = 1e-5
PI = math.pi


class Prog:
    def __init__(self, nc):
        self.nc = nc
        self.q = {e: [] for e in ENGS}
        self.cnt = {}
        self.last_w = {}
        self.readers = {}
        self.waited = {}
        self.dma_n = {e: 0 for e in ENGS}
        self.sems = {}
        self.n_inst = 0

    def _deps(self, eng, reads, writes, skip_self_w=False):
        deps = {}

        def add(d, same_ok=True):
            if d is None:
                return
            pid, c = d
            if pid == eng and not same_ok:
                return
            if deps.get(pid, 0) < c:
                deps[pid] = c

        for r in reads:
            add(self.last_w.get(r))
        for w in writes:
            add(self.last_w.get(w), same_ok=not skip_self_w)
            for rd in self.readers.get(w, ()):
                add(rd, same_ok=not skip_self_w)
        out = []
        for pid, c in deps.items():
            if self.waited.get((eng, pid), 0) >= c:
                continue
            if pid in ENGS and c > self.cnt.get(pid, 0):
                if pid == eng:
                    continue
                raise RuntimeError(f"dep on pending no-inc instr {pid}:{c} from {eng}")
            self.waited[(eng, pid)] = c
            out.append((pid, c))
        return out

    def _record(self, pid, c, reads, writes):
        for r in reads:
            self.readers.setdefault(r, []).append((pid, c))
        for w in writes:
            self.last_w[w] = (pid, c)
            self.readers[w] = []

    def op(self, eng, fn, reads=(), writes=(), inc=True, skip_self_w=False):
        reads = tuple(reads)
        writes = tuple(writes)
        deps = self._deps(eng, reads, writes, skip_self_w)
        c = self.cnt.get(eng, 0) + 1
        if inc:
            self.cnt[eng] = c
        self._record(eng, c, reads, writes)
        self.q[eng].append(("op", fn, deps, inc))
        self.n_inst += 1

    def dma(self, eng, out, in_, reads=(), writes=(), bg=False):
        reads = tuple(reads)
        writes = tuple(writes)
        if bg:
            n = self.dma_n.get("bg", 0)
            self.dma_n["bg"] = n + 1
            pid = ("dma", "bg", n % NBG)
        else:
            n = self.dma_n[eng]
            self.dma_n[eng] = n + 1
            pid = ("dma", eng, n % NSLOT[eng])
        deps = self._deps(eng, reads, writes)
        pc = self.cnt.get(pid, 0)
        if pc > 0 and self.waited.get((eng, pid), 0) < pc:
            self.waited[(eng, pid)] = pc
            deps.append((pid, pc))
        c = pc + 16
        self.cnt[pid] = c
        self._record(pid, c, reads, writes)
        self.q[eng].append(("dma", (out, in_), deps, pid))
        self.n_inst += 1

    def barrier(self):
        snap = dict(self.cnt)
        for e in ENGS:
            deps = []
            for pid, c in snap.items():
                if pid == e or c == 0 or (isinstance(pid, tuple) and pid[1] == "bg"):
                    continue
                if self.waited.get((e, pid), 0) < c:
                    self.waited[(e, pid)] = c
                    deps.append((pid, c))
            self.q[e].append(("wait", None, deps, None))

    def emit(self):
        nc = self.nc
        pids = list(ENGS)
        for e in ("sp", "act", "pool"):
            for s in range(NSLOT[e]):
                pids.append(("dma", e, s))
        for s in range(NBG):
            pids.append(("dma", "bg", s))
        with contextlib.ExitStack() as st:
            for pid in pids:
                name = pid if isinstance(pid, str) else f"dma_{pid[1]}_{pid[2]}"
                self.sems[pid] = st.enter_context(nc.semaphore("s_" + name))
            block = st.enter_context(nc.Block())
            final_waits = {e: [] for e in ENGS}
            for e in ("sp", "act", "pool"):
                for s in range(NSLOT[e]):
                    pid = ("dma", e, s)
                    c = self.cnt.get(pid, 0)
                    if c:
                        final_waits[e].append((pid, c))
            for s in range(NBG):
                c = self.cnt.get(("dma", "bg", s), 0)
                if c:
                    final_waits["pool"].append((("dma", "bg", s), c))

            def run(e, handle):
                for kind, payload, deps, extra in self.q[e]:
                    for pid, c in deps:
                        handle.wait_ge(self.sems[pid], c)
                    if kind == "op":
                        ins = payload(handle)
                        if extra:
                            ins.then_inc(self.sems[e], 1)
                    elif kind == "dma":
                        out, in_ = payload
                        handle.dma_start(out=out, in_=in_).then_inc(self.sems[extra], 16)
                for pid, c in final_waits[e]:
                    handle.wait_ge(self.sems[pid], c)

            @block.sync
            def _(h):
                run("sp", h)

            @block.scalar
            def _(h):
                run("act", h)

            @block.vector
            def _(h):
                run("dve", h)

            @block.gpsimd
            def _(h):
                run("pool", h)

            @block.tensor
            def _(h):
                run("pe", h)


class T:
    def __init__(self, ap, key):
        self.ap = ap
        self.key = key

    def __getitem__(self, idx):
        return T(self.ap[idx], self.key)

    def r(self, pat, **kw):
        return T(self.ap.rearrange(pat, **kw), self.key)

    def bc(self, shape):
        return T(self.ap.to_broadcast(list(shape)), self.key)

    def us(self, ax):
        return T(self.ap.unsqueeze(ax), self.key)

    def k(self, key):
        return T(self.ap, key)


def _pp(v, m):
    return np.ascontiguousarray(v.reshape(m, 128).T)


def _host_layout(inp):
    f = np.float32
    sh = {}
    sh["w_in"] = np.ascontiguousarray(inp["w_in"][0])
    sh["w_glu"] = np.ascontiguousarray(inp["w_glu"][0])
    sh["w_pa"] = np.ascontiguousarray(inp["w_branch_a"][0])
    sh["w_pb"] = np.ascontiguousarray(inp["w_branch_b"][0])
    sh["w_out"] = np.ascontiguousarray(inp["w_out"][0])
    sh["w_up"] = np.ascontiguousarray(inp["w_up"][0])
    sh["w_down"] = np.ascontiguousarray(inp["w_down"][0])
    sh["w_ple"] = np.ascontiguousarray(inp["w_ple"][0])
    sh["w_pg"] = np.ascontiguousarray(inp["w_ple_gate"][0])
    sh["b_in_pp"] = _pp(inp["b_in"][0], 28)
    sh["b_glu_pp"] = _pp(inp["b_glu"][0], 4)
    sh["b_up_pp"] = _pp(inp["b_up"][0], 32)
    rows = [inp["b_in"][0][1024:1536], inp["sgu_ln_g"][0], inp["sgu_ln_b"][0],
            inp["sgu_b"][0].reshape(-1),
            inp["b_out"][0], inp["ln1_g"][0], inp["ln1_b"][0], inp["b_down"][0],
            inp["b_ple_gate"][0], inp["ln2_g"][0], inp["ln2_b"][0],
            inp["sgu_b"][0][:, 0]]
    row = np.concatenate([np.asarray(r, f).reshape(-1) for r in rows])
    sh["bc"] = np.ascontiguousarray(np.broadcast_to(row[None, :], (128, row.size)))
    sh["wsT"] = np.ascontiguousarray(np.transpose(inp["sgu_w"][0], (2, 0, 1)))
    sh["mask"] = np.ascontiguousarray(np.triu(np.ones((128, 128), f)))
    ws00 = np.zeros((16, 4, 16), f)
    for h in range(4):
        ws00[np.arange(16), h, np.arange(16)] = inp["sgu_w"][0, h, 0, 0]
    sh["ws00"] = ws00
    sh["ident"] = np.eye(128, dtype=f)
    lr = inp["ssm_lambda_re"][0]; li = inp["ssm_lambda_im"][0]; ldt = inp["ssm_log_dt"][0]
    def o1(a):
        return np.ascontiguousarray(a.reshape(16, 2, 64).transpose(1, 2, 0).reshape(128, 16))
    sh["lr1"] = o1(lr); sh["li1"] = o1(li)
    sh["ldt1"] = o1(np.broadcast_to(ldt[:, None], (32, 64)))
    def o1b(b):
        out = np.zeros((2, 64, 16, 2, 16), f)
        bb = b.reshape(16, 2, 64, 16)
        for g2 in range(2):
            out[g2, :, :, g2, :] = bb[:, g2].transpose(1, 0, 2)
        return out.reshape(128, 16, 32)
    sh["B1re"] = o1b(inp["ssm_b_re"][0]); sh["B1im"] = o1b(inp["ssm_b_im"][0])
    sh["C1re"] = o1b(np.transpose(inp["ssm_c_re"][0], (0, 2, 1)))
    sh["C1im"] = o1b(np.transpose(inp["ssm_c_im"][0], (0, 2, 1)))
    def o2(a):
        aa = a.reshape(4, 4, 2, 64)
        out = np.broadcast_to(aa.transpose(1, 0, 2, 3)[:, None, None], (4, 2, 16, 4, 2, 64))
        return np.ascontiguousarray(out.reshape(128, 4, 128))
    sh["lrc"] = np.ascontiguousarray(np.concatenate([sh.pop("lr1"), o2(lr).reshape(128, 512)], 1))
    sh["lic"] = np.ascontiguousarray(np.concatenate([sh.pop("li1"), o2(li).reshape(128, 512)], 1))
    sh["ldtc"] = np.ascontiguousarray(np.concatenate(
        [sh.pop("ldt1"), o2(np.broadcast_to(ldt[:, None], (32, 64))).reshape(128, 512)], 1))
    def o2b(b):
        bb = b.reshape(4, 4, 2, 64, 16)
        out = np.zeros((4, 2, 16, 4, 2, 64), f)
        for g2 in range(2):
            out[:, g2, :, :, g2, :] = bb[:, :, g2].transpose(1, 3, 0, 2)
        return out.reshape(128, 4, 128)
    sh["B2re"] = o2b(inp["ssm_b_re"][0]); sh["B2im"] = o2b(inp["ssm_b_im"][0])
    dsk = np.zeros((4, 2, 16, 4, 2, 16), f)
    dd = inp["ssm_d"][0].reshape(4, 4, 2, 16)
    for g2 in range(2):
        for c in range(16):
            dsk[:, g2, c, :, g2, c] = dd[:, :, g2, c].T
    sh["dsk"] = dsk.reshape(128, 4, 32)
    return sh


def _core_layout(inp, b):
    f = np.float32
    xs = inp["x_sample"][16 * b:16 * b + 16, 0]
    xtok = np.concatenate([inp["x_prompt"][b], xs], 0)
    ptok = np.concatenate([inp["p_prompt"][0, b], inp["p_sample"][0, 16 * b:16 * b + 16, 0]], 0)
    def oh(a):
        return np.ascontiguousarray(a.reshape(16, 16, 2, 64).transpose(2, 3, 1, 0).reshape(128, 16, 16))
    return {
        "x_tok": np.ascontiguousarray(xtok, f),
        "xT": np.ascontiguousarray(xtok.T, f),
        "pT": np.ascontiguousarray(ptok.T, f),
        "h0re": oh(inp["state_ssm_re"][0, 16 * b:16 * b + 16]),
        "h0im": oh(inp["state_ssm_im"][0, 16 * b:16 * b + 16]),
    }


DBG = {"stop": None}


def build_program(shared_shapes, core_shapes):
    nc = bass.Bass("TRN2", target_bir_lowering=False)
    P = Prog(nc)
    dbg_outs = {}

    def dump(name, t, shape):
        o = nc.dram_tensor("dbg_" + name, list(shape), F32, kind="ExternalOutput").ap()
        P.dma("pool", o, t.ap, reads=[t.key])
        dbg_outs[name] = o
    dr = {}
    for k, shp in list(shared_shapes.items()) + list(core_shapes.items()):
        dr[k] = nc.dram_tensor(k, list(shp), F32, kind="ExternalInput").ap()
    o_y = nc.dram_tensor("o_y", [TT, D], F32, kind="ExternalOutput").ap()
    o_spre = nc.dram_tensor("o_spre", [128, 16], F32, kind="ExternalOutput").ap()
    o_spim = nc.dram_tensor("o_spim", [128, 16], F32, kind="ExternalOutput").ap()
    o_ssre = nc.dram_tensor("o_ssre", [128, 16, 16], F32, kind="ExternalOutput").ap()
    o_ssim = nc.dram_tensor("o_ssim", [128, 16, 16], F32, kind="ExternalOutput").ap()
    o_vs = nc.dram_tensor("o_vs", [16, 512], F32, kind="ExternalOutput").ap()

    st = contextlib.ExitStack()
    with st:
        A32W = 22900
        A16W = 60500
        a32 = st.enter_context(nc.sbuf_tensor("a32", [128, A32W], F32))
        a16 = st.enter_context(nc.sbuf_tensor("a16", [128, A16W], BF16))
        psb = [st.enter_context(nc.psum_tensor(f"ps{i}", [128, 512], F32)) for i in range(8)]
        PS = [T(psb[i][:, :], f"ps{i}") for i in range(8)]

        class Arena:
            def __init__(s, t, size, nm):
                s.t, s.size, s.nm, s.off, s.n = t, size, nm, 0, 0

            def alloc(s, shape, key=None):
                n = int(np.prod(shape[1:]))
                assert s.off + n <= s.size, (s.nm, s.off, n, s.size)
                ap = s.t[0:shape[0], s.off:s.off + n]
                s.off += n
                s.n += 1
                key = key or f"{s.nm}_{s.n}"
                if len(shape) == 3:
                    ap = ap.rearrange("p (a b) -> p a b", a=shape[1])
                elif len(shape) == 4:
                    ap = ap.rearrange("p (a b c) -> p a b c", a=shape[1], b=shape[2])
                return T(ap, key)

        R32 = Arena(a32, A32W, "f")
        R16 = Arena(a16, A16W, "h")

        yT = R16.alloc([128, 4, TT], "yT")
        bpp_in = R32.alloc([128, 28], "bpp_in")
        bpp_glu = R32.alloc([128, 4], "bpp_glu")
        bpp_up = R32.alloc([128, 32], "bpp_up")
        P.dma("sp", bpp_in.ap, dr["b_in_pp"], writes=[bpp_in.key])
        P.dma("sp", bpp_glu.ap, dr["b_glu_pp"], writes=[bpp_glu.key])
        P.dma("sp", bpp_up.ap, dr["b_up_pp"], writes=[bpp_up.key])
        mark32, mark16 = R32.off, R16.off

        def tt(eng, out, a, b, op):
            P.op(eng, lambda e: e.tensor_tensor(out=out.ap, in0=a.ap, in1=b.ap, op=op),
                 reads=[a.key, b.key], writes=[out.key])

        def ts(eng, out, a, s1, s2, op0, op1=None):
            rd = [a.key]
            s1a = s1.ap if isinstance(s1, T) else s1
            s2a = s2.ap if isinstance(s2, T) else s2
            if isinstance(s1, T):
                rd.append(s1.key)
            if isinstance(s2, T):
                rd.append(s2.key)
            if op1 is None:
                P.op(eng, lambda e: e.tensor_scalar(out=out.ap, in0=a.ap, scalar1=s1a, scalar2=None, op0=op0),
                     reads=rd, writes=[out.key])
            else:
                P.op(eng, lambda e: e.tensor_scalar(out=out.ap, in0=a.ap, scalar1=s1a, scalar2=s2a, op0=op0, op1=op1),
                     reads=rd, writes=[out.key])

        def stt(eng, out, a, s, b, op0, op1):
            eng = "dve"
            rd = [a.key, b.key]
            sa = s.ap if isinstance(s, T) else s
            if isinstance(s, T):
                rd.append(s.key)
            P.op(eng, lambda e: e.scalar_tensor_tensor(out=out.ap, in0=a.ap, scalar=sa, in1=b.ap, op0=op0, op1=op1),
                 reads=rd, writes=[out.key])

        def cp(eng, out, a):
            P.op(eng, lambda e: e.tensor_copy(out=out.ap, in_=a.ap), reads=[a.key], writes=[out.key])

        def act(out, a, func, bias=None, scale=None):
            rd = [a.key]
            kw = {}
            if bias is not None:
                kw["bias"] = bias.ap if isinstance(bias, T) else bias
                if isinstance(bias, T):
                    rd.append(bias.key)
            if scale is not None:
                kw["scale"] = scale.ap if isinstance(scale, T) else scale
                if isinstance(scale, T):
                    rd.append(scale.key)
            P.op("act", lambda e: e.activation(out=out.ap, in_=a.ap, func=func, **kw), reads=rd, writes=[out.key])

        def memset(eng, out, v):
            P.op(eng, lambda e: e.memset(out.ap, v), writes=[out.key])

        def mm(out, lhsT, rhs, start, stop, inc, tp=None):
            kw = {}
            if tp is not None:
                kw["tile_position"] = tp
            P.op("pe", lambda e: e.matmul(out.ap, lhsT=lhsT.ap, rhs=rhs.ap, start=start, stop=stop, **kw),
                 reads=[lhsT.key, rhs.key], writes=[out.key], inc=inc, skip_self_w=True)

        def cmul(eng, ore, oim, are, aim, bre, bim, t1, t2):
            tt(eng, t1, are, bre, ALU.mult)
            tt(eng, t2, aim, bim, ALU.mult)
            tt(eng, ore, t1, t2, ALU.subtract)
            tt(eng, t1, are, bim, ALU.mult)
            tt(eng, t2, aim, bre, ALU.mult)
            tt(eng, oim, t1, t2, ALU.add)

        def exp_taylor(eng, out, x, deg):
            ts(eng, out, x, 1.0 / math.factorial(deg), None, ALU.mult)
            for kk in range(deg - 1, 0, -1):
                stt(eng, out, out, 1.0 / math.factorial(kk), x, ALU.add, ALU.mult)
            ts(eng, out, out, 1.0, None, ALU.add)

        def disc(eng, shape, lr, li, ldt, pref):
            al = lambda nm: R32.alloc(shape, pref + nm)
            s0 = al("s0"); s1 = al("s1"); s2 = al("s2"); s3 = al("s3"); s4 = al("s4"); s5 = al("s5")
            s6 = al("s6"); s7 = al("s7"); s8 = al("s8"); t1 = al("t1"); t2 = al("t2")
            dt, a, th, mag, imag, n, thr, sn, cs = s0, s1, s2, s3, s4, s5, s6, s7, s8
            c0 = 4.605170185988092
            ts(eng, t1, ldt, c0, 0.25, ALU.add, ALU.mult)
            exp_taylor(eng, dt, t1, 10)
            tt(eng, dt, dt, dt, ALU.mult)
            tt(eng, dt, dt, dt, ALU.mult)
            ts(eng, dt, dt, math.exp(-c0), None, ALU.mult)
            tt(eng, a, lr, dt, ALU.mult)
            tt(eng, th, li, dt, ALU.mult)
            exp_taylor(eng, mag, a, 6)
            ts(eng, t1, a, -1.0, None, ALU.mult)
            exp_taylor(eng, imag, t1, 6)
            ts(eng, n, th, PI, None, ALU.is_ge)
            for m_ in (3, 5, 7):
                stt(eng, n, th, m_ * PI, n, ALU.is_ge, ALU.add)
            stt(eng, thr, n, -2.0 * PI, th, ALU.mult, ALU.add)
            tt(eng, t2, thr, thr, ALU.mult)
            sc = [(-1.0) ** k / math.factorial(2 * k + 1) for k in range(10)]
            cc = [(-1.0) ** k / math.factorial(2 * k) for k in range(11)]
            ts(eng, sn, t2, sc[9], None, ALU.mult)
            for k in range(8, 0, -1):
                stt(eng, sn, sn, sc[k], t2, ALU.add, ALU.mult)
            stt(eng, sn, sn, 1.0, thr, ALU.add, ALU.mult)
            ts(eng, cs, t2, cc[10], None, ALU.mult)
            for k in range(9, 0, -1):
                stt(eng, cs, cs, cc[k], t2, ALU.add, ALU.mult)
            ts(eng, cs, cs, 1.0, None, ALU.add)
            are, aim, ire, iim = s0, s1, s2, s4
            tt(eng, are, mag, cs, ALU.mult)
            tt(eng, aim, mag, sn, ALU.mult)
            tt(eng, ire, imag, cs, ALU.mult)
            tt(eng, iim, imag, sn, ALU.mult)
            ts(eng, iim, iim, -1.0, None, ALU.mult)
            den, nr = s5, s6
            tt(eng, t1, lr, lr, ALU.mult)
            tt(eng, t2, li, li, ALU.mult)
            tt(eng, den, t1, t2, ALU.add)
            P.op("dve", lambda e: e.reciprocal(out=den.ap, in_=den.ap), reads=[den.key], writes=[den.key])
            ts(eng, nr, are, -1.0, None, ALU.add)
            cre, cim = s7, s8
            tt(eng, t1, nr, lr, ALU.mult)
            tt(eng, t2, aim, li, ALU.mult)
            tt(eng, t1, t1, t2, ALU.add)
            tt(eng, cre, t1, den, ALU.mult)
            tt(eng, t1, aim, lr, ALU.mult)
            tt(eng, t2, nr, li, ALU.mult)
            tt(eng, t1, t1, t2, ALU.subtract)
            tt(eng, cim, t1, den, ALU.mult)
            return dict(are=are, aim=aim, ire=ire, iim=iim, cre=cre, cim=cim, t1=t1, t2=t2, mag=mag)

        mark16_main = R16.off
        uT = R16.alloc([128, 4, TT], "uT")
        mark32, mark16 = R32.off, R16.off
        blocks = [(i * 512, 512) for i in range(4)] + [(SEQ, NS)]
        Bb = R16.alloc([128, 2, 16, 32], "Bb")
        CA = R16.alloc([128, 2, 16 * 17 * 32], "CA")
        mark16_lt = R16.off
        Wu = R16.alloc([128, 8, 512], "Wu")
        P.dma("pool", Wu.ap, dr["w_in"][:, 0:512].rearrange("(k p) n -> p k n", p=128), writes=[Wu.key])
        xb_s5 = [R16.alloc([128, 8, 512], f"xb_s5_{i}") for i in range(5)]
        for bi, (t0, n) in enumerate(blocks):
            P.dma("pool", xb_s5[bi].ap[:, :, 0:n], dr["xT"][:, t0:t0 + n].rearrange("(k p) n -> p k n", p=128),
                  writes=[xb_s5[bi].key])
        wsc = {}
        cast_list = [("w_in", 1024, 3584), ("w_glu", 512, 512), ("w_pa", 512, 1024), ("w_pb", 512, 1024),
                     ("w_out", 1024, 1024), ("w_up", 1024, 4096), ("w_down", 4096, 1024), ("w_ple", 256, 1024),
                     ("w_pg", 1024, 1024), ("xT", 1024, TT), ("pT", 256, TT)]
        for nm, r_, c_ in cast_list:
            wsc[nm] = nc.dram_tensor("sc_" + nm, [r_, c_], BF16, kind="Internal").ap()
            step = 256 if c_ <= 2064 else 128
            for k in range(0, r_, step):
                P.dma("pool", wsc[nm][k:k + step, :], dr[nm][k:k + step, :], writes=[f"sc_{nm}_{k // 128}"] +
                      ([f"sc_{nm}_{k // 128 + 1}"] if step == 256 else []), bg=True)
        def ld(name, shape, eng="sp"):
            t = R32.alloc(shape, name)
            P.dma(eng, t.ap, dr[name], writes=[t.key])
            return t

        dsk = ld("dsk", [128, 4, 32], "act")
        P1re = R32.alloc([128, 17, 16], "P1re"); P1im = R32.alloc([128, 17, 16], "P1im")
        I1re = R32.alloc([128, 16, 16], "I1re"); I1im = R32.alloc([128, 16, 16], "I1im")
        rho = R32.alloc([128, 16], "rho")
        mark32b = R32.off
        lrc = ld("lrc", [128, 528]); lic = ld("lic", [128, 528]); ldtc = ld("ldtc", [128, 528])
        B1re = ld("B1re", [128, 16, 32]); B1im = ld("B1im", [128, 16, 32])
        C1re = ld("C1re", [128, 16, 32]); C1im = ld("C1im", [128, 16, 32])
        B2re = ld("B2re", [128, 4, 128], "act"); B2im = ld("B2im", [128, 4, 128], "act")
        dd = disc("dve", [128, 528], lrc, lic, ldtc, "dd")
        d1 = {k: v[:, 0:16] for k, v in dd.items()}
        d2 = {k: v[:, 16:528].r("p (a b) -> p a b", a=4) for k, v in dd.items()}
        pt1 = R32.alloc([128, 8, 16], "pt1"); pt2 = R32.alloc([128, 8, 16], "pt2")
        tt("dve", rho, d1["mag"], d1["mag"], ALU.mult)
        for _ in range(3):
            tt("dve", rho, rho, rho, ALU.mult)
        memset("dve", P1re[:, 0, :], 1.0)
        memset("dve", P1im[:, 0, :], 0.0)
        cp("dve", P1re[:, 1, :], d1["are"])
        cp("dve", P1im[:, 1, :], d1["aim"])
        n_ = 1
        while n_ < 16:
            sl_o = slice(n_ + 1, 2 * n_ + 1)
            sl_i = slice(1, n_ + 1)
            bre = P1re[:, n_:n_ + 1, :].bc([128, n_, 16]); bim = P1im[:, n_:n_ + 1, :].bc([128, n_, 16])
            cmul("dve", P1re[:, sl_o, :], P1im[:, sl_o, :],
                 P1re[:, sl_i, :], P1im[:, sl_i, :], bre, bim, pt1[:, 0:n_, :], pt2[:, 0:n_, :])
            n_ *= 2
        w1 = dd["t1"][:, 16:528].r("p (a b) -> p a b", a=16); w2 = dd["t2"][:, 16:528].r("p (a b) -> p a b", a=16)
        cob_re = d1["cre"].us(2).bc([128, 16, 32]); cob_im = d1["cim"].us(2).bc([128, 16, 32])
        cmul("dve", Bb[:, 0], Bb[:, 1], B1re, B1im, cob_re, cob_im, w1, w2)
        CAv = lambda ri: CA[:, ri, :].r("p (q m c) -> p q m c", q=16, m=17)
        y1_ = R32.alloc([128, 17, 32], "y1_"); y2_ = R32.alloc([128, 17, 32], "y2_")
        z1_ = R32.alloc([128, 17, 32], "z1_"); z2_ = R32.alloc([128, 17, 32], "z2_")
        for q in range(16):
            cr = C1re[:, q:q + 1, :].bc([128, 17, 32]); ci = C1im[:, q:q + 1, :].bc([128, 17, 32])
            pr = P1re[:, :, q:q + 1].bc([128, 17, 32]); pi_ = P1im[:, :, q:q + 1].bc([128, 17, 32])
            eng, a1, a2 = ("dve", z1_, z2_) if q % 2 == 0 else ("dve", y1_, y2_)
            tt(eng, a1, cr, pr, ALU.mult)
            tt(eng, a2, ci, pi_, ALU.mult)
            tt(eng, CAv(0)[:, q].k(f"CA0_{q}"), a1, a2, ALU.subtract)
            tt(eng, a1, cr, pi_, ALU.mult)
            tt(eng, a2, ci, pr, ALU.mult)
            stt(eng, CAv(1)[:, q].k(f"CA1_{q}"), a1, -1.0, a2, ALU.mult, ALU.subtract)
        Are = P1re[:, 16, :]; Aim = P1im[:, 16, :]
        cp("dve", I1re[:, 0, :], d1["ire"]); cp("dve", I1im[:, 0, :], d1["iim"])
        n_ = 1
        while n_ < 16:
            sl_o = slice(n_, 2 * n_); sl_i = slice(0, n_)
            bre = I1re[:, n_ - 1:n_, :].bc([128, n_, 16]); bim = I1im[:, n_ - 1:n_, :].bc([128, n_, 16])
            cmul("dve", I1re[:, sl_o, :], I1im[:, sl_o, :],
                 I1re[:, sl_i, :], I1im[:, sl_i, :], bre, bim, pt1[:, 0:n_, :], pt2[:, 0:n_, :])
            n_ *= 2
        ai15re = I1re[:, 14, :]; ai15im = I1im[:, 14, :]

        for bi, (t0, n) in enumerate(blocks):
            xb = xb_s5[bi]
            for m in range(4):
                ps = PS[m % 2]
                for k in range(8):
                    mm(ps[:, 0:n], Wu[:, k, m * 128:(m + 1) * 128], xb[:, k, 0:n], k == 0, k == 7, k == 7)
                act(uT[:, m, t0:t0 + n], ps[:, 0:n], AF.Identity, bias=bpp_in[:, m:m + 1])
        if DBG["stop"] == "p0":
            dump("uT", uT, [128, 4, TT])
            P.emit()
            return nc
        P.barrier()
        R16.off = mark16_lt

        LT = R16.alloc([128, 4, 2 * 16 * 128], "LT")
        LTv = lambda qt, ri: LT[:, qt, :].r("p (r j n) -> p r j n", r=2, j=16)[:, ri].k(f"LT{qt}")
        Bb2re = R32.alloc([128, 4, 128], "Bb2re"); Bb2im = R32.alloc([128, 4, 128], "Bb2im")
        cmul("dve", Bb2re, Bb2im, B2re, B2im, d2["cre"], d2["cim"], d2["t1"], d2["t2"])
        IPre = R32.alloc([128, 4, 128], "IPre"); IPim = R32.alloc([128, 4, 128], "IPim")
        iq1 = dd["t1"][:, 16:144]; iq2 = dd["t2"][:, 16:144]
        L32re = R32.alloc([128, 16, 128], "L32re"); L32im = R32.alloc([128, 16, 128], "L32im")
        q1 = R32.alloc([128, 8, 128], "q1"); q2 = R32.alloc([128, 8, 128], "q2")
        for qt in range(4):
            cs_ = slice(0, 128)
            cp("dve", IPre[:, 0, :], d2["ire"][:, qt, :]); cp("dve", IPim[:, 0, :], d2["iim"][:, qt, :])
            for lv in range(3):
                cmul("dve", IPre[:, lv + 1, :], IPim[:, lv + 1, :], IPre[:, lv, :], IPim[:, lv, :],
                     IPre[:, lv, :], IPim[:, lv, :], iq1, iq2)
            cmul("dve", L32re[:, 0, :], L32im[:, 0, :], Bb2re[:, qt, :], Bb2im[:, qt, :],
                 IPre[:, 0, cs_], IPim[:, 0, cs_], q1[:, 0, :], q2[:, 0, :])
            n_ = 1
            lv = 0
            while n_ < 16:
                sl_o = slice(n_, 2 * n_); sl_i = slice(0, n_)
                bre = IPre[:, lv, cs_].us(1).bc([128, n_, 128]); bim = IPim[:, lv, cs_].us(1).bc([128, n_, 128])
                cmul("dve", L32re[:, sl_o, :], L32im[:, sl_o, :],
                     L32re[:, sl_i, :], L32im[:, sl_i, :], bre, bim, q1[:, 0:n_, :], q2[:, 0:n_, :])
                n_ *= 2
                lv += 1
            act(LTv(qt, 0), L32re, AF.Copy)
            act(LTv(qt, 1), L32im, AF.Copy)

        Kt = R16.alloc([128, 4, 16, 32], "Kt")
        kt32 = R32.alloc([128, 16, 32], "kt32")
        for qt in range(4):
            ps = PS[2 + qt % 2]
            for q4 in range(4):
                q = qt * 4 + q4
                for ri in range(2):
                    mm(ps[32 * q4:32 * q4 + 32, :], Bb[:, ri, q, :],
                       CA[:, ri, q * 544:q * 544 + 512].k(f"CA{ri}_{q}"), ri == 0, ri == 1, (ri == 1 and q4 == 3), tp=(0, 32 * q4))
            cp("dve", kt32, ps.r("p (m c) -> p m c", m=16))
            tt("dve", kt32[:, 0, :], kt32[:, 0, :], dsk[:, qt, :], ALU.add)
            cp("dve", Kt[:, qt], kt32)
        Hp = R16.alloc([128, 2, 16, NK + NS], "Hp")
        if DBG["stop"] == "p1a":
            dump("P1re", P1re, [128, 17, 16]); dump("P1im", P1im, [128, 17, 16])
            dump("I1re", I1re, [128, 16, 16]); dump("I1im", I1im, [128, 16, 16])
            dump("Bb", Bb, [128, 2, 16, 32]); dump("CA", CA, [128, 2, 16 * 17 * 32])
            dump("LT", LT, [128, 4, 2 * 16 * 128]); dump("Kt", Kt, [128, 4, 16, 32])
            P.emit()
            return nc
        P.barrier()
        R32.off = mark32b

        h0re = ld("h0re", [128, 16, 16], "act"); h0im = ld("h0im", [128, 16, 16], "act")
        Gp = R32.alloc([128, 2, 16, 128], "Gp")
        Gs = R32.alloc([128, 2, 16, 16], "Gs")
        for ri in range(2):
            for qt in range(4):
                for j in range(16):
                    for q4 in range(4):
                        ps = PS[4 + q4]
                        mm(ps[:, qt * 128:(qt + 1) * 128], LTv(qt, ri)[32 * q4:32 * q4 + 32, j, :],
                           uT[32 * q4:32 * q4 + 32, qt, j:SEQ:16], j == 0, j == 15,
                           (j == 15 and qt == 3), tp=(32 * q4, 0))
            for q4 in range(4):
                cp("dve", Gp[:, ri, q4:16:4, :], PS[4 + q4].r("p (a k) -> p a k", a=4))
        for q4 in range(4):
            ps = PS[4 + q4]
            for ri in range(2):
                for qt in range(4):
                    cb = (ri * 4 + qt) * 16
                    mm(ps[:, cb:cb + 16], LTv(qt, ri)[32 * q4:32 * q4 + 32, 15, :],
                       uT[32 * q4:32 * q4 + 32, qt, SEQ:TT], True, True, (ri == 1 and qt == 3), tp=(32 * q4, 0))
        for q4 in range(4):
            for ri in range(2):
                cp("dve", Gs[:, ri, q4:16:4, :], PS[4 + q4][:, ri * 64:ri * 64 + 64].r("p (a t) -> p a t", a=4))
        if DBG["stop"] == "p1b1":
            dump("Gp", Gp, [128, 2, 16, 128]); dump("Gs", Gs, [128, 2, 16, 16])
            P.emit()
            return nc
        KA = R32.alloc([128, 2, 16, 128], "KA")
        k1 = R32.alloc([128, 16, 128], "k1"); k2 = R32.alloc([128, 16, 128], "k2")
        Wre = R32.alloc([128, 16, 128], "Wre"); Wim = R32.alloc([128, 16, 128], "Wim")
        RHO = R32.alloc([128, 16, 128], "RHO")
        ure = R32.alloc([128, 16], "ure"); uim = R32.alloc([128, 16], "uim")
        sp1 = R32.alloc([128, 16], "sp1"); sp2 = R32.alloc([128, 16], "sp2")
        P.op("dve", lambda e: e.reciprocal(out=sp1.ap, in_=rho.ap), reads=[rho.key], writes=[sp1.key])
        tt("dve", ure, Are, sp1, ALU.mult)
        tt("dve", uim, Aim, sp1, ALU.mult)
        tt("dve", sp1, ure, ure, ALU.mult)
        tt("dve", sp2, uim, uim, ALU.mult)
        tt("dve", sp1, sp1, sp2, ALU.add)
        ts("dve", sp1, sp1, -0.5, 1.5, ALU.mult, ALU.add)
        tt("dve", ure, ure, sp1, ALU.mult)
        tt("dve", uim, uim, sp1, ALU.mult)
        cp("dve", Wre[:, :, 0], ure); cp("dve", Wim[:, :, 0], uim)
        n_ = 1
        while n_ < 128:
            sl_o = slice(n_, 2 * n_); sl_i = slice(0, n_)
            bre = Wre[:, :, n_ - 1:n_].bc([128, 16, n_]); bim = Wim[:, :, n_ - 1:n_].bc([128, 16, n_])
            cmul("dve", Wre[:, :, sl_o], Wim[:, :, sl_o], Wre[:, :, sl_i], Wim[:, :, sl_i], bre, bim,
                 k1[:, :, 0:n_], k2[:, :, 0:n_])
            n_ *= 2
        cp("pool", RHO, rho.us(2).bc([128, 16, 128]))
        memset("pool", RHO[:, :, 0:1], 0.0)
        Ab_re = Are.us(2).bc([128, 16, 128]); Ab_im = Aim.us(2).bc([128, 16, 128])
        cmul("dve", KA[:, 0], KA[:, 1], Gp[:, 0], Gp[:, 1], Ab_re, Ab_im, k1, k2)
        tt("dve", k1, Wre, KA[:, 0], ALU.mult)
        tt("dve", k2, Wim, KA[:, 1], ALU.mult)
        tt("dve", Gp[:, 0], k1, k2, ALU.add)
        tt("dve", k1, Wre, KA[:, 1], ALU.mult)
        tt("dve", k2, Wim, KA[:, 0], ALU.mult)
        tt("dve", Gp[:, 1], k1, k2, ALU.subtract)
        for ri in range(2):
            P.op("dve", lambda e, ri=ri: e.tensor_tensor_scan(
                out=KA.ap[:, ri].rearrange("p q k -> p (q k)"), data0=RHO.ap.rearrange("p q k -> p (q k)"),
                data1=Gp.ap[:, ri].rearrange("p q k -> p (q k)"), initial=0.0, op0=ALU.mult, op1=ALU.add),
                reads=[RHO.key, Gp.key], writes=[KA.key])
        cmul("dve", Gp[:, 0], Gp[:, 1], KA[:, 0], KA[:, 1], Wre, Wim, k1, k2)
        Hre, Him = Gp[:, 0], Gp[:, 1]
        KAre, KAim = Hre, Him
        fin = R32.alloc([128, 2, 16], "fin")
        cp("dve", fin[:, 0, :], Hre[:, :, 127]); cp("dve", fin[:, 1, :], Him[:, :, 127])
        P.dma("sp", o_spre, fin.ap[:, 0, :], reads=[fin.key])
        P.dma("sp", o_spim, fin.ap[:, 1, :], reads=[fin.key])
        memset("dve", Hp[:, :, :, 0:1], 0.0)
        cp("dve", Hp[:, 0, :, 1:128], Hre[:, :, 0:127])
        cp("pool", Hp[:, 1, :, 1:128], Him[:, :, 0:127])
        s1 = R32.alloc([128, 16, 16], "s1"); s2 = R32.alloc([128, 16, 16], "s2")
        hsre = R32.alloc([128, 16, 16], "hsre"); hsim = R32.alloc([128, 16, 16], "hsim")
        cmul("dve", hsre, hsim, h0re, h0im, ai15re.us(2).bc([128, 16, 16]), ai15im.us(2).bc([128, 16, 16]), s1, s2)
        cp("dve", Hp[:, 0, :, 128:144], hsre)
        cp("dve", Hp[:, 1, :, 128:144], hsim)
        s3 = R32.alloc([128, 16, 16], "s3"); s4 = R32.alloc([128, 16, 16], "s4")
        tt("dve", s3, hsre, Gs[:, 0], ALU.add)
        tt("dve", s4, hsim, Gs[:, 1], ALU.add)
        nsre = R32.alloc([128, 16, 16], "nsre"); nsim = R32.alloc([128, 16, 16], "nsim")
        cmul("dve", nsre, nsim, s3, s4, Are.us(2).bc([128, 16, 16]), Aim.us(2).bc([128, 16, 16]), s1, s2)
        P.dma("sp", o_ssre, nsre.ap, reads=[nsre.key])
        P.dma("sp", o_ssim, nsim.ap, reads=[nsim.key])

        if DBG["stop"] == "p1b2":
            dump("KAre", KAre, [128, 16, 128]); dump("KAim", KAim, [128, 16, 128]); dump("Hp", Hp, [128, 2, 16, NK + NS])
            P.emit()
            return nc
        ystg = [R16.alloc([128, 512], "ystg0"), R16.alloc([128, 512], "ystg1")]
        for qt in range(4):
            for jg in range(4):
                YB = [PS[q4_ + 4 * ((qt * 4 + jg) % 2)] for q4_ in range(4)]
                for jl in range(4):
                    jt = jg * 4 + jl
                    cs_ = slice(jl * 128, jl * 128 + 128)
                    for ri in range(2):
                        for q4 in range(4):
                            q = qt * 4 + q4
                            rows = slice(32 * q4, 32 * q4 + 32)
                            mm(YB[q4][rows, cs_], CAv(ri)[:, q, jt + 1, :].k(f"CA{ri}_{q}"), Hp[:, ri, q, 0:128],
                               ri == 0, False, False, tp=(0, 32 * q4))
                    for js in range(jt + 1):
                        last = js == jt
                        for q4 in range(4):
                            rows = slice(32 * q4, 32 * q4 + 32)
                            mm(YB[q4][rows, cs_], Kt[rows, qt, jt - js, :], uT[rows, qt, js:SEQ:16],
                               False, last, (last and jl == 3), tp=(32 * q4, 32 * q4))
                if DBG["stop"] == "yA" and qt == 0 and jg == 0:
                    P.emit()
                    return nc
                stg = ystg[(qt * 4 + jg) % 2]
                for q4 in range(4):
                    rows = slice(32 * q4, 32 * q4 + 32)
                    act(stg[rows, :].k(f"{stg.key}_{q4}"), YB[q4][rows, :], AF.Gelu_apprx_tanh)
                P.op("dve", lambda e, stg=stg, qt=qt, jg=jg: e.tensor_copy(
                    out=yT.ap[:, qt, 0:SEQ].rearrange("p (k j) -> p j k", j=16)[:, jg * 4:(jg + 1) * 4, :],
                    in_=stg.ap.rearrange("p (j k) -> p j k", j=4)),
                    reads=[f"{stg.key}_{q4_}" for q4_ in range(4)], writes=[yT.key])
                if DBG["stop"] == "yB" and qt == 0 and jg == 0:
                    P.emit()
                    return nc
                if DBG["stop"] == "yC" and qt == 0 and jg == 1:
                    P.emit()
                    return nc
                if DBG["stop"] == "yD" and qt == 0 and jg == 3:
                    P.emit()
                    return nc
            ps = PS[4 + qt % 2]
            for q4 in range(4):
                q = qt * 4 + q4
                rows = slice(32 * q4, 32 * q4 + 32)
                for ri in range(2):
                    mm(ps[rows, 0:16], CAv(ri)[:, q, 16, :].k(f"CA{ri}_{q}"), Hp[:, ri, q, 128:144], ri == 0, False, False,
                       tp=(0, 32 * q4))
                mm(ps[rows, 0:16], Kt[rows, qt, 0, :], uT[rows, qt, SEQ:TT], False, True, q4 == 3,
                   tp=(32 * q4, 32 * q4))
            act(yT[:, qt, SEQ:TT], ps[:, 0:16], AF.Gelu_apprx_tanh)
            if DBG["stop"] == "yE" and qt == 0:
                P.emit()
                return nc
            if DBG["stop"] == "yF" and qt == 1:
                P.emit()
                return nc

        if DBG["stop"] == "p1":
            dump("yT", yT, [128, 4, TT]); dump("Hp", Hp, [128, 2, 16, NK + NS]); dump("Gp", Gp, [128, 2, 16, 128])
            P.emit()
            return nc
        P.barrier()
        R32.off, R16.off = mark32, mark16_main
        BCW = 512 * 3 + 512 + 7 * 1024 + 4
        bc = R32.alloc([128, BCW], "bc")
        P.dma("sp", bc.ap, dr["bc"], writes=[bc.key])
        o_ = [0]
        def bcs(n):
            v = bc[:, o_[0]:o_[0] + n]
            o_[0] += n
            return v
        bc_binv = bcs(512); bc_sg = bcs(512); bc_sb = bcs(512); bc_bs = bcs(512)
        bc_bo = bcs(1024); bc_g1 = bcs(1024); bc_b1 = bcs(1024); bc_bdn = bcs(1024)
        bc_bpg = bcs(1024); bc_g2 = bcs(1024); bc_b2 = bcs(1024); bc_bs0 = bcs(4)
        wsT32 = R32.alloc([128, 4, 128], "wsT32"); mask = R32.alloc([128, 128], "mask")
        P.dma("act", wsT32.ap, dr["wsT"], writes=[wsT32.key])
        P.dma("act", mask.ap, dr["mask"], writes=[mask.key])
        wsTb = R16.alloc([128, 4, 128], "wsTb")
        tt("dve", wsTb, wsT32, mask.us(1).bc([128, 4, 128]), ALU.mult)
        ws00b = R16.alloc([16, 4, 16], "ws00b")
        P.dma("pool", ws00b.ap, dr["ws00"], writes=[ws00b.key])
        identb = R16.alloc([128, 128], "identb")
        P.dma("pool", identb.ap, dr["ident"], writes=[identb.key])
        PSB4 = [T(psb[6][:, :].bitcast(BF16)[:, 0:512].rearrange("p (j t) -> p j t", j=4), "ps6"),
                T(psb[7][:, :].bitcast(BF16)[:, 0:512].rearrange("p (j t) -> p j t", j=4), "ps7")]

        NRING = 28
        ring = [R16.alloc([128, 512], f"ring{i}") for i in range(NRING)]
        rn = [0]
        def wload(nm, k, c0_):
            t = ring[rn[0] % NRING]
            rn[0] += 1
            P.dma("sp", t.ap[:, 0:512], wsc[nm][k * 128:(k + 1) * 128, c0_:c0_ + 512],
                  reads=[f"sc_{nm}_{k}"], writes=[t.key])
            return t
        xTb = R16.alloc([128, 8, 512], "xTb")
        pTb = R16.alloc([128, 2, 512], "pTb")
        strm = R32.alloc([128, 4, 1024], "strm")
        ub = R32.alloc([128, 4, 512], "ub")
        vg4 = R32.alloc([128, 4, 512], "vg4")
        vy_raw = R16.alloc([128, 4096], "vy")
        vhb = vy_raw[:, 0:2048].r("p (a b) -> p a b", a=4).k("vhb")
        yb = vy_raw[:, 2048:4096].r("p (a b) -> p a b", a=4).k("yb")
        x1b4 = vy_raw.r("p (a b) -> p a b", a=4).k("x1b4")
        ya = R16.alloc([128, 4, 512], "ya")
        sga = R32.alloc([128, 512], "sga"); sgb = R32.alloc([128, 512], "sgb")
        mg2 = R32.alloc([128, 512], "mg2"); mg3 = R32.alloc([128, 512], "mg3")
        sgc = R32.alloc([128, 512], "sgc")
        rl2 = R32.alloc([128, 512], "rl2")
        merged = R16.alloc([128, 8, 512], "merged")
        x1T = R16.alloc([128, 8, 512], "x1T")
        hid_raw = R16.alloc([128, 32 * 512], "hid")
        hid = hid_raw.r("p (a b) -> p a b", a=32)
        mgA = T(hid_raw.ap.bitcast(F32), "hid").r("p (a b) -> p a b", a=8)
        rl = R32.alloc([128, 512], "rl")
        stats = R32.alloc([128, 4, 2, 6], "stats"); mv = R32.alloc([128, 4, 2], "mv")
        rstd = R32.alloc([128, 4], "rstd"); nmr = R32.alloc([128, 4], "nmr")
        SB1 = (stats, mv, rstd, nmr, R32.alloc([128, 4], "vv1"), R32.alloc([128, 4], "tq1"))
        SB2 = (R32.alloc([128, 4, 2, 6], "stats2"), R32.alloc([128, 4, 2], "mv2"),
               R32.alloc([128, 4], "rstd2"), R32.alloc([128, 4], "nmr2"),
               R32.alloc([128, 4], "vv2"), R32.alloc([128, 4], "tq2"))
        tmp1k = R32.alloc([128, 1024], "tmp1k")

        def ln_stat_step(xc, cw, c, h, sb=None):
            stats = (sb or SB1)[0]
            P.op("dve", lambda e: e.bn_stats(out=stats.ap[0:cw, c, h, :], in_=xc.ap[0:cw, h * 512:(h + 1) * 512]),
                 reads=[xc.key], writes=[stats.key])

        def layer_norm_steps(chunks, cw, width, g, b, out_bfs=None, sb=None, with_stats=True, offload=False):
            stats, mv, rstd, nmr, vv, tq = sb or SB1
            nch = len(chunks)
            nchk = width // 512
            steps = []
            if with_stats:
                for c in range(nch):
                    for h in range(nchk):
                        steps.append(lambda c=c, h=h: ln_stat_step(chunks[c], cw, c, h, sb))
            for c in range(nch):
                steps.append(lambda c=c: P.op(
                    "dve", lambda e: e.bn_aggr(out=mv.ap[0:cw, c, :], in_=stats.ap[0:cw, c, 0:nchk, :]),
                    reads=[stats.key], writes=[mv.key]))

            def rs():
                y_ = rstd[0:cw, 0:nch]
                ts("dve", y_, mv[0:cw, 0:nch, 1], EPS, None, ALU.add)
                act(y_, y_, AF.Sqrt)
                P.op("dve", lambda e: e.reciprocal(out=y_.ap, in_=y_.ap), reads=[y_.key], writes=[y_.key])
                stt("dve", nmr[0:cw, 0:nch], mv[0:cw, 0:nch, 0], -1.0, y_, ALU.mult, ALU.mult)
            steps.append(rs)
            for c in range(nch):
                def aff(c=c):
                    xc = chunks[c][0:cw, 0:width]
                    if offload:
                        act(xc, xc, AF.Identity, bias=nmr[0:cw, c:c + 1], scale=rstd[0:cw, c:c + 1])
                        tt("dve", xc, xc, g[0:cw, 0:width], ALU.mult)
                        tt("pool" if offload == "pool" else "dve", xc, xc, b[0:cw, 0:width], ALU.add)
                    else:
                        stt("dve", xc, xc, mv[0:cw, c, 0:1], g[0:cw, 0:width], ALU.subtract, ALU.mult)
                        stt("dve", xc, xc, rstd[0:cw, c:c + 1], b[0:cw, 0:width], ALU.mult, ALU.add)
                    if out_bfs is not None:
                        act(out_bfs[c][0:cw, 0:width], xc, AF.Copy)
                steps.append(aff)
            return steps

        STRM = [T(strm.ap[:, c, :], f"strm{c}") for c in range(4)]
        SKEYS = [f"strm{c}" for c in range(4)]
        VG = [T(vg4.ap[:, c, :], f"vg{c}") for c in range(4)]
        VHB = [T(vhb.ap[:, c, :], f"vhb{c}") for c in range(4)]
        X1B = [T(x1b4.ap[:, c, :], f"x1b{c}") for c in range(4)]
        deferred = []

        def tick(k=1):
            for _ in range(k):
                if deferred:
                    deferred.pop(0)()

        stores = []

        def flush():
            while deferred:
                deferred.pop(0)()
            while stores:
                stores.pop(0)()

        def finish_block(t0, n):
            nch = max(1, n // 128)
            cw = min(n, 128)
            deferred.extend(layer_norm_steps(STRM[0:nch], cw, 1024, bc_g2, bc_b2, sb=SB2, with_stats=False, offload="pool"))
            stores.append(lambda: P.dma("sp", o_y[t0:t0 + n, :].rearrange("(c p) d -> p c d", p=cw),
                                        strm.ap[0:cw, 0:nch, :], reads=SKEYS[0:nch]))

        pending_finish = None
        for bi, (t0, n) in enumerate(blocks):
            nch = max(1, n // 128)
            cw = min(n, 128)
            P.dma("sp", xTb.ap[:, :, 0:n], wsc["xT"][:, t0:t0 + n].rearrange("(k p) n -> p k n", p=128),
                  reads=[f"sc_xT_{k}" for k in range(8)], writes=[xTb.key])
            wp = [wload("w_in", k, 512) for k in range(8)]
            for m in range(4):
                ps = PS[m % 2]
                for k in range(8):
                    mm(ps[:, 0:n], wp[k][:, m * 128:(m + 1) * 128], xTb[:, k, 0:n], k == 0, k == 7, k == 7)
                act(ub[:, m, 0:n], ps[:, 0:n], AF.Gelu_apprx_tanh, bias=bpp_in[:, 4 + m:5 + m])
            if pending_finish is not None:
                finish_block(*pending_finish)
                pending_finish = None
            wv = [wload("w_in", k, 1024) for k in range(8)]
            for c in range(nch):
                ps = PS[2 + c]
                for k in range(8):
                    mm(ps[0:cw, :], xTb[:, k, c * 128:c * 128 + cw], wv[k][:, 0:512], k == 0, k == 7, k == 7)
                tick(1)
            for c in range(nch):
                tt("dve", VG[c][0:cw], PS[2 + c][0:cw, :], bc_binv[0:cw], ALU.add)
            for c in range(nch):
                act(VG[c][0:cw], VG[c][0:cw], AF.Gelu_apprx_tanh)
            deferred.extend(layer_norm_steps(VG[0:nch], cw, 512, bc_sg, bc_sb, out_bfs=VHB[0:nch], offload=True))
            if n == NS:
                stores.append(lambda: P.dma("sp", o_vs, vg4.ap[0:NS, 0, :], reads=["vg0"]))
            wg = [wload("w_glu", k, 0) for k in range(4)]
            for m in range(4):
                ps = PS[m % 2]
                for k in range(4):
                    mm(ps[:, 0:n], wg[k][:, m * 128:(m + 1) * 128], yT[:, k, t0:t0 + n], k == 0, k == 3, k == 3)
                sg_ = sga if m % 2 == 0 else sgc
                act(sg_[:, 0:n], ps[:, 0:n], AF.Sigmoid, bias=bpp_glu[:, m:m + 1])
                tt("dve", ya[:, m, 0:n], sg_[:, 0:n], yT[:, m, t0:t0 + n], ALU.mult)
                tick(1)
            P.dma("sp", pTb.ap[:, :, 0:n], wsc["pT"][:, t0:t0 + n].rearrange("(k p) n -> p k n", p=128),
                  reads=[f"sc_pT_{k}" for k in range(2)], writes=[pTb.key])

            def merge_half(half):
                ysrc = ya if half == 0 else yb
                wnm = "w_pa" if half == 0 else "w_pb"
                for mg in range(2):
                    wsrc = [wload(wnm, k, mg * 512) for k in range(4)]
                    wga = [wload("w_in", k, 1536 + half * 1024 + mg * 512) for k in range(8)]
                    for ml in range(4):
                        m = mg * 4 + ml
                        mcol = 12 + half * 8 + m
                        ps = PS[m % 4]
                        for k in range(8):
                            mm(ps[:, 0:n], wga[k][:, ml * 128:(ml + 1) * 128], xTb[:, k, 0:n], k == 0, k == 7, k == 7)
                        sg_ = sga if m % 2 == 0 else sgc
                        mg_ = mg2 if m % 2 == 0 else mg3
                        act(sg_[:, 0:n], ps[:, 0:n], AF.Sigmoid, bias=bpp_in[:, mcol:mcol + 1])
                        ps2 = PS[4 + m % 4]
                        for k in range(4):
                            mm(ps2[:, 0:n], wsrc[k][:, ml * 128:(ml + 1) * 128], ysrc[:, k, 0:n], k == 0, k == 3, k == 3)
                        if half == 0:
                            tt("dve", mgA[:, m, 0:n].k(f"mgA{m}"), sg_[:, 0:n], ps2[:, 0:n], ALU.mult)
                            tick(2)
                        else:
                            tt("dve", mg_[:, 0:n], sg_[:, 0:n], ps2[:, 0:n], ALU.mult)
                            tt("pool", merged[:, m, 0:n].k(f"merged{m}"), mg_[:, 0:n], mgA[:, m, 0:n].k(f"mgA{m}"),
                               ALU.add)

            merge_half(0)
            flush()
            P.dma("sp", strm.ap[0:cw, 0:nch, :], dr["x_tok"][t0:t0 + n, :].rearrange("(c p) d -> p c d", p=cw),
                  writes=SKEYS[0:nch])
            for h in range(4):
                ps = PS[h % 2]
                if n == NS:
                    mm(ps[:, 0:NS], VHB[0][0:NS, h * 128:(h + 1) * 128], ws00b[0:NS, h, :], True, True, True)
                    ts("dve", sgb[:, 0:NS], ps[:, 0:NS], bc_bs0[:, h:h + 1], None, ALU.add)
                else:
                    for c in range(nch):
                        mm(ps[:, c * 128:(c + 1) * 128], VHB[c][:, h * 128:(h + 1) * 128], wsTb[:, h, :],
                           True, True, c == nch - 1)
                    tt("dve", sgb.r("p (c t) -> p c t", c=4), ps.r("p (c t) -> p c t", c=4),
                       bc_bs[:, h * 128:(h + 1) * 128].us(1).bc([128, 4, 128]), ALU.add)
                tt("dve", yb[:, h, 0:n], sgb[:, 0:n], ub[:, h, 0:n], ALU.mult)
            merge_half(1)
            if DBG["stop"] == "mF" and bi == 0:
                dump("merged", merged, [128, 8, 512])
                P.emit()
                return nc
            for half in range(2):
                hs = slice(half * 512, (half + 1) * 512)
                wo = [wload("w_out", k, half * 512) for k in range(8)]
                for c in range(nch):
                    ps = PS[4 + c]
                    for k in range(8):
                        mm(ps[0:cw, :], merged[:, k, c * 128:c * 128 + cw].k(f"merged{k}"), wo[k][:, 0:512],
                           k == 0, k == 7, k == 7)
                    tt("dve", tmp1k[0:cw, hs], ps[0:cw, :], bc_bo[0:cw, hs], ALU.add)
                    stt("dve", STRM[c][0:cw, hs], STRM[c][0:cw, hs], ALPHA, tmp1k[0:cw, hs], ALU.mult, ALU.add)
                    ln_stat_step(STRM[c], cw, c, half, SB1)
            ln1 = layer_norm_steps(STRM[0:nch], cw, 1024, bc_g1, bc_b1, out_bfs=X1B[0:nch], with_stats=False)
            for st_ in ln1[0:nch + 1]:
                st_()
            def transposes(c):
                for kq in range(2):
                    pst = PSB4[kq]
                    for j in range(4):
                        k = kq * 4 + j
                        P.op("pe", lambda e, pst=pst, k=k, j=j, cw=cw, c=c: e.transpose(
                            pst.ap[:, j, 0:cw], x1b4.ap[0:cw, c, k * 128:(k + 1) * 128], identb.ap[0:cw, 0:cw]),
                            reads=[f"x1b{c}", identb.key], writes=[pst.key], inc=(j == 3), skip_self_w=True)
                    act(x1T[:, kq * 4:(kq + 1) * 4, c * 128:c * 128 + cw], pst[:, :, 0:cw], AF.Copy)
            for c in range(nch):
                ln1[nch + 1 + c]()
                if c >= 1:
                    transposes(c - 1)
            transposes(nch - 1)
            if DBG["stop"] == "mG" and bi == 0:
                dump("strm", strm, [128, 4, 1024]); dump("x1T", x1T, [128, 8, 512])
                P.emit()
                return nc
            for cg in range(8):
                wu_ = [wload("w_up", k, cg * 512) for k in range(8)]
                for ml in range(4):
                    ps = PS[ml % 4]
                    for k in range(8):
                        mm(ps[:, 0:n], wu_[k][:, ml * 128:(ml + 1) * 128], x1T[:, k, 0:n], k == 0, k == 7, k == 7)
                    mi = cg * 4 + ml
                    rl_ = rl if ml % 2 == 0 else rl2
                    act(rl_[:, 0:n], ps[:, 0:n], AF.Relu, bias=bpp_up[:, mi:mi + 1])
                    tt("dve", hid[:, mi, 0:n].k(f"hid{mi}"), rl_[:, 0:n], rl_[:, 0:n], ALU.mult)
            for half in range(2):
                hs = slice(half * 512, (half + 1) * 512)
                for kg in range(4):
                    wd = [wload("w_down", kg * 8 + k, half * 512) for k in range(8)]
                    for c in range(nch):
                        ps = PS[4 + c]
                        for k in range(8):
                            kk = kg * 8 + k
                            mm(ps[0:cw, :], hid[:, kk, c * 128:c * 128 + cw].k(f"hid{kk}"), wd[k][:, 0:512],
                               kk == 0, kk == 31, k == 7)
                wpl = [wload("w_ple", k, half * 512) for k in range(2)]
                wpg = [wload("w_pg", k, half * 512) for k in range(8)]
                for c in range(nch):
                    ps = PS[4 + c]
                    tt("dve", tmp1k[0:cw, 0:512], ps[0:cw, :], bc_bdn[0:cw, hs], ALU.add)
                    stt("dve", STRM[c][0:cw, hs], STRM[c][0:cw, hs], ALPHA, tmp1k[0:cw, 0:512], ALU.mult, ALU.add)
                    psl = PS[0 + 2 * (c % 2)]
                    for k in range(2):
                        mm(psl[0:cw, :], pTb[:, k, c * 128:c * 128 + cw], wpl[k][:, 0:512], k == 0, k == 1, k == 1)
                    psg = PS[1 + 2 * (c % 2)]
                    for k in range(8):
                        mm(psg[0:cw, :], x1T[:, k, c * 128:c * 128 + cw], wpg[k][:, 0:512], k == 0, k == 7, k == 7)
                    tt("dve", tmp1k[0:cw, 512:1024], psg[0:cw, :], bc_bpg[0:cw, hs], ALU.add)
                    sg_ = sga if c % 2 == 0 else sgc
                    mg_ = mg2 if c % 2 == 0 else mg3
                    act(sg_[0:cw, :], tmp1k[0:cw, 512:1024], AF.Sigmoid)
                    tt("dve", mg_[0:cw, :], sg_[0:cw, :], psl[0:cw, :], ALU.mult)
                    tt("pool", STRM[c][0:cw, hs], STRM[c][0:cw, hs], mg_[0:cw, :], ALU.add)
                    ln_stat_step(STRM[c], cw, c, half, SB2)
            pending_finish = (t0, n)
            if DBG["stop"] == "blk%d" % bi:
                finish_block(*pending_finish)
                flush()
                P.emit()
                return nc
        finish_block(*pending_finish)
        flush()
        P.emit()
    return nc


_CACHE = {}


def kernel(**inputs):
    inp = {k: np.asarray(v) for k, v in inputs.items()}
    sh = _host_layout(inp)
    cores = [_core_layout(inp, b) for b in range(8)]
    key = "prog"
    if key not in _CACHE:
        _CACHE[key] = build_program({k: v.shape for k, v in sh.items()}, {k: v.shape for k, v in cores[0].items()})
    nc = _CACHE[key]
    in_maps = [{**sh, **cores[b]} for b in range(8)]
    res = run_bass_kernel_spmd(nc, in_maps, core_ids=list(range(8)))
    R = res.results
    y_p = np.stack([R[b]["o_y"][:SEQ] for b in range(8)], 0).astype(np.float32)
    y_s = np.concatenate([R[b]["o_y"][SEQ:TT] for b in range(8)], 0).reshape(128, 1, D).astype(np.float32)

    def unp(a):
        return a.reshape(2, 64, 16).transpose(2, 0, 1).reshape(32, 64)

    def uns(a):
        return a.reshape(2, 64, 16, 16).transpose(3, 2, 0, 1).reshape(16, 32, 64)

    spre = np.stack([unp(R[b]["o_spre"]) for b in range(8)], 0)[None].astype(np.float32)
    spim = np.stack([unp(R[b]["o_spim"]) for b in range(8)], 0)[None].astype(np.float32)
    ssre = np.concatenate([uns(R[b]["o_ssre"]) for b in range(8)], 0)[None].astype(np.float32)
    ssim = np.concatenate([uns(R[b]["o_ssim"]) for b in range(8)], 0)[None].astype(np.float32)
    vs = np.concatenate([R[b]["o_vs"] for b in range(8)], 0).reshape(1, 128, 1, 512).astype(np.float32)
    return (y_p, y_s, spre, spim, ssre, ssim, vs)
```

```python
import contextlib
import math
import numpy as np
import concourse.bass as bass
import concourse.mybir as mybir
from concourse.bass_utils import run_bass_kernel_spmd

F32 = mybir.dt.float32
BF16 = mybir.dt.bfloat16
ALU = mybir.AluOpType
AF = mybir.ActivationFunctionType

ENGS = ("pe", "act", "dve", "pool", "sp")
NSLOT = {"sp": 24, "act": 8, "pool": 16}
NBG = 40

D = 1024
SEQ = 2048
NS = 16
TT = SEQ + NS
L = 16
NK = SEQ // L
ALPHA = float(2.0 ** 0.25)
EPS = 1e-5
PI = math.pi


class Prog:
    def __init__(self, nc):
        self.nc = nc
        self.q = {e: [] for e in ENGS}
        self.cnt = {}
        self.last_w = {}
        self.readers = {}
        self.waited = {}
        self.dma_n = {e: 0 for e in ENGS}
        self.sems = {}
        self.n_inst = 0

    def _deps(self, eng, reads, writes, skip_self_w=False):
        deps = {}

        def add(d, same_ok=True):
            if d is None:
                return
            pid, c = d
            if pid == eng and not same_ok:
                return
            if deps.get(pid, 0) < c:
                deps[pid] = c

        for r in reads:
            add(self.last_w.get(r))
        for w in writes:
            add(self.last_w.get(w), same_ok=not skip_self_w)
            for rd in self.readers.get(w, ()):
                add(rd, same_ok=not skip_self_w)
        out = []
        for pid, c in deps.items():
            if self.waited.get((eng, pid), 0) >= c:
                continue
            if pid in ENGS and c > self.cnt.get(pid, 0):
                if pid == eng:
                    continue
                raise RuntimeError(f"dep on pending no-inc instr {pid}:{c} from {eng}")
            self.waited[(eng, pid)] = c
            out.append((pid, c))
        return out

    def _record(self, pid, c, reads, writes):
        for r in reads:
            self.readers.setdefault(r, []).append((pid, c))
        for w in writes:
            self.last_w[w] = (pid, c)
            self.readers[w] = []

    def op(self, eng, fn, reads=(), writes=(), inc=True, skip_self_w=False):
        reads = tuple(reads)
        writes = tuple(writes)
        deps = self._deps(eng, reads, writes, skip_self_w)
        c = self.cnt.get(eng, 0) + 1
        if inc:
            self.cnt[eng] = c
        self._record(eng, c, reads, writes)
        self.q[eng].append(("op", fn, deps, inc))
        self.n_inst += 1

    def dma(self, eng, out, in_, reads=(), writes=(), bg=False):
        reads = tuple(reads)
        writes = tuple(writes)
        if bg:
            n = self.dma_n.get("bg", 0)
            self.dma_n["bg"] = n + 1
            pid = ("dma", "bg", n % NBG)
        else:
            n = self.dma_n[eng]
            self.dma_n[eng] = n + 1
            pid = ("dma", eng, n % NSLOT[eng])
        deps = self._deps(eng, reads, writes)
        pc = self.cnt.get(pid, 0)
        if pc > 0 and self.waited.get((eng, pid), 0) < pc:
            self.waited[(eng, pid)] = pc
            deps.append((pid, pc))
        c = pc + 16
        self.cnt[pid] = c
        self._record(pid, c, reads, writes)
        self.q[eng].append(("dma", (out, in_), deps, pid))
        self.n_inst += 1

    def barrier(self):
        snap = dict(self.cnt)
        for e in ENGS:
            deps = []
            for pid, c in snap.items():
                if pid == e or c == 0 or (isinstance(pid, tuple) and pid[1] == "bg"):
                    continue
                if self.waited.get((e, pid), 0) < c:
                    self.waited[(e, pid)] = c
                    deps.append((pid, c))
            self.q[e].append(("wait", None, deps, None))

    def emit(self):
        nc = self.nc
        pids = list(ENGS)
        for e in ("sp", "act", "pool"):
            for s in range(NSLOT[e]):
                pids.append(("dma", e, s))
        for s in range(NBG):
            pids.append(("dma", "bg", s))
        with contextlib.ExitStack() as st:
            for pid in pids:
                name = pid if isinstance(pid, str) else f"dma_{pid[1]}_{pid[2]}"
                self.sems[pid] = st.enter_context(nc.semaphore("s_" + name))
            block = st.enter_context(nc.Block())
            final_waits = {e: [] for e in ENGS}
            for e in ("sp", "act", "pool"):
                for s in range(NSLOT[e]):
                    pid = ("dma", e, s)
                    c = self.cnt.get(pid, 0)
                    if c:
                        final_waits[e].append((pid, c))
            for s in range(NBG):
                c = self.cnt.get(("dma", "bg", s), 0)
                if c:
                    final_waits["pool"].append((("dma", "bg", s), c))

            def run(e, handle):
                for kind, payload, deps, extra in self.q[e]:
                    for pid, c in deps:
                        handle.wait_ge(self.sems[pid], c)
                    if kind == "op":
                        ins = payload(handle)
                        if extra:
                            ins.then_inc(self.sems[e], 1)
                    elif kind == "dma":
                        out, in_ = payload
                        handle.dma_start(out=out, in_=in_).then_inc(self.sems[extra], 16)
                for pid, c in final_waits[e]:
                    handle.wait_ge(self.sems[pid], c)

            @block.sync
            def _(h):
                run("sp", h)

            @block.scalar
            def _(h):
                run("act", h)

            @block.vector
            def _(h):
                run("dve", h)

            @block.gpsimd
            def _(h):
                run("pool", h)

            @block.tensor
            def _(h):
                run("pe", h)


class T:
    def __init__(self, ap, key):
        self.ap = ap
        self.key = key

    def __getitem__(self, idx):
        return T(self.ap[idx], self.key)

    def r(self, pat, **kw):
        return T(self.ap.rearrange(pat, **kw), self.key)

    def bc(self, shape):
        return T(self.ap.to_broadcast(list(shape)), self.key)

    def us(self, ax):
        return T(self.ap.unsqueeze(ax), self.key)

    def k(self, key):
        return T(self.ap, key)


def _pp(v, m):
    return np.ascontiguousarray(v.reshape(m, 128).T)


def _host_layout(inp):
    f = np.float32
    sh = {}
    sh["w_in"] = np.ascontiguousarray(inp["w_in"][0])
    sh["w_glu"] = np.ascontiguousarray(inp["w_glu"][0])
    sh["w_pa"] = np.ascontiguousarray(inp["w_branch_a"][0])
    sh["w_pb"] = np.ascontiguousarray(inp["w_branch_b"][0])
    sh["w_out"] = np.ascontiguousarray(inp["w_out"][0])
    sh["w_up"] = np.ascontiguousarray(inp["w_up"][0])
    sh["w_down"] = np.ascontiguousarray(inp["w_down"][0])
    sh["w_ple"] = np.ascontiguousarray(inp["w_ple"][0])
    sh["w_pg"] = np.ascontiguousarray(inp["w_ple_gate"][0])
    sh["b_in_pp"] = _pp(inp["b_in"][0], 28)
    sh["b_glu_pp"] = _pp(inp["b_glu"][0], 4)
    sh["b_up_pp"] = _pp(inp["b_up"][0], 32)
    rows = [inp["b_in"][0][1024:1536], inp["sgu_ln_g"][0], inp["sgu_ln_b"][0],
            inp["sgu_b"][0].reshape(-1),
            inp["b_out"][0], inp["ln1_g"][0], inp["ln1_b"][0], inp["b_down"][0],
            inp["b_ple_gate"][0], inp["ln2_g"][0], inp["ln2_b"][0],
            inp["sgu_b"][0][:, 0]]
    row = np.concatenate([np.asarray(r, f).reshape(-1) for r in rows])
    sh["bc"] = np.ascontiguousarray(np.broadcast_to(row[None, :], (128, row.size)))
    sh["wsT"] = np.ascontiguousarray(np.transpose(inp["sgu_w"][0], (2, 0, 1)))
    sh["mask"] = np.ascontiguousarray(np.triu(np.ones((128, 128), f)))
    ws00 = np.zeros((16, 4, 16), f)
    for h in range(4):
        ws00[np.arange(16), h, np.arange(16)] = inp["sgu_w"][0, h, 0, 0]
    sh["ws00"] = ws00
    sh["ident"] = np.eye(128, dtype=f)
    lr = inp["ssm_lambda_re"][0]; li = inp["ssm_lambda_im"][0]; ldt = inp["ssm_log_dt"][0]
    def o1(a):
        return np.ascontiguousarray(a.reshape(16, 2, 64).transpose(1, 2, 0).reshape(128, 16))
    sh["lr1"] = o1(lr); sh["li1"] = o1(li)
    sh["ldt1"] = o1(np.broadcast_to(ldt[:, None], (32, 64)))
    def o1b(b):
        out = np.zeros((2, 64, 16, 2, 16), f)
        bb = b.reshape(16, 2, 64, 16)
        for g2 in range(2):
            out[g2, :, :, g2, :] = bb[:, g2].transpose(1, 0, 2)
        return out.reshape(128, 16, 32)
    sh["B1re"] = o1b(inp["ssm_b_re"][0]); sh["B1im"] = o1b(inp["ssm_b_im"][0])
    sh["C1re"] = o1b(np.transpose(inp["ssm_c_re"][0], (0, 2, 1)))
    sh["C1im"] = o1b(np.transpose(inp["ssm_c_im"][0], (0, 2, 1)))
    def o2(a):
        aa = a.reshape(4, 4, 2, 64)
        out = np.broadcast_to(aa.transpose(1, 0, 2, 3)[:, None, None], (4, 2, 16, 4, 2, 64))
        return np.ascontiguousarray(out.reshape(128, 4, 128))
    sh["lrc"] = np.ascontiguousarray(np.concatenate([sh.pop("lr1"), o2(lr).reshape(128, 512)], 1))
    sh["lic"] = np.ascontiguousarray(np.concatenate([sh.pop("li1"), o2(li).reshape(128, 512)], 1))
    sh["ldtc"] = np.ascontiguousarray(np.concatenate(
        [sh.pop("ldt1"), o2(np.broadcast_to(ldt[:, None], (32, 64))).reshape(128, 512)], 1))
    def o2b(b):
        bb = b.reshape(4, 4, 2, 64, 16)
        out = np.zeros((4, 2, 16, 4, 2, 64), f)
        for g2 in range(2):
            out[:, g2, :, :, g2, :] = bb[:, :, g2].transpose(1, 3, 0, 2)
        return out.reshape(128, 4, 128)
    sh["B2re"] = o2b(inp["ssm_b_re"][0]); sh["B2im"] = o2b(inp["ssm_b_im"][0])
    dsk = np.zeros((4, 2, 16, 4, 2, 16), f)
    dd = inp["ssm_d"][0].reshape(4, 4, 2, 16)
    for g2 in range(2):
        for c in range(16):
            dsk[:, g2, c, :, g2, c] = dd[:, :, g2, c].T
    sh["dsk"] = dsk.reshape(128, 4, 32)
    return sh


def _core_layout(inp, b):
    f = np.float32
    xs = inp["x_sample"][16 * b:16 * b + 16, 0]
    xtok = np.concatenate([inp["x_prompt"][b], xs], 0)
    ptok = np.concatenate([inp["p_prompt"][0, b], inp["p_sample"][0, 16 * b:16 * b + 16, 0]], 0)
    def oh(a):
        return np.ascontiguousarray(a.reshape(16, 16, 2, 64).transpose(2, 3, 1, 0).reshape(128, 16, 16))
    return {
        "x_tok": np.ascontiguousarray(xtok, f),
        "xT": np.ascontiguousarray(xtok.T, f),
        "pT": np.ascontiguousarray(ptok.T, f),
        "h0re": oh(inp["state_ssm_re"][0, 16 * b:16 * b + 16]),
        "h0im": oh(inp["state_ssm_im"][0, 16 * b:16 * b + 16]),
    }


DBG = {"stop": None}


def build_program(shared_shapes, core_shapes):
    nc = bass.Bass("TRN2", target_bir_lowering=False)
    P = Prog(nc)
    dbg_outs = {}

    def dump(name, t, shape):
        o = nc.dram_tensor("dbg_" + name, list(shape), F32, kind="ExternalOutput").ap()
        P.dma("pool", o, t.ap, reads=[t.key])
        dbg_outs[name] = o
    dr = {}
    for k, shp in list(shared_shapes.items()) + list(core_shapes.items()):
        dr[k] = nc.dram_tensor(k, list(shp), F32, kind="ExternalInput").ap()
    o_y = nc.dram_tensor("o_y", [TT, D], F32, kind="ExternalOutput").ap()
    o_spre = nc.dram_tensor("o_spre", [128, 16], F32, kind="ExternalOutput").ap()
    o_spim = nc.dram_tensor("o_spim", [128, 16], F32, kind="ExternalOutput").ap()
    o_ssre = nc.dram_tensor("o_ssre", [128, 16, 16], F32, kind="ExternalOutput").ap()
    o_ssim = nc.dram_tensor("o_ssim", [128, 16, 16], F32, kind="ExternalOutput").ap()
    o_vs = nc.dram_tensor("o_vs", [16, 512], F32, kind="ExternalOutput").ap()

    st = contextlib.ExitStack()
    with st:
        A32W = 22900
        A16W = 60500
        a32 = st.enter_context(nc.sbuf_tensor("a32", [128, A32W], F32))
        a16 = st.enter_context(nc.sbuf_tensor("a16", [128, A16W], BF16))
        psb = [st.enter_context(nc.psum_tensor(f"ps{i}", [128, 512], F32)) for i in range(8)]
        PS = [T(psb[i][:, :], f"ps{i}") for i in range(8)]

        class Arena:
            def __init__(s, t, size, nm):
                s.t, s.size, s.nm, s.off, s.n = t, size, nm, 0, 0

            def alloc(s, shape, key=None):
                n = int(np.prod(shape[1:]))
                assert s.off + n <= s.size, (s.nm, s.off, n, s.size)
                ap = s.t[0:shape[0], s.off:s.off + n]
                s.off += n
                s.n += 1
                key = key or f"{s.nm}_{s.n}"
                if len(shape) == 3:
                    ap = ap.rearrange("p (a b) -> p a b", a=shape[1])
                elif len(shape) == 4:
                    ap = ap.rearrange("p (a b c) -> p a b c", a=shape[1], b=shape[2])
                return T(ap, key)

        R32 = Arena(a32, A32W, "f")
        R16 = Arena(a16, A16W, "h")

        yT = R16.alloc([128, 4, TT], "yT")
        bpp_in = R32.alloc([128, 28], "bpp_in")
        bpp_glu = R32.alloc([128, 4], "bpp_glu")
        bpp_up = R32.alloc([128, 32], "bpp_up")
        P.dma("sp", bpp_in.ap, dr["b_in_pp"], writes=[bpp_in.key])
        P.dma("sp", bpp_glu.ap, dr["b_glu_pp"], writes=[bpp_glu.key])
        P.dma("sp", bpp_up.ap, dr["b_up_pp"], writes=[bpp_up.key])
        mark32, mark16 = R32.off, R16.off

        def tt(eng, out, a, b, op):
            P.op(eng, lambda e: e.tensor_tensor(out=out.ap, in0=a.ap, in1=b.ap, op=op),
                 reads=[a.key, b.key], writes=[out.key])

        def ts(eng, out, a, s1, s2, op0, op1=None):
            rd = [a.key]
            s1a = s1.ap if isinstance(s1, T) else s1
            s2a = s2.ap if isinstance(s2, T) else s2
            if isinstance(s1, T):
                rd.append(s1.key)
            if isinstance(s2, T):
                rd.append(s2.key)
            if op1 is None:
                P.op(eng, lambda e: e.tensor_scalar(out=out.ap, in0=a.ap, scalar1=s1a, scalar2=None, op0=op0),
                     reads=rd, writes=[out.key])
            else:
                P.op(eng, lambda e: e.tensor_scalar(out=out.ap, in0=a.ap, scalar1=s1a, scalar2=s2a, op0=op0, op1=op1),
                     reads=rd, writes=[out.key])

        def stt(eng, out, a, s, b, op0, op1):
            eng = "dve"
            rd = [a.key, b.key]
            sa = s.ap if isinstance(s, T) else s
            if isinstance(s, T):
                rd.append(s.key)
            P.op(eng, lambda e: e.scalar_tensor_tensor(out=out.ap, in0=a.ap, scalar=sa, in1=b.ap, op0=op0, op1=op1),
                 reads=rd, writes=[out.key])

        def cp(eng, out, a):
            P.op(eng, lambda e: e.tensor_copy(out=out.ap, in_=a.ap), reads=[a.key], writes=[out.key])

        def act(out, a, func, bias=None, scale=None):
            rd = [a.key]
            kw = {}
            if bias is not None:
                kw["bias"] = bias.ap if isinstance(bias, T) else bias
                if isinstance(bias, T):
                    rd.append(bias.key)
            if scale is not None:
                kw["scale"] = scale.ap if isinstance(scale, T) else scale
                if isinstance(scale, T):
                    rd.append(scale.key)
            P.op("act", lambda e: e.activation(out=out.ap, in_=a.ap, func=func, **kw), reads=rd, writes=[out.key])

        def memset(eng, out, v):
            P.op(eng, lambda e: e.memset(out.ap, v), writes=[out.key])

        def mm(out, lhsT, rhs, start, stop, inc, tp=None):
            kw = {}
            if tp is not None:
                kw["tile_position"] = tp
            P.op("pe", lambda e: e.matmul(out.ap, lhsT=lhsT.ap, rhs=rhs.ap, start=start, stop=stop, **kw),
                 reads=[lhsT.key, rhs.key], writes=[out.key], inc=inc, skip_self_w=True)

        def cmul(eng, ore, oim, are, aim, bre, bim, t1, t2):
            tt(eng, t1, are, bre, ALU.mult)
            tt(eng, t2, aim, bim, ALU.mult)
            tt(eng, ore, t1, t2, ALU.subtract)
            tt(eng, t1, are, bim, ALU.mult)
            tt(eng, t2, aim, bre, ALU.mult)
            tt(eng, oim, t1, t2, ALU.add)

        def exp_taylor(eng, out, x, deg):
            ts(eng, out, x, 1.0 / math.factorial(deg), None, ALU.mult)
            for kk in range(deg - 1, 0, -1):
                stt(eng, out, out, 1.0 / math.factorial(kk), x, ALU.add, ALU.mult)
            ts(eng, out, out, 1.0, None, ALU.add)

        def disc(eng, shape, lr, li, ldt, pref):
            al = lambda nm: R32.alloc(shape, pref + nm)
            s0 = al("s0"); s1 = al("s1"); s2 = al("s2"); s3 = al("s3"); s4 = al("s4"); s5 = al("s5")
            s6 = al("s6"); s7 = al("s7"); s8 = al("s8"); t1 = al("t1"); t2 = al("t2")
            dt, a, th, mag, imag, n, thr, sn, cs = s0, s1, s2, s3, s4, s5, s6, s7, s8
            c0 = 4.605170185988092
            ts(eng, t1, ldt, c0, 0.25, ALU.add, ALU.mult)
            exp_taylor(eng, dt, t1, 10)
            tt(eng, dt, dt, dt, ALU.mult)
            tt(eng, dt, dt, dt, ALU.mult)
            ts(eng, dt, dt, math.exp(-c0), None, ALU.mult)
            tt(eng, a, lr, dt, ALU.mult)
            tt(eng, th, li, dt, ALU.mult)
            exp_taylor(eng, mag, a, 6)
            ts(eng, t1, a, -1.0, None, ALU.mult)
            exp_taylor(eng, imag, t1, 6)
            ts(eng, n, th, PI, None, ALU.is_ge)
            for m_ in (3, 5, 7):
                stt(eng, n, th, m_ * PI, n, ALU.is_ge, ALU.add)
            stt(eng, thr, n, -2.0 * PI, th, ALU.mult, ALU.add)
            tt(eng, t2, thr, thr, ALU.mult)
            sc = [(-1.0) ** k / math.factorial(2 * k + 1) for k in range(10)]
            cc = [(-1.0) ** k / math.factorial(2 * k) for k in range(11)]
            ts(eng, sn, t2, sc[9], None, ALU.mult)
            for k in range(8, 0, -1):
                stt(eng, sn, sn, sc[k], t2, ALU.add, ALU.mult)
            stt(eng, sn, sn, 1.0, thr, ALU.add, ALU.mult)
            ts(eng, cs, t2, cc[10], None, ALU.mult)
            for k in range(9, 0, -1):
                stt(eng, cs, cs, cc[k], t2, ALU.add, ALU.mult)
            ts(eng, cs, cs, 1.0, None, ALU.add)
            are, aim, ire, iim = s0, s1, s2, s4
            tt(eng, are, mag, cs, ALU.mult)
            tt(eng, aim, mag, sn, ALU.mult)
            tt(eng, ire, imag, cs, ALU.mult)
            tt(eng, iim, imag, sn, ALU.mult)
            ts(eng, iim, iim, -1.0, None, ALU.mult)
            den, nr = s5, s6
            tt(eng, t1, lr, lr, ALU.mult)
            tt(eng, t2, li, li, ALU.mult)
            tt(eng, den, t1, t2, ALU.add)
            P.op("dve", lambda e: e.reciprocal(out=den.ap, in_=den.ap), reads=[den.key], writes=[den.key])
            ts(eng, nr, are, -1.0, None, ALU.add)
            cre, cim = s7, s8
            tt(eng, t1, nr, lr, ALU.mult)
            tt(eng, t2, aim, li, ALU.mult)
            tt(eng, t1, t1, t2, ALU.add)
            tt(eng, cre, t1, den, ALU.mult)
            tt(eng, t1, aim, lr, ALU.mult)
            tt(eng, t2, nr, li, ALU.mult)
            tt(eng, t1, t1, t2, ALU.subtract)
            tt(eng, cim, t1, den, ALU.mult)
            return dict(are=are, aim=aim, ire=ire, iim=iim, cre=cre, cim=cim, t1=t1, t2=t2, mag=mag)

        mark16_main = R16.off
        uT = R16.alloc([128, 4, TT], "uT")
        mark32, mark16 = R32.off, R16.off
        blocks = [(i * 512, 512) for i in range(4)] + [(SEQ, NS)]
        Bb = R16.alloc([128, 2, 16, 32], "Bb")
        CA = R16.alloc([128, 2, 16 * 17 * 32], "CA")
        mark16_lt = R16.off
        Wu = R16.alloc([128, 8, 512], "Wu")
        P.dma("pool", Wu.ap, dr["w_in"][:, 0:512].rearrange("(k p) n -> p k n", p=128), writes=[Wu.key])
        xb_s5 = [R16.alloc([128, 8, 512], f"xb_s5_{i}") for i in range(5)]
        for bi, (t0, n) in enumerate(blocks):
            P.dma("pool", xb_s5[bi].ap[:, :, 0:n], dr["xT"][:, t0:t0 + n].rearrange("(k p) n -> p k n", p=128),
                  writes=[xb_s5[bi].key])
        wsc = {}
        cast_list = [("w_in", 1024, 3584), ("w_glu", 512, 512), ("w_pa", 512, 1024), ("w_pb", 512, 1024),
                     ("w_out", 1024, 1024), ("w_up", 1024, 4096), ("w_down", 4096, 1024), ("w_ple", 256, 1024),
                     ("w_pg", 1024, 1024), ("xT", 1024, TT), ("pT", 256, TT)]
        for nm, r_, c_ in cast_list:
            wsc[nm] = nc.dram_tensor("sc_" + nm, [r_, c_], BF16, kind="Internal").ap()
            step = 256 if c_ <= 2064 else 128
            for k in range(0, r_, step):
                P.dma("pool", wsc[nm][k:k + step, :], dr[nm][k:k + step, :], writes=[f"sc_{nm}_{k // 128}"] +
                      ([f"sc_{nm}_{k // 128 + 1}"] if step == 256 else []), bg=True)
        def ld(name, shape, eng="sp"):
            t = R32.alloc(shape, name)
            P.dma(eng, t.ap, dr[name], writes=[t.key])
            return t

        dsk = ld("dsk", [128, 4, 32], "act")
        P1re = R32.alloc([128, 17, 16], "P1re"); P1im = R32.alloc([128, 17, 16], "P1im")
        I1re = R32.alloc([128, 16, 16], "I1re"); I1im = R32.alloc([128, 16, 16], "I1im")
        rho = R32.alloc([128, 16], "rho")
        mark32b = R32.off
        lrc = ld("lrc", [128, 528]); lic = ld("lic", [128, 528]); ldtc = ld("ldtc", [128, 528])
        B1re = ld("B1re", [128, 16, 32]); B1im = ld("B1im", [128, 16, 32])
        C1re = ld("C1re", [128, 16, 32]); C1im = ld("C1im", [128, 16, 32])
        B2re = ld("B2re", [128, 4, 128], "act"); B2im = ld("B2im", [128, 4, 128], "act")
        dd = disc("dve", [128, 528], lrc, lic, ldtc, "dd")
        d1 = {k: v[:, 0:16] for k, v in dd.items()}
        d2 = {k: v[:, 16:528].r("p (a b) -> p a b", a=4) for k, v in dd.items()}
        pt1 = R32.alloc([128, 8, 16], "pt1"); pt2 = R32.alloc([128, 8, 16], "pt2")
        tt("dve", rho, d1["mag"], d1["mag"], ALU.mult)
        for _ in range(3):
            tt("dve", rho, rho, rho, ALU.mult)
        memset("dve", P1re[:, 0, :], 1.0)
        memset("dve", P1im[:, 0, :], 0.0)
        cp("dve", P1re[:, 1, :], d1["are"])
        cp("dve", P1im[:, 1, :], d1["aim"])
        n_ = 1
        while n_ < 16:
            sl_o = slice(n_ + 1, 2 * n_ + 1)
            sl_i = slice(1, n_ + 1)
            bre = P1re[:, n_:n_ + 1, :].bc([128, n_, 16]); bim = P1im[:, n_:n_ + 1, :].bc([128, n_, 16])
            cmul("dve", P1re[:, sl_o, :], P1im[:, sl_o, :],
                 P1re[:, sl_i, :], P1im[:, sl_i, :], bre, bim, pt1[:, 0:n_, :], pt2[:, 0:n_, :])
            n_ *= 2
        w1 = dd["t1"][:, 16:528].r("p (a b) -> p a b", a=16); w2 = dd["t2"][:, 16:528].r("p (a b) -> p a b", a=16)
        cob_re = d1["cre"].us(2).bc([128, 16, 32]); cob_im = d1["cim"].us(2).bc([128, 16, 32])
        cmul("dve", Bb[:, 0], Bb[:, 1], B1re, B1im, cob_re, cob_im, w1, w2)
        CAv = lambda ri: CA[:, ri, :].r("p (q m c) -> p q m c", q=16, m=17)
        y1_ = R32.alloc([128, 17, 32], "y1_"); y2_ = R32.alloc([128, 17, 32], "y2_")
        z1_ = R32.alloc([128, 17, 32], "z1_"); z2_ = R32.alloc([128, 17, 32], "z2_")
        for q in range(16):
            cr = C1re[:, q:q + 1, :].bc([128, 17, 32]); ci = C1im[:, q:q + 1, :].bc([128, 17, 32])
            pr = P1re[:, :, q:q + 1].bc([128, 17, 32]); pi_ = P1im[:, :, q:q + 1].bc([128, 17, 32])
            eng, a1, a2 = ("dve", z1_, z2_) if q % 2 == 0 else ("dve", y1_, y2_)
            tt(eng, a1, cr, pr, ALU.mult)
            tt(eng, a2, ci, pi_, ALU.mult)
            tt(eng, CAv(0)[:, q].k(f"CA0_{q}"), a1, a2, ALU.subtract)
            tt(eng, a1, cr, pi_, ALU.mult)
            tt(eng, a2, ci, pr, ALU.mult)
            stt(eng, CAv(1)[:, q].k(f"CA1_{q}"), a1, -1.0, a2, ALU.mult, ALU.subtract)
        Are = P1re[:, 16, :]; Aim = P1im[:, 16, :]
        cp("dve", I1re[:, 0, :], d1["ire"]); cp("dve", I1im[:, 0, :], d1["iim"])
        n_ = 1
        while n_ < 16:
            sl_o = slice(n_, 2 * n_); sl_i = slice(0, n_)
            bre = I1re[:, n_ - 1:n_, :].bc([128, n_, 16]); bim = I1im[:, n_ - 1:n_, :].bc([128, n_, 16])
            cmul("dve", I1re[:, sl_o, :], I1im[:, sl_o, :],
                 I1re[:, sl_i, :], I1im[:, sl_i, :], bre, bim, pt1[:, 0:n_, :], pt2[:, 0:n_, :])
            n_ *= 2
        ai15re = I1re[:, 14, :]; ai15im = I1im[:, 14, :]

        for bi, (t0, n) in enumerate(blocks):
            xb = xb_s5[bi]
            for m in range(4):
                ps = PS[m % 2]
                for k in range(8):
                    mm(ps[:, 0:n], Wu[:, k, m * 128:(m + 1) * 128], xb[:, k, 0:n], k == 0, k == 7, k == 7)
                act(uT[:, m, t0:t0 + n], ps[:, 0:n], AF.Identity, bias=bpp_in[:, m:m + 1])
        if DBG["stop"] == "p0":
            dump("uT", uT, [128, 4, TT])
            P.emit()
            return nc
        P.barrier()
        R16.off = mark16_lt

        LT = R16.alloc([128, 4, 2 * 16 * 128], "LT")
        LTv = lambda qt, ri: LT[:, qt, :].r("p (r j n) -> p r j n", r=2, j=16)[:, ri].k(f"LT{qt}")
        Bb2re = R32.alloc([128, 4, 128], "Bb2re"); Bb2im = R32.alloc([128, 4, 128], "Bb2im")
        cmul("dve", Bb2re, Bb2im, B2re, B2im, d2["cre"], d2["cim"], d2["t1"], d2["t2"])
        IPre = R32.alloc([128, 4, 128], "IPre"); IPim = R32.alloc([128, 4, 128], "IPim")
        iq1 = dd["t1"][:, 16:144]; iq2 = dd["t2"][:, 16:144]
        L32re = R32.alloc([128, 16, 128], "L32re"); L32im = R32.alloc([128, 16, 128], "L32im")
        q1 = R32.alloc([128, 8, 128], "q1"); q2 = R32.alloc([128, 8, 128], "q2")
        for qt in range(4):
            cs_ = slice(0, 128)
            cp("dve", IPre[:, 0, :], d2["ire"][:, qt, :]); cp("dve", IPim[:, 0, :], d2["iim"][:, qt, :])
            for lv in range(3):
                cmul("dve", IPre[:, lv + 1, :], IPim[:, lv + 1, :], IPre[:, lv, :], IPim[:, lv, :],
                     IPre[:, lv, :], IPim[:, lv, :], iq1, iq2)
            cmul("dve", L32re[:, 0, :], L32im[:, 0, :], Bb2re[:, qt, :], Bb2im[:, qt, :],
                 IPre[:, 0, cs_], IPim[:, 0, cs_], q1[:, 0, :], q2[:, 0, :])
            n_ = 1
            lv = 0
            while n_ < 16:
                sl_o = slice(n_, 2 * n_); sl_i = slice(0, n_)
                bre = IPre[:, lv, cs_].us(1).bc([128, n_, 128]); bim = IPim[:, lv, cs_].us(1).bc([128, n_, 128])
                cmul("dve", L32re[:, sl_o, :], L32im[:, sl_o, :],
                     L32re[:, sl_i, :], L32im[:, sl_i, :], bre, bim, q1[:, 0:n_, :], q2[:, 0:n_, :])
                n_ *= 2
                lv += 1
            act(LTv(qt, 0), L32re, AF.Copy)
            act(LTv(qt, 1), L32im, AF.Copy)

        Kt = R16.alloc([128, 4, 16, 32], "Kt")
        kt32 = R32.alloc([128, 16, 32], "kt32")
        for qt in range(4):
            ps = PS[2 + qt % 2]
            for q4 in range(4):
                q = qt * 4 + q4
                for ri in range(2):
                    mm(ps[32 * q4:32 * q4 + 32, :], Bb[:, ri, q, :],
                       CA[:, ri, q * 544:q * 544 + 512].k(f"CA{ri}_{q}"), ri == 0, ri == 1, (ri == 1 and q4 == 3), tp=(0, 32 * q4))
            cp("dve", kt32, ps.r("p (m c) -> p m c", m=16))
            tt("dve", kt32[:, 0, :], kt32[:, 0, :], dsk[:, qt, :], ALU.add)
            cp("dve", Kt[:, qt], kt32)
        Hp = R16.alloc([128, 2, 16, NK + NS], "Hp")
        if DBG["stop"] == "p1a":
            dump("P1re", P1re, [128, 17, 16]); dump("P1im", P1im, [128, 17, 16])
            dump("I1re", I1re, [128, 16, 16]); dump("I1im", I1im, [128, 16, 16])
            dump("Bb", Bb, [128, 2, 16, 32]); dump("CA", CA, [128, 2, 16 * 17 * 32])
            dump("LT", LT, [128, 4, 2 * 16 * 128]); dump("Kt", Kt, [128, 4, 16, 32])
            P.emit()
            return nc
        P.barrier()
        R32.off = mark32b

        h0re = ld("h0re", [128, 16, 16], "act"); h0im = ld("h0im", [128, 16, 16], "act")
        Gp = R32.alloc([128, 2, 16, 128], "Gp")
        Gs = R32.alloc([128, 2, 16, 16], "Gs")
        for ri in range(2):
            for qt in range(4):
                for j in range(16):
                    for q4 in range(4):
                        ps = PS[4 + q4]
                        mm(ps[:, qt * 128:(qt + 1) * 128], LTv(qt, ri)[32 * q4:32 * q4 + 32, j, :],
                           uT[32 * q4:32 * q4 + 32, qt, j:SEQ:16], j == 0, j == 15,
                           (j == 15 and qt == 3), tp=(32 * q4, 0))
            for q4 in range(4):
                cp("dve", Gp[:, ri, q4:16:4, :], PS[4 + q4].r("p (a k) -> p a k", a=4))
        for q4 in range(4):
            ps = PS[4 + q4]
            for ri in range(2):
                for qt in range(4):
                    cb = (ri * 4 + qt) * 16
                    mm(ps[:, cb:cb + 16], LTv(qt, ri)[32 * q4:32 * q4 + 32, 15, :],
                       uT[32 * q4:32 * q4 + 32, qt, SEQ:TT], True, True, (ri == 1 and qt == 3), tp=(32 * q4, 0))
        for q4 in range(4):
            for ri in range(2):
                cp("dve", Gs[:, ri, q4:16:4, :], PS[4 + q4][:, ri * 64:ri * 64 + 64].r("p (a t) -> p a t", a=4))
        if DBG["stop"] == "p1b1":
            dump("Gp", Gp, [128, 2, 16, 128]); dump("Gs", Gs, [128, 2, 16, 16])
            P.emit()
            return nc
        KA = R32.alloc([128, 2, 16, 128], "KA")
        k1 = R32.alloc([128, 16, 128], "k1"); k2 = R32.alloc([128, 16, 128], "k2")
        Wre = R32.alloc([128, 16, 128], "Wre"); Wim = R32.alloc([128, 16, 128], "Wim")
        RHO = R32.alloc([128, 16, 128], "RHO")
        ure = R32.alloc([128, 16], "ure"); uim = R32.alloc([128, 16], "uim")
        sp1 = R32.alloc([128, 16], "sp1"); sp2 = R32.alloc([128, 16], "sp2")
        P.op("dve", lambda e: e.reciprocal(out=sp1.ap, in_=rho.ap), reads=[rho.key], writes=[sp1.key])
        tt("dve", ure, Are, sp1, ALU.mult)
        tt("dve", uim, Aim, sp1, ALU.mult)
        tt("dve", sp1, ure, ure, ALU.mult)
        tt("dve", sp2, uim, uim, ALU.mult)
        tt("dve", sp1, sp1, sp2, ALU.add)
        ts("dve", sp1, sp1, -0.5, 1.5, ALU.mult, ALU.add)
        tt("dve", ure, ure, sp1, ALU.mult)
        tt("dve", uim, uim, sp1, ALU.mult)
        cp("dve", Wre[:, :, 0], ure); cp("dve", Wim[:, :, 0], uim)
        n_ = 1
        while n_ < 128:
            sl_o = slice(n_, 2 * n_); sl_i = slice(0, n_)
            bre = Wre[:, :, n_ - 1:n_].bc([128, 16, n_]); bim = Wim[:, :, n_ - 1:n_].bc([128, 16, n_])
            cmul("dve", Wre[:, :, sl_o], Wim[:, :, sl_o], Wre[:, :, sl_i], Wim[:, :, sl_i], bre, bim,
                 k1[:, :, 0:n_], k2[:, :, 0:n_])
            n_ *= 2
        cp("pool", RHO, rho.us(2).bc([128, 16, 128]))
        memset("pool", RHO[:, :, 0:1], 0.0)
        Ab_re = Are.us(2).bc([128, 16, 128]); Ab_im = Aim.us(2).bc([128, 16, 128])
        cmul("dve", KA[:, 0], KA[:, 1], Gp[:, 0], Gp[:, 1], Ab_re, Ab_im, k1, k2)
        tt("dve", k1, Wre, KA[:, 0], ALU.mult)
        tt("dve", k2, Wim, KA[:, 1], ALU.mult)
        tt("dve", Gp[:, 0], k1, k2, ALU.add)
        tt("dve", k1, Wre, KA[:, 1], ALU.mult)
        tt("dve", k2, Wim, KA[:, 0], ALU.mult)
        tt("dve", Gp[:, 1], k1, k2, ALU.subtract)
        for ri in range(2):
            P.op("dve", lambda e, ri=ri: e.tensor_tensor_scan(
                out=KA.ap[:, ri].rearrange("p q k -> p (q k)"), data0=RHO.ap.rearrange("p q k -> p (q k)"),
                data1=Gp.ap[:, ri].rearrange("p q k -> p (q k)"), initial=0.0, op0=ALU.mult, op1=ALU.add),
                reads=[RHO.key, Gp.key], writes=[KA.key])
        cmul("dve", Gp[:, 0], Gp[:, 1], KA[:, 0], KA[:, 1], Wre, Wim, k1, k2)
        Hre, Him = Gp[:, 0], Gp[:, 1]
        KAre, KAim = Hre, Him
        fin = R32.alloc([128, 2, 16], "fin")
        cp("dve", fin[:, 0, :], Hre[:, :, 127]); cp("dve", fin[:, 1, :], Him[:, :, 127])
        P.dma("sp", o_spre, fin.ap[:, 0, :], reads=[fin.key])
        P.dma("sp", o_spim, fin.ap[:, 1, :], reads=[fin.key])
        memset("dve", Hp[:, :, :, 0:1], 0.0)
        cp("dve", Hp[:, 0, :, 1:128], Hre[:, :, 0:127])
        cp("pool", Hp[:, 1, :, 1:128], Him[:, :, 0:127])
        s1 = R32.alloc([128, 16, 16], "s1"); s2 = R32.alloc([128, 16, 16], "s2")
        hsre = R32.alloc([128, 16, 16], "hsre"); hsim = R32.alloc([128, 16, 16], "hsim")
        cmul("dve", hsre, hsim, h0re, h0im, ai15re.us(2).bc([128, 16, 16]), ai15im.us(2).bc([128, 16, 16]), s1, s2)
        cp("dve", Hp[:, 0, :, 128:144], hsre)
        cp("dve", Hp[:, 1, :, 128:144], hsim)
        s3 = R32.alloc([128, 16, 16], "s3"); s4 = R32.alloc([128, 16, 16], "s4")
        tt("dve", s3, hsre, Gs[:, 0], ALU.add)
        tt("dve", s4, hsim, Gs[:, 1], ALU.add)
        nsre = R32.alloc([128, 16, 16], "nsre"); nsim = R32.alloc([128, 16, 16], "nsim")
        cmul("dve", nsre, nsim, s3, s4, Are.us(2).bc([128, 16, 16]), Aim.us(2).bc([128, 16, 16]), s1, s2)
        P.dma("sp", o_ssre, nsre.ap, reads=[nsre.key])
        P.dma("sp", o_ssim, nsim.ap, reads=[nsim.key])

        if DBG["stop"] == "p1b2":
            dump("KAre", KAre, [128, 16, 128]); dump("KAim", KAim, [128, 16, 128]); dump("Hp", Hp, [128, 2, 16, NK + NS])
            P.emit()
            return nc
        ystg = [R16.alloc([128, 512], "ystg0"), R16.alloc([128, 512], "ystg1")]
        for qt in range(4):
            for jg in range(4):
                YB = [PS[q4_ + 4 * ((qt * 4 + jg) % 2)] for q4_ in range(4)]
                for jl in range(4):
                    jt = jg * 4 + jl
                    cs_ = slice(jl * 128, jl * 128 + 128)
                    for ri in range(2):
                        for q4 in range(4):
                            q = qt * 4 + q4
                            rows = slice(32 * q4, 32 * q4 + 32)
                            mm(YB[q4][rows, cs_], CAv(ri)[:, q, jt + 1, :].k(f"CA{ri}_{q}"), Hp[:, ri, q, 0:128],
                               ri == 0, False, False, tp=(0, 32 * q4))
                    for js in range(jt + 1):
                        last = js == jt
                        for q4 in range(4):
                            rows = slice(32 * q4, 32 * q4 + 32)
                            mm(YB[q4][rows, cs_], Kt[rows, qt, jt - js, :], uT[rows, qt, js:SEQ:16],
                               False, last, (last and jl == 3), tp=(32 * q4, 32 * q4))
                if DBG["stop"] == "yA" and qt == 0 and jg == 0:
                    P.emit()
                    return nc
                stg = ystg[(qt * 4 + jg) % 2]
                for q4 in range(4):
                    rows = slice(32 * q4, 32 * q4 + 32)
                    act(stg[rows, :].k(f"{stg.key}_{q4}"), YB[q4][rows, :], AF.Gelu_apprx_tanh)
                P.op("dve", lambda e, stg=stg, qt=qt, jg=jg: e.tensor_copy(
                    out=yT.ap[:, qt, 0:SEQ].rearrange("p (k j) -> p j k", j=16)[:, jg * 4:(jg + 1) * 4, :],
                    in_=stg.ap.rearrange("p (j k) -> p j k", j=4)),
                    reads=[f"{stg.key}_{q4_}" for q4_ in range(4)], writes=[yT.key])
                if DBG["stop"] == "yB" and qt == 0 and jg == 0:
                    P.emit()
                    return nc
                if DBG["stop"] == "yC" and qt == 0 and jg == 1:
                    P.emit()
                    return nc
                if DBG["stop"] == "yD" and qt == 0 and jg == 3:
                    P.emit()
                    return nc
            ps = PS[4 + qt % 2]
            for q4 in range(4):
                q = qt * 4 + q4
                rows = slice(32 * q4, 32 * q4 + 32)
                for ri in range(2):
                    mm(ps[rows, 0:16], CAv(ri)[:, q, 16, :].k(f"CA{ri}_{q}"), Hp[:, ri, q, 128:144], ri == 0, False, False,
                       tp=(0, 32 * q4))
                mm(ps[rows, 0:16], Kt[rows, qt, 0, :], uT[rows, qt, SEQ:TT], False, True, q4 == 3,
                   tp=(32 * q4, 32 * q4))
            act(yT[:, qt, SEQ:TT], ps[:, 0:16], AF.Gelu_apprx_tanh)
            if DBG["stop"] == "yE" and qt == 0:
                P.emit()
                return nc
            if DBG["stop"] == "yF" and qt == 1:
                P.emit()
                return nc

        if DBG["stop"] == "p1":
            dump("yT", yT, [128, 4, TT]); dump("Hp", Hp, [128, 2, 16, NK + NS]); dump("Gp", Gp, [128, 2, 16, 128])
            P.emit()
            return nc
        P.barrier()
        R32.off, R16.off = mark32, mark16_main
        BCW = 512 * 3 + 512 + 7 * 1024 + 4
        bc = R32.alloc([128, BCW], "bc")
        P.dma("sp", bc.ap, dr["bc"], writes=[bc.key])
        o_ = [0]
        def bcs(n):
            v = bc[:, o_[0]:o_[0] + n]
            o_[0] += n
            return v
        bc_binv = bcs(512); bc_sg = bcs(512); bc_sb = bcs(512); bc_bs = bcs(512)
        bc_bo = bcs(1024); bc_g1 = bcs(1024); bc_b1 = bcs(1024); bc_bdn = bcs(1024)
        bc_bpg = bcs(1024); bc_g2 = bcs(1024); bc_b2 = bcs(1024); bc_bs0 = bcs(4)
        wsT32 = R32.alloc([128, 4, 128], "wsT32"); mask = R32.alloc([128, 128], "mask")
        P.dma("act", wsT32.ap, dr["wsT"], writes=[wsT32.key])
        P.dma("act", mask.ap, dr["mask"], writes=[mask.key])
        wsTb = R16.alloc([128, 4, 128], "wsTb")
        tt("dve", wsTb, wsT32, mask.us(1).bc([128, 4, 128]), ALU.mult)
        ws00b = R16.alloc([16, 4, 16], "ws00b")
        P.dma("pool", ws00b.ap, dr["ws00"], writes=[ws00b.key])
        identb = R16.alloc([128, 128], "identb")
        P.dma("pool", identb.ap, dr["ident"], writes=[identb.key])
        PSB4 = [T(psb[6][:, :].bitcast(BF16)[:, 0:512].rearrange("p (j t) -> p j t", j=4), "ps6"),
                T(psb[7][:, :].bitcast(BF16)[:, 0:512].rearrange("p (j t) -> p j t", j=4), "ps7")]

        NRING = 28
        ring = [R16.alloc([128, 512], f"ring{i}") for i in range(NRING)]
        rn = [0]
        def wload(nm, k, c0_):
            t = ring[rn[0] % NRING]
            rn[0] += 1
            P.dma("sp", t.ap[:, 0:512], wsc[nm][k * 128:(k + 1) * 128, c0_:c0_ + 512],
                  reads=[f"sc_{nm}_{k}"], writes=[t.key])
            return t
        xTb = R16.alloc([128, 8, 512], "xTb")
        pTb = R16.alloc([128, 2, 512], "pTb")
        strm = R32.alloc([128, 4, 1024], "strm")
        ub = R32.alloc([128, 4, 512], "ub")
        vg4 = R32.alloc([128, 4, 512], "vg4")
        vy_raw = R16.alloc([128, 4096], "vy")
        vhb = vy_raw[:, 0:2048].r("p (a b) -> p a b", a=4).k("vhb")
        yb = vy_raw[:, 2048:4096].r("p (a b) -> p a b", a=4).k("yb")
        x1b4 = vy_raw.r("p (a b) -> p a b", a=4).k("x1b4")
        ya = R16.alloc([128, 4, 512], "ya")
        sga = R32.alloc([128, 512], "sga"); sgb = R32.alloc([128, 512], "sgb")
        mg2 = R32.alloc([128, 512], "mg2"); mg3 = R32.alloc([128, 512], "mg3")
        sgc = R32.alloc([128, 512], "sgc")
        rl2 = R32.alloc([128, 512], "rl2")
        merged = R16.alloc([128, 8, 512], "merged")
        x1T = R16.alloc([128, 8, 512], "x1T")
        hid_raw = R16.alloc([128, 32 * 512], "hid")
        hid = hid_raw.r("p (a b) -> p a b", a=32)
        mgA = T(hid_raw.ap.bitcast(F32), "hid").r("p (a b) -> p a b", a=8)
        rl = R32.alloc([128, 512], "rl")
        stats = R32.alloc([128, 4, 2, 6], "stats"); mv = R32.alloc([128, 4, 2], "mv")
        rstd = R32.alloc([128, 4], "rstd"); nmr = R32.alloc([128, 4], "nmr")
        SB1 = (stats, mv, rstd, nmr, R32.alloc([128, 4], "vv1"), R32.alloc([128, 4], "tq1"))
        SB2 = (R32.alloc([128, 4, 2, 6], "stats2"), R32.alloc([128, 4, 2], "mv2"),
               R32.alloc([128, 4], "rstd2"), R32.alloc([128, 4], "nmr2"),
               R32.alloc([128, 4], "vv2"), R32.alloc([128, 4], "tq2"))
        tmp1k = R32.alloc([128, 1024], "tmp1k")

        def ln_stat_step(xc, cw, c, h, sb=None):
            stats = (sb or SB1)[0]
            P.op("dve", lambda e: e.bn_stats(out=stats.ap[0:cw, c, h, :], in_=xc.ap[0:cw, h * 512:(h + 1) * 512]),
                 reads=[xc.key], writes=[stats.key])

        def layer_norm_steps(chunks, cw, width, g, b, out_bfs=None, sb=None, with_stats=True, offload=False):
            stats, mv, rstd, nmr, vv, tq = sb or SB1
            nch = len(chunks)
            nchk = width // 512
            steps = []
            if with_stats:
                for c in range(nch):
                    for h in range(nchk):
                        steps.append(lambda c=c, h=h: ln_stat_step(chunks[c], cw, c, h, sb))
            for c in range(nch):
                steps.append(lambda c=c: P.op(
                    "dve", lambda e: e.bn_aggr(out=mv.ap[0:cw, c, :], in_=stats.ap[0:cw, c, 0:nchk, :]),
                    reads=[stats.key], writes=[mv.key]))

            def rs():
                y_ = rstd[0:cw, 0:nch]
                ts("dve", y_, mv[0:cw, 0:nch, 1], EPS, None, ALU.add)
                act(y_, y_, AF.Sqrt)
                P.op("dve", lambda e: e.reciprocal(out=y_.ap, in_=y_.ap), reads=[y_.key], writes=[y_.key])
                stt("dve", nmr[0:cw, 0:nch], mv[0:cw, 0:nch, 0], -1.0, y_, ALU.mult, ALU.mult)
            steps.append(rs)
            for c in range(nch):
                def aff(c=c):
                    xc = chunks[c][0:cw, 0:width]
                    if offload:
                        act(xc, xc, AF.Identity, bias=nmr[0:cw, c:c + 1], scale=rstd[0:cw, c:c + 1])
                        tt("dve", xc, xc, g[0:cw, 0:width], ALU.mult)
                        tt("pool" if offload == "pool" else "dve", xc, xc, b[0:cw, 0:width], ALU.add)
                    else:
                        stt("dve", xc, xc, mv[0:cw, c, 0:1], g[0:cw, 0:width], ALU.subtract, ALU.mult)
                        stt("dve", xc, xc, rstd[0:cw, c:c + 1], b[0:cw, 0:width], ALU.mult, ALU.add)
                    if out_bfs is not None:
                        act(out_bfs[c][0:cw, 0:width], xc, AF.Copy)
                steps.append(aff)
            return steps

        STRM = [T(strm.ap[:, c, :], f"strm{c}") for c in range(4)]
        SKEYS = [f"strm{c}" for c in range(4)]
        VG = [T(vg4.ap[:, c, :], f"vg{c}") for c in range(4)]
        VHB = [T(vhb.ap[:, c, :], f"vhb{c}") for c in range(4)]
        X1B = [T(x1b4.ap[:, c, :], f"x1b{c}") for c in range(4)]
        deferred = []

        def tick(k=1):
            for _ in range(k):
                if deferred:
                    deferred.pop(0)()

        stores = []

        def flush():
            while deferred:
                deferred.pop(0)()
            while stores:
                stores.pop(0)()

        def finish_block(t0, n):
            nch = max(1, n // 128)
            cw = min(n, 128)
            deferred.extend(layer_norm_steps(STRM[0:nch], cw, 1024, bc_g2, bc_b2, sb=SB2, with_stats=False, offload="pool"))
            stores.append(lambda: P.dma("sp", o_y[t0:t0 + n, :].rearrange("(c p) d -> p c d", p=cw),
                                        strm.ap[0:cw, 0:nch, :], reads=SKEYS[0:nch]))

        pending_finish = None
        for bi, (t0, n) in enumerate(blocks):
            nch = max(1, n // 128)
            cw = min(n, 128)
            P.dma("sp", xTb.ap[:, :, 0:n], wsc["xT"][:, t0:t0 + n].rearrange("(k p) n -> p k n", p=128),
                  reads=[f"sc_xT_{k}" for k in range(8)], writes=[xTb.key])
            wp = [wload("w_in", k, 512) for k in range(8)]
            for m in range(4):
                ps = PS[(0, 1, 6, 7)[m]]
                for k in range(8):
                    mm(ps[:, 0:n], wp[k][:, m * 128:(m + 1) * 128], xTb[:, k, 0:n], k == 0, k == 7, k == 7)
                act(ub[:, m, 0:n], ps[:, 0:n], AF.Gelu_apprx_tanh, bias=bpp_in[:, 4 + m:5 + m])
            if pending_finish is not None:
                finish_block(*pending_finish)
                pending_finish = None
            wv = [wload("w_in", k, 1024) for k in range(8)]
            for c in range(nch):
                ps = PS[2 + c]
                for k in range(8):
                    mm(ps[0:cw, :], xTb[:, k, c * 128:c * 128 + cw], wv[k][:, 0:512], k == 0, k == 7, k == 7)
                tick(1)
            for c in range(nch):
                tt("dve", VG[c][0:cw], PS[2 + c][0:cw, :], bc_binv[0:cw], ALU.add)
            for c in range(nch):
                act(VG[c][0:cw], VG[c][0:cw], AF.Gelu_apprx_tanh)
            deferred.extend(layer_norm_steps(VG[0:nch], cw, 512, bc_sg, bc_sb, out_bfs=VHB[0:nch], offload=True))
            if n == NS:
                stores.append(lambda: P.dma("sp", o_vs, vg4.ap[0:NS, 0, :], reads=["vg0"]))
            wg = [wload("w_glu", k, 0) for k in range(4)]
            for m in range(4):
                ps = PS[(0, 1, 6, 7)[m]]
                for k in range(4):
                    mm(ps[:, 0:n], wg[k][:, m * 128:(m + 1) * 128], yT[:, k, t0:t0 + n], k == 0, k == 3, k == 3)
                sg_ = sga if m % 2 == 0 else sgc
                act(sg_[:, 0:n], ps[:, 0:n], AF.Sigmoid, bias=bpp_glu[:, m:m + 1])
                tt("dve", ya[:, m, 0:n], sg_[:, 0:n], yT[:, m, t0:t0 + n], ALU.mult)
                tick(1)
            P.dma("sp", pTb.ap[:, :, 0:n], wsc["pT"][:, t0:t0 + n].rearrange("(k p) n -> p k n", p=128),
                  reads=[f"sc_pT_{k}" for k in range(2)], writes=[pTb.key])

            def merge_half(half):
                ysrc = ya if half == 0 else yb
                wnm = "w_pa" if half == 0 else "w_pb"
                for mg in range(2):
                    wsrc = [wload(wnm, k, mg * 512) for k in range(4)]
                    wga = [wload("w_in", k, 1536 + half * 1024 + mg * 512) for k in range(8)]
                    for ml in range(4):
                        m = mg * 4 + ml
                        mcol = 12 + half * 8 + m
                        ps = PS[m % 4]
                        for k in range(8):
                            mm(ps[:, 0:n], wga[k][:, ml * 128:(ml + 1) * 128], xTb[:, k, 0:n], k == 0, k == 7, k == 7)
                        sg_ = sga if m % 2 == 0 else sgc
                        mg_ = mg2 if m % 2 == 0 else mg3
                        act(sg_[:, 0:n], ps[:, 0:n], AF.Sigmoid, bias=bpp_in[:, mcol:mcol + 1])
                        ps2 = PS[4 + m % 4]
                        for k in range(4):
                            mm(ps2[:, 0:n], wsrc[k][:, ml * 128:(ml + 1) * 128], ysrc[:, k, 0:n], k == 0, k == 3, k == 3)
                        if half == 0:
                            tt("dve", mgA[:, m, 0:n].k(f"mgA{m}"), sg_[:, 0:n], ps2[:, 0:n], ALU.mult)
                            tick(2)
                        else:
                            tt("dve", mg_[:, 0:n], sg_[:, 0:n], ps2[:, 0:n], ALU.mult)
                            tt("pool", merged[:, m, 0:n].k(f"merged{m}"), mg_[:, 0:n], mgA[:, m, 0:n].k(f"mgA{m}"),
                               ALU.add)

            merge_half(0)
            flush()
            P.dma("sp", strm.ap[0:cw, 0:nch, :], dr["x_tok"][t0:t0 + n, :].rearrange("(c p) d -> p c d", p=cw),
                  writes=SKEYS[0:nch])
            for h in range(4):
                ps = PS[h % 2]
                if n == NS:
                    mm(ps[:, 0:NS], VHB[0][0:NS, h * 128:(h + 1) * 128], ws00b[0:NS, h, :], True, True, True)
                    ts("dve", sgb[:, 0:NS], ps[:, 0:NS], bc_bs0[:, h:h + 1], None, ALU.add)
                else:
                    for c in range(nch):
                        mm(ps[:, c * 128:(c + 1) * 128], VHB[c][:, h * 128:(h + 1) * 128], wsTb[:, h, :],
                           True, True, c == nch - 1)
                    tt("dve", sgb.r("p (c t) -> p c t", c=4), ps.r("p (c t) -> p c t", c=4),
                       bc_bs[:, h * 128:(h + 1) * 128].us(1).bc([128, 4, 128]), ALU.add)
                tt("dve", yb[:, h, 0:n], sgb[:, 0:n], ub[:, h, 0:n], ALU.mult)
            merge_half(1)
            if DBG["stop"] == "mF" and bi == 0:
                dump("merged", merged, [128, 8, 512])
                P.emit()
                return nc
            for half in range(2):
                hs = slice(half * 512, (half + 1) * 512)
                wo = [wload("w_out", k, half * 512) for k in range(8)]
                for c in range(nch):
                    ps = PS[4 + c]
                    for k in range(8):
                        mm(ps[0:cw, :], merged[:, k, c * 128:c * 128 + cw].k(f"merged{k}"), wo[k][:, 0:512],
                           k == 0, k == 7, k == 7)
                    tt("dve", tmp1k[0:cw, hs], ps[0:cw, :], bc_bo[0:cw, hs], ALU.add)
                    stt("dve", STRM[c][0:cw, hs], STRM[c][0:cw, hs], ALPHA, tmp1k[0:cw, hs], ALU.mult, ALU.add)
                    ln_stat_step(STRM[c], cw, c, half, SB1)
            ln1 = layer_norm_steps(STRM[0:nch], cw, 1024, bc_g1, bc_b1, out_bfs=X1B[0:nch], with_stats=False)
            for st_ in ln1[0:nch + 1]:
                st_()
            def transposes(c):
                for kq in range(2):
                    pst = PSB4[kq]
                    for j in range(4):
                        k = kq * 4 + j
                        P.op("pe", lambda e, pst=pst, k=k, j=j, cw=cw, c=c: e.transpose(
                            pst.ap[:, j, 0:cw], x1b4.ap[0:cw, c, k * 128:(k + 1) * 128], identb.ap[0:cw, 0:cw]),
                            reads=[f"x1b{c}", identb.key], writes=[pst.key], inc=(j == 3), skip_self_w=True)
                    act(x1T[:, kq * 4:(kq + 1) * 4, c * 128:c * 128 + cw], pst[:, :, 0:cw], AF.Copy)
            for c in range(nch):
                ln1[nch + 1 + c]()
                if c >= 1:
                    transposes(c - 1)
            transposes(nch - 1)
            if DBG["stop"] == "mG" and bi == 0:
                dump("strm", strm, [128, 4, 1024]); dump("x1T", x1T, [128, 8, 512])
                P.emit()
                return nc
            for cg in range(8):
                wu_ = [wload("w_up", k, cg * 512) for k in range(8)]
                for ml in range(4):
                    ps = PS[ml % 4]
                    for k in range(8):
                        mm(ps[:, 0:n], wu_[k][:, ml * 128:(ml + 1) * 128], x1T[:, k, 0:n], k == 0, k == 7, k == 7)
                    mi = cg * 4 + ml
                    rl_ = rl if ml % 2 == 0 else rl2
                    act(rl_[:, 0:n], ps[:, 0:n], AF.Relu, bias=bpp_up[:, mi:mi + 1])
                    tt("dve", hid[:, mi, 0:n].k(f"hid{mi}"), rl_[:, 0:n], rl_[:, 0:n], ALU.mult)
            for half in range(2):
                hs = slice(half * 512, (half + 1) * 512)
                for kg in range(4):
                    wd = [wload("w_down", kg * 8 + k, half * 512) for k in range(8)]
                    for c in range(nch):
                        ps = PS[4 + c]
                        for k in range(8):
                            kk = kg * 8 + k
                            mm(ps[0:cw, :], hid[:, kk, c * 128:c * 128 + cw].k(f"hid{kk}"), wd[k][:, 0:512],
                               kk == 0, kk == 31, k == 7)
                wpl = [wload("w_ple", k, half * 512) for k in range(2)]
                wpg = [wload("w_pg", k, half * 512) for k in range(8)]
                for c in range(nch):
                    ps = PS[4 + c]
                    tt("dve", tmp1k[0:cw, 0:512], ps[0:cw, :], bc_bdn[0:cw, hs], ALU.add)
                    stt("dve", STRM[c][0:cw, hs], STRM[c][0:cw, hs], ALPHA, tmp1k[0:cw, 0:512], ALU.mult, ALU.add)
                    psl = PS[0 + 2 * (c % 2)]
                    for k in range(2):
                        mm(psl[0:cw, :], pTb[:, k, c * 128:c * 128 + cw], wpl[k][:, 0:512], k == 0, k == 1, k == 1)
                    psg = PS[1 + 2 * (c % 2)]
                    for k in range(8):
                        mm(psg[0:cw, :], x1T[:, k, c * 128:c * 128 + cw], wpg[k][:, 0:512], k == 0, k == 7, k == 7)
                    tt("dve", tmp1k[0:cw, 512:1024], psg[0:cw, :], bc_bpg[0:cw, hs], ALU.add)
                    sg_ = sga if c % 2 == 0 else sgc
                    mg_ = mg2 if c % 2 == 0 else mg3
                    act(sg_[0:cw, :], tmp1k[0:cw, 512:1024], AF.Sigmoid)
                    tt("dve", mg_[0:cw, :], sg_[0:cw, :], psl[0:cw, :], ALU.mult)
                    tt("pool", STRM[c][0:cw, hs], STRM[c][0:cw, hs], mg_[0:cw, :], ALU.add)
                    ln_stat_step(STRM[c], cw, c, half, SB2)
            pending_finish = (t0, n)
            if DBG["stop"] == "blk%d" % bi:
                finish_block(*pending_finish)
                flush()
                P.emit()
                return nc
        finish_block(*pending_finish)
        flush()
        P.emit()
    return nc


_CACHE = {}


def kernel(**inputs):
    inp = {k: np.asarray(v) for k, v in inputs.items()}
    sh = _host_layout(inp)
    cores = [_core_layout(inp, b) for b in range(8)]
    key = "prog"
    if key not in _CACHE:
        _CACHE[key] = build_program({k: v.shape for k, v in sh.items()}, {k: v.shape for k, v in cores[0].items()})
    nc = _CACHE[key]
    in_maps = [{**sh, **cores[b]} for b in range(8)]
    res = run_bass_kernel_spmd(nc, in_maps, core_ids=list(range(8)))
    R = res.results
    y_p = np.stack([R[b]["o_y"][:SEQ] for b in range(8)], 0).astype(np.float32)
    y_s = np.concatenate([R[b]["o_y"][SEQ:TT] for b in range(8)], 0).reshape(128, 1, D).astype(np.float32)

    def unp(a):
        return a.reshape(2, 64, 16).transpose(2, 0, 1).reshape(32, 64)

    def uns(a):
        return a.reshape(2, 64, 16, 16).transpose(3, 2, 0, 1).reshape(16, 32, 64)

    spre = np.stack([unp(R[b]["o_spre"]) for b in range(8)], 0)[None].astype(np.float32)
    spim = np.stack([unp(R[b]["o_spim"]) for b in range(8)], 0)[None].astype(np.float32)
    ssre = np.concatenate([uns(R[b]["o_ssre"]) for b in range(8)], 0)[None].astype(np.float32)
    ssim = np.concatenate([uns(R[b]["o_ssim"]) for b in range(8)], 0)[None].astype(np.float32)
    vs = np.concatenate([R[b]["o_vs"] for b in range(8)], 0).reshape(1, 128, 1, 512).astype(np.float32)
    return (y_p, y_s, spre, spim, ssre, ssim, vs)
```
